# Optimizing a Trainium2 kernel written in Bass

```python
import math
import jax, jax.numpy as jnp
from jax import lax
import numpy as np

D_MODEL = 2048
BATCH = 8
SEQ = 2048
DEPTH = 4
DEC_BATCH = 4
DEC_SEQ = 8192
PAST_LEN = 128

N_MIXERS = 3
N_RG_LAYERS = (DEPTH + 2) // 3
N_SSD_LAYERS = (DEPTH + 1) // 3
N_NA_LAYERS = DEPTH // 3
ALPHA = (2 * DEPTH) ** 0.25
BETA = (8 * DEPTH) ** -0.25
LN_EPS = 1e-5
CONV_W = 4
CONV_PAD_L = 1
LRU_W = D_MODEL
LRU_BLOCKS = 8
LRU_BW = LRU_W // LRU_BLOCKS
LRU_C = 8.0
SSD_D_INNER = 2 * D_MODEL
SSD_HEAD_DIM = 64
SSD_HEADS = SSD_D_INNER // SSD_HEAD_DIM
SSD_GROUPS = 8
SSD_STATE = 128
SSD_CHUNK = 128
SSD_CONV_DIM = SSD_D_INNER + 2 * SSD_GROUPS * SSD_STATE
SSD_IN_DIM = SSD_D_INNER + SSD_CONV_DIM + 2 * SSD_HEADS
GRID_W = 64
WIN_H = 8
WIN_W = 16
NA_HEAD_DIM = 128
NA_HEADS = D_MODEL // NA_HEAD_DIM
NA_QBLK = WIN_W
NA_KBLK = 2 * WIN_W
NA_NBLK = GRID_W // NA_QBLK
MLP_HIDDEN = 4 * D_MODEL

kernel_name = "hybrid_bidir_rglru_ssd_natten_encoder"


def layer_norm(x, g, b):
    xf = x.astype(jnp.float32)
    mu = jnp.mean(xf, axis=-1, keepdims=True)
    xc = xf - mu
    var = jnp.mean(xc * xc, axis=-1, keepdims=True)
    return (xc * lax.rsqrt(var + LN_EPS) * g + b).astype(x.dtype)


def centred_dwconv(x, w, b):
    L = x.shape[1]
    xp = jnp.pad(x, ((0, 0), (CONV_PAD_L, CONV_W - 1 - CONV_PAD_L), (0, 0)))
    y = b
    for k in range(CONV_W):
        y = y + xp[:, k:k + L] * w[k]
    return y


def _lin_combine(left, right):
    a1, b1 = left
    a2, b2 = right
    return a1 * a2, a2 * b1 + b2


def linear_scan(a, b, reverse):
    if reverse:
        a, b = jnp.flip(a, axis=1), jnp.flip(b, axis=1)
    _, h = lax.associative_scan(_lin_combine, (a, b), axis=1)
    return jnp.flip(h, axis=1) if reverse else h


def rglru_mixer(x, w_in, conv_w, conv_b, w_a, b_a, w_x, b_x, lam, w_out):
    bsz, L, _ = x.shape
    gate, u = jnp.split(x @ w_in, 2, axis=-1)
    gate = jax.nn.gelu(gate)
    u = centred_dwconv(u, conv_w, conv_b)
    ub = u.reshape(bsz, L, LRU_BLOCKS, LRU_BW)
    h = jnp.zeros((bsz, L, LRU_W), jnp.float32)
    for d in range(2):
        r = jax.nn.sigmoid(jnp.einsum('blnk,nkj->blnj', ub, w_a[d]).reshape(bsz, L, LRU_W) + b_a[d])
        i = jax.nn.sigmoid(jnp.einsum('blnk,nkj->blnj', ub, w_x[d]).reshape(bsz, L, LRU_W) + b_x[d])
        log_a = -LRU_C * r.astype(jnp.float32) * jax.nn.softplus(-lam[d].astype(jnp.float32))
        a = jnp.exp(log_a)
        inp = jnp.sqrt(-jnp.expm1(2.0 * log_a)) * (i * u).astype(jnp.float32)
        h = h + linear_scan(a, inp, reverse=(d == 1))
    y = (h * gate.astype(jnp.float32)).astype(x.dtype)
    return y @ w_out


def ssd_chunked(xs, dt, A, B, C):
    bsz, L, H, P = xs.shape
    G, N = B.shape[-2:]
    Hg = H // G
    nc = L // SSD_CHUNK
    f32 = jnp.float32
    x = xs.astype(f32).reshape(bsz, nc, SSD_CHUNK, G, Hg, P)
    dt = dt.astype(f32).reshape(bsz, nc, SSD_CHUNK, G, Hg)
    B = B.astype(f32).reshape(bsz, nc, SSD_CHUNK, G, N)
    C = C.astype(f32).reshape(bsz, nc, SSD_CHUNK, G, N)
    xdt = x * dt[..., None]
    a_cum = jnp.cumsum(dt * A.astype(f32).reshape(G, Hg), axis=2)
    causal = jnp.tril(jnp.ones((SSD_CHUNK, SSD_CHUNK), dtype=bool))
    seg = a_cum[:, :, :, None] - a_cum[:, :, None, :]
    decay = jnp.exp(jnp.where(causal[:, :, None, None], seg, -jnp.inf))
    cb = jnp.einsum('bclgn,bcsgn->bclsg', C, B)
    y_diag = jnp.einsum('bclsgh,bcsghp->bclghp', cb[..., None] * decay, xdt)
    decay_end = jnp.exp(a_cum[:, :, -1:] - a_cum)
    states = jnp.einsum('bclgn,bclghp->bcghpn', B, xdt * decay_end[..., None])
    chunk_decay = jnp.exp(a_cum[:, :, -1])

    def carry_step(h, inp):
        st, dec = inp
        return h * dec[..., None, None] + st, h

    h0 = jnp.zeros((bsz, G, Hg, P, N), f32)
    _, h_in = lax.scan(carry_step, h0, (jnp.moveaxis(states, 1, 0), jnp.moveaxis(chunk_decay, 1, 0)))
    h_in = jnp.moveaxis(h_in, 0, 1)
    y_off = jnp.einsum('bclgn,bcghpn->bclghp', C, h_in) * jnp.exp(a_cum)[..., None]
    return (y_diag + y_off).reshape(bsz, L, H, P)


def ssd_mixer(x, w_in, conv_w, conv_b, dt_bias, a_log, d_skip, norm_g, w_out):
    bsz, L, _ = x.shape
    f32 = jnp.float32
    proj = x @ w_in
    z, xbc, dt = jnp.split(proj, [SSD_D_INNER, SSD_D_INNER + SSD_CONV_DIM], axis=-1)
    xbc = jax.nn.silu(centred_dwconv(xbc, conv_w, conv_b))
    xs, Bm, Cm = jnp.split(xbc, [SSD_D_INNER, SSD_D_INNER + SSD_GROUPS * SSD_STATE], axis=-1)
    xs = xs.reshape(bsz, L, SSD_HEADS, SSD_HEAD_DIM)
    Bm = Bm.reshape(bsz, L, SSD_GROUPS, SSD_STATE)
    Cm = Cm.reshape(bsz, L, SSD_GROUPS, SSD_STATE)
    dt = jax.nn.softplus(dt.astype(f32).reshape(bsz, L, 2, SSD_HEADS) + dt_bias.astype(f32))
    A = -jnp.exp(a_log.astype(f32))
    flip = lambda t: jnp.flip(t, axis=1)
    y_f = ssd_chunked(xs, dt[:, :, 0], A[0], Bm, Cm)
    y_b = flip(ssd_chunked(flip(xs), flip(dt[:, :, 1]), A[1], flip(Bm), flip(Cm)))
    y = y_f + y_b + xs.astype(f32) * d_skip.astype(f32)[:, None]
    y = y.reshape(bsz, L, SSD_D_INNER) * jax.nn.silu(z.astype(f32))
    yg = y.reshape(bsz, L, SSD_GROUPS, SSD_D_INNER // SSD_GROUPS)
    yg = yg * lax.rsqrt(jnp.mean(yg * yg, axis=-1, keepdims=True) + LN_EPS)
    y = (yg.reshape(bsz, L, SSD_D_INNER) * norm_g).astype(x.dtype)
    return y @ w_out


def natten_mixer(x, w_qkv, b_qkv, rpb, w_out):
    bsz, L, _ = x.shape
    rows = L // GRID_W
    kh = min(WIN_H, rows)
    qkv = (x @ w_qkv + b_qkv).reshape(bsz, rows, GRID_W, 3, NA_HEADS, NA_HEAD_DIM)
    q = qkv[:, :, :, 0] * (NA_HEAD_DIM ** -0.5)
    k = qkv[:, :, :, 1]
    v = qkv[:, :, :, 2]
    qcol = np.arange(GRID_W).reshape(NA_NBLK, NA_QBLK)
    kstart = np.clip(np.arange(NA_NBLK) * NA_QBLK - WIN_W // 2, 0, GRID_W - NA_KBLK)
    kcol = kstart[:, None] + np.arange(NA_KBLK)
    wstart = np.clip(qcol - WIN_W // 2, 0, GRID_W - WIN_W)
    col_ok = (kcol[:, None, :] >= wstart[..., None]) & (kcol[:, None, :] < wstart[..., None] + WIN_W)
    dx_idx = np.clip(kcol[:, None, :] - qcol[:, :, None] + WIN_W - 1, 0, 2 * WIN_W - 2)
    col_bias = jnp.transpose(rpb[:, :, dx_idx], (0, 2, 3, 1, 4)).astype(jnp.float32)
    kc = k[:, :, kcol]
    vc = v[:, :, kcol]
    q = q.reshape(bsz, rows, NA_NBLK, NA_QBLK, NA_HEADS, NA_HEAD_DIM)

    def row_block(r):
        rs = jnp.clip(r - kh // 2, 0, rows - kh)
        kr = lax.dynamic_slice_in_dim(kc, rs, kh, axis=1)
        vr = lax.dynamic_slice_in_dim(vc, rs, kh, axis=1)
        qr = lax.dynamic_index_in_dim(q, r, axis=1, keepdims=False)
        s = jnp.einsum('bjqhd,bkjchd->bhjqkc', qr, kr).astype(jnp.float32)
        dy_idx = rs + jnp.arange(kh) - r + (WIN_H - 1)
        s = s + jnp.take(col_bias, dy_idx, axis=3)[None]
        s = jnp.where(col_ok[:, :, None, :], s, -jnp.inf)
        p = jax.nn.softmax(s.reshape(bsz, NA_HEADS, NA_NBLK, NA_QBLK, kh * NA_KBLK), axis=-1)
        p = p.reshape(s.shape).astype(vr.dtype)
        return jnp.einsum('bhjqkc,bkjchd->bjqhd', p, vr)

    o = lax.map(row_block, jnp.arange(rows))
    o = jnp.moveaxis(o, 0, 1).reshape(bsz, L, D_MODEL)
    return o @ w_out


def sq_relu_mlp(x, w_up, w_down):
    return jnp.square(jax.nn.relu(x @ w_up)) @ w_down


def setup_inputs(seed: int = 0) -> dict:
    key = jax.random.key(seed)
    ks = iter(jax.random.split(key, 32))
    f32 = jnp.float32

    def nrm(shape, fan_in, scale=1.0):
        return jax.random.normal(next(ks), shape, f32) * (scale * fan_in ** -0.5)

    def small(shape, s=0.01):
        return jax.random.normal(next(ks), shape, f32) * s

    x_prompt = jax.random.normal(next(ks), (BATCH, SEQ, D_MODEL), f32)
    x_sample = jax.random.normal(next(ks), (DEC_BATCH, DEC_SEQ, D_MODEL), f32)
    nR, nS, nN = N_RG_LAYERS, N_SSD_LAYERS, N_NA_LAYERS
    rg_w_in = nrm((nR, D_MODEL, 2 * LRU_W), D_MODEL)
    rg_conv_w = nrm((nR, CONV_W, LRU_W), CONV_W)
    rg_conv_b = small((nR, LRU_W))
    rg_w_a = nrm((nR, 2, LRU_BLOCKS, LRU_BW, LRU_BW), LRU_BW)
    rg_b_a = small((nR, 2, LRU_W))
    rg_w_x = nrm((nR, 2, LRU_BLOCKS, LRU_BW, LRU_BW), LRU_BW)
    rg_b_x = small((nR, 2, LRU_W))
    a0 = jax.random.uniform(next(ks), (nR, 2, LRU_W), f32, minval=0.9, maxval=0.999)
    s0 = a0 ** (1.0 / LRU_C)
    rg_lambda = jnp.log(s0) - jnp.log1p(-s0)
    rg_w_out = nrm((nR, LRU_W, D_MODEL), LRU_W, BETA)
    ssd_w_in = nrm((nS, D_MODEL, SSD_IN_DIM), D_MODEL)
    ssd_conv_w = nrm((nS, CONV_W, SSD_CONV_DIM), CONV_W)
    ssd_conv_b = small((nS, SSD_CONV_DIM))
    dt0 = jnp.exp(jax.random.uniform(next(ks), (nS, 2, SSD_HEADS), f32, minval=math.log(1e-3), maxval=math.log(1e-1)))
    ssd_dt_bias = dt0 + jnp.log(-jnp.expm1(-dt0))
    ssd_a_log = jnp.log(jax.random.uniform(next(ks), (nS, 2, SSD_HEADS), f32, minval=1.0, maxval=16.0))
    ssd_d = 1.0 + small((nS, SSD_HEADS))
    ssd_norm_g = 1.0 + small((nS, SSD_D_INNER))
    ssd_w_out = nrm((nS, SSD_D_INNER, D_MODEL), SSD_D_INNER, BETA)
    na_w_qkv = nrm((nN, D_MODEL, 3 * D_MODEL), D_MODEL)
    na_b_qkv = small((nN, 3 * D_MODEL))
    na_rpb = small((nN, NA_HEADS, 2 * WIN_H - 1, 2 * WIN_W - 1), 0.02)
    na_w_out = nrm((nN, D_MODEL, D_MODEL), D_MODEL, BETA)
    mlp_w_up = nrm((DEPTH, D_MODEL, MLP_HIDDEN), D_MODEL)
    mlp_w_down = nrm((DEPTH, MLP_HIDDEN, D_MODEL), MLP_HIDDEN, BETA)
    ln1_g = 1.0 + small((DEPTH, D_MODEL))
    ln1_b = small((DEPTH, D_MODEL))
    ln2_g = 1.0 + small((DEPTH, D_MODEL))
    ln2_b = small((DEPTH, D_MODEL))
    return {"x_prompt": x_prompt, "x_sample": x_sample,
            "rg_w_in": rg_w_in, "rg_conv_w": rg_conv_w, "rg_conv_b": rg_conv_b,
            "rg_w_a": rg_w_a, "rg_b_a": rg_b_a, "rg_w_x": rg_w_x, "rg_b_x": rg_b_x,
            "rg_lambda": rg_lambda, "rg_w_out": rg_w_out,
            "ssd_w_in": ssd_w_in, "ssd_conv_w": ssd_conv_w, "ssd_conv_b": ssd_conv_b,
            "ssd_dt_bias": ssd_dt_bias, "ssd_a_log": ssd_a_log, "ssd_d": ssd_d,
            "ssd_norm_g": ssd_norm_g, "ssd_w_out": ssd_w_out,
            "na_w_qkv": na_w_qkv, "na_b_qkv": na_b_qkv, "na_rpb": na_rpb, "na_w_out": na_w_out,
            "mlp_w_up": mlp_w_up, "mlp_w_down": mlp_w_down,
            "ln1_g": ln1_g, "ln1_b": ln1_b, "ln2_g": ln2_g, "ln2_b": ln2_b}


def reference(x_prompt, x_sample,
              rg_w_in, rg_conv_w, rg_conv_b, rg_w_a, rg_b_a, rg_w_x, rg_b_x, rg_lambda, rg_w_out,
              ssd_w_in, ssd_conv_w, ssd_conv_b, ssd_dt_bias, ssd_a_log, ssd_d, ssd_norm_g, ssd_w_out,
              na_w_qkv, na_b_qkv, na_rpb, na_w_out,
              mlp_w_up, mlp_w_down,
              ln1_g, ln1_b, ln2_g, ln2_b):
    def trunk(x):
        for i in range(DEPTH):
            kind = i % N_MIXERS
            j = i // N_MIXERS
            if kind == 0:
                h = rglru_mixer(x, rg_w_in[j], rg_conv_w[j], rg_conv_b[j], rg_w_a[j], rg_b_a[j],
                                rg_w_x[j], rg_b_x[j], rg_lambda[j], rg_w_out[j])
            elif kind == 1:
                h = ssd_mixer(x, ssd_w_in[j], ssd_conv_w[j], ssd_conv_b[j], ssd_dt_bias[j],
                              ssd_a_log[j], ssd_d[j], ssd_norm_g[j], ssd_w_out[j])
            else:
                h = natten_mixer(x, na_w_qkv[j], na_b_qkv[j], na_rpb[j], na_w_out[j])
            x = layer_norm(ALPHA * x + h, ln1_g[i], ln1_b[i])
            x = layer_norm(ALPHA * x + sq_relu_mlp(x, mlp_w_up[i], mlp_w_down[i]), ln2_g[i], ln2_b[i])
        return x

    y_prompt = trunk(x_prompt)
    y_sample = trunk(x_sample)
    return (y_prompt, y_sample)
```

```python
import contextlib
import math
import os
import numpy as np
import concourse.bass as bass
import concourse.mybir as mybir
from concourse.bass_utils import run_bass_kernel_spmd

F32 = mybir.dt.float32
BF16 = mybir.dt.bfloat16
AF = mybir.ActivationFunctionType
ALU = mybir.AluOpType

D = 2048
DC = 16
DEPTH = 4
ALPHA = (2 * DEPTH) ** 0.25
LN_EPS = 1e-5
NT = 512
MLP_H = 8192
NEG = -30000.0
GELU_K = 1.5957691216057308
NS = 8


_uc = [0]


def _un():
    _uc[0] += 1
    return "t%d_" % _uc[0]


class Buf:
    __slots__ = ("w", "r")

    def __init__(self):
        self.w = None
        self.r = {}


class Tl:
    def __init__(self, t, n=1):
        self.t = t
        self.b = [Buf() for _ in range(n)]


class TR:
    def __init__(self, nc, es):
        self.nc = nc
        self.eng = {"pe": nc.tensor, "act": nc.scalar, "dve": nc.vector, "pool": nc.gpsimd, "sp": nc.sync}
        self.sem = {k: es.enter_context(nc.semaphore("p_" + k)) for k in ("pe", "act", "dve", "pool")}
        self.cnt = {k: 0 for k in self.sem}
        self.dsem = {q: [es.enter_context(nc.semaphore("d_%s%d" % (q, i))) for i in range(NS)] for q in ("sp", "pool")}
        self.dval = {q: [0] * NS for q in self.dsem}
        self.dn = {q: 0 for q in self.dsem}
        self.waited = {}

    def _wait(self, e, ev):
        key, sem, val = ev
        k = (e, key)
        if self.waited.get(k, 0) >= val:
            return
        self.eng[e].wait_ge(sem, val)
        self.waited[k] = val

    def _deps(self, e, R, W, same):
        for b in R:
            if b.w is not None and (same or b.w[0] != e):
                self._wait(e, b.w)
        for b in W:
            if b.w is not None and (same or b.w[0] != e):
                self._wait(e, b.w)
            for ev in b.r.values():
                if same or ev[0] != e:
                    self._wait(e, ev)

    def op(self, e, fn, R=(), W=(), inc=True):
        self._deps(e, R, W, e != "pe")
        ins = fn()
        if inc:
            self.cnt[e] += 1
            ins.then_inc(self.sem[e], 1)
            ev = (e, self.sem[e], self.cnt[e])
        else:
            ev = (e, self.sem[e], self.cnt[e] + 1)
        for b in R:
            b.r[e] = ev
        for b in W:
            b.w = ev
            b.r = {}
        return ins

    def dma(self, q, out, in_, R=(), W=()):
        i = self.dn[q] % NS
        self.dn[q] += 1
        sem = self.dsem[q][i]
        key = "%s%d" % (q, i)
        if self.dval[q][i] > 0:
            self._wait(q, (key, sem, self.dval[q][i]))
        self._deps(q, R, W, True)
        ins = self.eng[q].dma_start(out=out, in_=in_)
        ins.then_inc(sem, 16)
        self.dval[q][i] += 16
        ev = (key, sem, self.dval[q][i])
        for b in R:
            b.r[key] = ev
        for b in W:
            b.w = ev
            b.r = {}

    def barrier(self):
        evs = [(k, self.sem[k], self.cnt[k]) for k in self.sem if self.cnt[k] > 0]
        for q in self.dsem:
            for i in range(NS):
                if self.dval[q][i] > 0:
                    evs.append(("%s%d" % (q, i), self.dsem[q][i], self.dval[q][i]))
        for e in self.eng:
            for ev in evs:
                self._wait(e, ev)


class Ctx:
    pass


def _vec_layout(v):
    v = np.asarray(v, np.float32).reshape(-1)
    return np.ascontiguousarray(v.reshape(-1, 128).T)


def build(LA, LB, layers, wshapes, nvec, nrow, natab_shape):
    T = LA + LB
    seqs = [(0, LA), (LA, LB)]
    nc = bass.Bass("TRN2", target_bir_lowering=False)
    g = Ctx()
    g.nc = nc

    def din(name, shape, dt=F32):
        return nc.dram_tensor(name, list(shape), dt, kind="ExternalInput").ap()

    def dscr(name, shape, dt=F32):
        return nc.dram_tensor(name, list(shape), dt, kind="Internal").ap()

    xa = din("xa", [LA, D])
    xbm = din("xb", [LB, D])
    ya = nc.dram_tensor("ya", [LA, D], F32, kind="ExternalOutput").ap()
    yb = nc.dram_tensor("yb", [LB, D], F32, kind="ExternalOutput").ap()
    W = {k: din(k, s) for k, s in wshapes.items()}
    vecs_d = din("vecs", [128, nvec])
    rows_d = din("rows", [1, nrow])
    ident_d = din("ident", [128, 128])
    tri_d = din("tri", [4, 128, 128])
    if natab_shape is not None:
        natab_d = din("natab", natab_shape)

    XT = dscr("XT", [D, T])
    XTv = XT.rearrange("(c p) t -> p c t", p=128)

    with contextlib.ExitStack() as es:
        tr = TR(nc, es)
        vecs = Tl(es.enter_context(nc.sbuf_tensor(_un() + "vecs", [128, nvec], F32)))
        nvecs = Tl(es.enter_context(nc.sbuf_tensor(_un() + "nvecs", [128, nvec], F32)))
        ident = Tl(es.enter_context(nc.sbuf_tensor(_un() + "ident", [128, 128], F32)))
        ones_f = Tl(es.enter_context(nc.sbuf_tensor(_un() + "ones_f", [128, 128], F32)))
        ones_b = Tl(es.enter_context(nc.sbuf_tensor(_un() + "ones_b", [128, 128], BF16)))
        ps = [Tl(es.enter_context(nc.psum_tensor(_un() + "ps%d" % i, [128, 512], F32))) for i in range(8)]
        tr.dma("sp", vecs.t[:], vecs_d, W=vecs.b)
        tr.dma("sp", ident.t[:], ident_d, W=ident.b)
        tri = Tl(es.enter_context(nc.sbuf_tensor(_un() + "tri", [128, 4, 128], F32)))
        tr.dma("sp", tri.t[:], tri_d.rearrange("n p q -> p n q"), W=tri.b)
        g.tri = tri
        tr.op("act", lambda: nc.scalar.activation(nvecs.t[:], vecs.t[:], AF.Copy, scale=-1.0), R=vecs.b, W=nvecs.b)
        tr.op("dve", lambda: nc.vector.memset(ones_f.t[:], 1.0), W=ones_f.b)
        tr.op("dve", lambda: nc.vector.memset(ones_b.t[:], 1.0), W=ones_b.b)
        g.tr, g.vecs, g.nvecs, g.ident, g.ones_f, g.ones_b, g.ps = tr, vecs, nvecs, ident, ones_f, ones_b, ps
        psi = [0]

        def nps(lo=0, hi=8):
            i = lo + psi[0] % (hi - lo)
            psi[0] += 1
            return ps[i]

        rr = [0]

        def rot(engs):
            rr[0] += 1
            return engs[rr[0] % len(engs)]

        def copy_op(e, out, in_, R, Wb):
            if e == "act":
                tr.op("act", lambda: nc.scalar.copy(out, in_), R=R, W=Wb)
            elif e == "dve":
                tr.op("dve", lambda: nc.vector.tensor_copy(out, in_), R=R, W=Wb)
            else:
                tr.op("pool", lambda: nc.gpsimd.tensor_copy(out, in_), R=R, W=Wb)

        Wb = {}

        def convert_all(specs):
            with contextlib.ExitStack() as ps_:
                stage = [Tl(ps_.enter_context(nc.sbuf_tensor(_un() + "cst%d" % i, [128, 8192], F32))) for i in range(3)]
                cvt = [Tl(ps_.enter_context(nc.sbuf_tensor(_un() + "ccv%d" % i, [128, 8192], BF16))) for i in range(3)]
                n = 0
                for name, src, KC, NF in specs:
                    PW = min(8192 // KC, NF)
                    npan = NF // PW
                    dst = dscr("wb_" + name, [npan, 128, KC * PW], BF16)
                    Wb[name] = (dst, KC, NF, PW)
                    sv = src.rearrange("(k p) f -> p k f", p=128)
                    for pi in range(npan):
                        s = n % 3
                        n += 1
                        tr.dma("sp", stage[s].t[:, 0:KC * PW].rearrange("p (k f) -> p k f", k=KC), sv[:, :, pi * PW:(pi + 1) * PW],
                               W=stage[s].b)
                        copy_op(("act", "dve", "pool", "dve", "act")[n % 5], cvt[s].t[:, 0:KC * PW], stage[s].t[:, 0:KC * PW], stage[s].b, cvt[s].b)
                        tr.dma("pool", dst[pi], cvt[s].t[:, 0:KC * PW], R=cvt[s].b)
            tr.barrier()

        specs = []
        for li, (kind, j) in enumerate(layers):
            if kind == "rg" and ("rg_in%d" % j) not in [s[0] for s in specs]:
                specs.append(("rg_in%d" % j, W["rg_w_in"][j], 16, 4096))
                specs.append(("rg_out%d" % j, W["rg_w_out"][j], 16, 2048))
            if kind == "ssd" and "ssd_in" not in [s[0] for s in specs]:
                specs.append(("ssd_in", W["ssd_w_in"][j][:, 0:10240], 16, 10240))
                specs.append(("ssd_dt", W["ssd_w_in"][j][:, 10240:10368], 16, 128))
                specs.append(("ssd_out", W["ssd_w_out"][j], 32, 2048))
            if kind == "na" and "na_qkv" not in [s[0] for s in specs]:
                specs.append(("na_qkv", W["na_w_qkv"][j], 16, 6144))
                specs.append(("na_out", W["na_w_out"][j], 16, 2048))
            specs.append(("up%d" % li, W["mlp_w_up"][li], 16, 8192))
            specs.append(("dn%d" % li, W["mlp_w_down"][li], 64, 2048))
        convert_all(specs)

        def linear(wname, wslots, wctr, rhs, rhsR, consume, f_lo=0, f_hi=None, pslo=0, pshi=4):
            dst, KC, NF, PW = Wb[wname]
            cpp = PW // 128
            if f_hi is None:
                f_hi = NF // 128
            for pi in range(f_lo // cpp, (f_hi + cpp - 1) // cpp):
                slot = wslots[wctr[0] % len(wslots)]
                wctr[0] += 1
                tr.dma("sp", slot.t[:, 0:KC * PW], dst[pi], W=slot.b)
                wv = slot.t[:, 0:KC * PW].rearrange("p (k f) -> p k f", k=KC)
                for cc in range(cpp):
                    fc = pi * cpp + cc
                    if fc < f_lo or fc >= f_hi:
                        continue
                    p_ = nps(pslo, pshi)
                    for kc in range(KC):
                        tr.op("pe", lambda: nc.tensor.matmul(p_.t[:], lhsT=wv[:, kc, cc * 128:(cc + 1) * 128], rhs=rhs(kc),
                                                            start=(kc == 0), stop=(kc == KC - 1)),
                              R=slot.b + rhsR(kc), W=p_.b, inc=(kc == KC - 1))
                    consume(fc, p_)

        def transpose_in():
            with contextlib.ExitStack() as ps_:
                xt = [Tl(ps_.enter_context(nc.sbuf_tensor(_un() + "p0x%d" % i, [128, 4, D], F32))) for i in range(2)]
                xo = [Tl(ps_.enter_context(nc.sbuf_tensor(_un() + "p0o%d" % i, [128, DC, NT], F32)), 4) for i in range(2)]
                n = 0
                for (src, (t0, L)) in ((xa, seqs[0]), (xbm, seqs[1])):
                    sv = src.rearrange("(n b p) d -> n p b d", p=128, b=4)
                    for ti in range(L // NT):
                        a, o = xt[n % 2], xo[n % 2]
                        n += 1
                        tr.dma("sp", a.t[:], sv[ti], W=a.b)
                        for c in range(DC):
                            p_ = nps()
                            for b in range(4):
                                tr.op("pe", lambda: nc.tensor.transpose(p_.t[:, b * 128:(b + 1) * 128], a.t[:, b, c * 128:(c + 1) * 128],
                                                                        ident.t[:]),
                                      R=a.b + ident.b, W=p_.b, inc=(b == 3))
                            copy_op(("act", "dve")[c % 2], o.t[:, c, :], p_.t[:], p_.b, [o.b[c // 4]])
                            if c % 4 == 3:
                                q = c // 4
                                tr.dma("pool", XTv[:, q * 4:(q + 1) * 4, t0 + ti * NT:t0 + (ti + 1) * NT], o.t[:, q * 4:(q + 1) * 4, :],
                                       R=[o.b[q]])
            tr.barrier()

        def transpose_out():
            with contextlib.ExitStack() as ps_:
                xi = [Tl(ps_.enter_context(nc.sbuf_tensor(_un() + "pzx%d" % i, [128, DC, NT], F32))) for i in range(2)]
                xo = [Tl(ps_.enter_context(nc.sbuf_tensor(_un() + "pzo%d" % i, [128, 4, D], F32)), 4) for i in range(2)]
                n = 0
                for (dst, (t0, L)) in ((ya, seqs[0]), (yb, seqs[1])):
                    dv = dst.rearrange("(n b p) d -> n p b d", p=128, b=4)
                    for ti in range(L // NT):
                        a, o = xi[n % 2], xo[n % 2]
                        n += 1
                        tr.dma("sp", a.t[:], XTv[:, :, t0 + ti * NT:t0 + (ti + 1) * NT], W=a.b)
                        for b in range(4):
                            for c4 in range(4):
                                p_ = nps()
                                for cc in range(4):
                                    c = c4 * 4 + cc
                                    tr.op("pe", lambda: nc.tensor.transpose(p_.t[:, cc * 128:(cc + 1) * 128], a.t[:, c, b * 128:(b + 1) * 128],
                                                                            ident.t[:]),
                                          R=a.b + ident.b, W=p_.b, inc=(cc == 3))
                                copy_op(("act", "dve")[c4 % 2], o.t[:, b, c4 * 512:(c4 + 1) * 512], p_.t[:], p_.b, [o.b[b]])
                            tr.dma("pool", dv[ti][:, b, :], o.t[:, b, :], R=[o.b[b]])
            tr.barrier()

        def post_pass(li, outname, KCY, Ysc, vc):
            Yv = Ysc.rearrange("(c p) t -> p c t", p=128)
            with contextlib.ExitStack() as ps_:
                hb = Tl(ps_.enter_context(nc.sbuf_tensor(_un() + "hb", [128, 64, NT], BF16)), 64)
                xres = Tl(ps_.enter_context(nc.sbuf_tensor(_un() + "xres", [128, DC, NT], F32)), DC)
                xb = Tl(ps_.enter_context(nc.sbuf_tensor(_un() + "xbf", [128, DC, NT], BF16)), DC)
                wsl = [Tl(ps_.enter_context(nc.sbuf_tensor(_un() + "wsl%d" % i, [128, 8192], BF16))) for i in range(3)]
                tmp = [Tl(ps_.enter_context(nc.sbuf_tensor(_un() + "ptmp%d" % i, [128, NT], F32))) for i in range(3)]
                tmpb = [Tl(ps_.enter_context(nc.sbuf_tensor(_un() + "ptmpb%d" % i, [128, NT], BF16))) for i in range(3)]
                st = [Tl(ps_.enter_context(nc.sbuf_tensor(_un() + "pst%d" % i, [128, NT], F32))) for i in range(3)]
                wctr = [0]
                tctr = [0]

                def layer_norm(gcol, bcol, want_bf):
                    s1, s2 = ps[4], ps[5]
                    for c in range(DC):
                        t_ = tmpb[tctr[0] % 3]
                        tctr[0] += 1
                        tr.op("act", lambda: nc.scalar.activation(t_.t[:], xres.t[:, c, :], AF.Square), R=[xres.b[c]], W=t_.b)
                        tr.op("pe", lambda: nc.tensor.matmul(s1.t[:], lhsT=ones_f.t[:], rhs=xres.t[:, c, :], start=(c == 0), stop=(c == DC - 1)),
                              R=ones_f.b + [xres.b[c]], W=s1.b)
                        tr.op("pe", lambda: nc.tensor.matmul(s2.t[:], lhsT=ones_b.t[:], rhs=t_.t[:], start=(c == 0), stop=(c == DC - 1)),
                              R=ones_b.b + t_.b, W=s2.b)
                    m, v, r = st
                    tr.op("act", lambda: nc.scalar.activation(m.t[:], s1.t[:], AF.Copy, scale=1.0 / D), R=s1.b, W=m.b)
                    tr.op("dve", lambda: nc.vector.tensor_tensor(v.t[:], m.t[:], m.t[:], ALU.mult), R=m.b, W=v.b)
                    tr.op("dve", lambda: nc.vector.scalar_tensor_tensor(v.t[:], s2.t[:], 1.0 / D, v.t[:], ALU.mult, ALU.subtract),
                          R=s2.b + v.b, W=v.b)
                    tr.op("dve", lambda: nc.vector.tensor_scalar(v.t[:], v.t[:], LN_EPS, None, ALU.add), R=v.b, W=v.b)
                    tr.op("act", lambda: nc.scalar.activation(r.t[:], v.t[:], AF.Ln), R=v.b, W=r.b)
                    tr.op("act", lambda: nc.scalar.activation(r.t[:], r.t[:], AF.Exp, scale=-0.5), R=r.b, W=r.b)
                    for c in range(DC):
                        xc = xres.t[:, c, :]
                        tr.op("dve", lambda: nc.vector.tensor_tensor(xc, xc, m.t[:], ALU.subtract), R=[xres.b[c]] + m.b, W=[xres.b[c]])
                        tr.op("dve", lambda: nc.vector.tensor_tensor(xc, xc, r.t[:], ALU.mult), R=[xres.b[c]] + r.b, W=[xres.b[c]])
                        if want_bf:
                            tr.op("act", lambda: nc.scalar.activation(xb.t[:, c, :], xc, AF.Identity, bias=vecs.t[:, bcol + c:bcol + c + 1],
                                                                      scale=vecs.t[:, gcol + c:gcol + c + 1]),
                                  R=[xres.b[c]] + vecs.b, W=[xb.b[c]])
                        tr.op("act", lambda: nc.scalar.activation(xc, xc, AF.Identity, bias=vecs.t[:, bcol + c:bcol + c + 1],
                                                                  scale=vecs.t[:, gcol + c:gcol + c + 1]),
                              R=[xres.b[c]] + vecs.b, W=[xres.b[c]])

                def resid(fc, p_):
                    xc = xres.t[:, fc, :]
                    tr.op("dve", lambda: nc.vector.scalar_tensor_tensor(xc, xc, ALPHA, p_.t[:], ALU.mult, ALU.add),
                          R=[xres.b[fc]] + p_.b, W=[xres.b[fc]])

                def up_consume(fc, p_):
                    t_ = tmp[tctr[0] % 3]
                    tctr[0] += 1
                    tr.op("act", lambda: nc.scalar.activation(t_.t[:], p_.t[:], AF.Relu), R=p_.b, W=t_.b)
                    e = ("dve", "pool")[fc % 2]
                    tr.op(e, lambda: tr.eng[e].tensor_tensor(hb.t[:, fc, :], t_.t[:], t_.t[:], ALU.mult), R=t_.b, W=[hb.b[fc]])

                for (t0, L) in seqs:
                    for ti in range(L // NT):
                        c0 = t0 + ti * NT
                        for q in range(KCY // 4):
                            tr.dma("sp", hb.t[:, q * 4:(q + 1) * 4, :], Yv[:, q * 4:(q + 1) * 4, c0:c0 + NT], W=hb.b[q * 4:(q + 1) * 4])
                        for q in range(4):
                            tr.dma("sp", xres.t[:, q * 4:(q + 1) * 4, :], XTv[:, q * 4:(q + 1) * 4, c0:c0 + NT], W=xres.b[q * 4:(q + 1) * 4])
                        linear(outname, wsl, wctr, lambda kc: hb.t[:, kc, :], lambda kc: [hb.b[kc]], resid)
                        layer_norm(vc["ln1g"], vc["ln1b"], True)
                        linear("up%d" % li, wsl, wctr, lambda kc: xb.t[:, kc, :], lambda kc: [xb.b[kc]], up_consume)
                        linear("dn%d" % li, wsl, wctr, lambda kc: hb.t[:, kc, :], lambda kc: [hb.b[kc]], resid)
                        layer_norm(vc["ln2g"], vc["ln2b"], False)
                        for q in range(4):
                            tr.dma("pool", XTv[:, q * 4:(q + 1) * 4, c0:c0 + NT], xres.t[:, q * 4:(q + 1) * 4, :], R=xres.b[q * 4:(q + 1) * 4])
            tr.barrier()

        g.linear, g.nps, g.copy_op, g.Wb, g.dscr, g.XTv, g.seqs, g.T, g.W = linear, nps, copy_op, Wb, dscr, XTv, seqs, T, W
        g.rows_d, g.tri_d = rows_d, tri_d
        if natab_shape is not None:
            g.natab_d = natab_d

        transpose_in()
        ysc = {}
        for li, (kind, j) in enumerate(layers):
            vc = g_vcols[li]
            if kind == "rg":
                if "rg" not in ysc:
                    ysc["rg"] = dscr("Y_rg", [2048, T], BF16)
                rg_mixer(g, j, vc, ysc["rg"])
                post_pass(li, "rg_out%d" % j, 16, ysc["rg"], vc)
            elif kind == "ssd":
                ysc["ssd"] = dscr("Y_ssd", [4096, T], BF16)
                ssd_mixer(g, j, vc, ysc["ssd"])
                post_pass(li, "ssd_out", 32, ysc["ssd"], vc)
            elif kind == "na":
                ysc["na"] = dscr("Y_na", [2048, T], BF16)
                na_mixer(g, j, vc, ysc["na"])
                post_pass(li, "na_out", 16, ysc["na"], vc)
        transpose_out()
    return nc


g_vcols = []


def rg_mixer(g, j, vc, Ysc):
    nc, tr, vecs, nvecs, ps = g.nc, g.tr, g.vecs, g.nvecs, g.ps
    T = g.T
    XTv = g.XTv
    Gs = g.dscr("rg_G%d" % j, [D, T])
    Us = g.dscr("rg_U%d" % j, [D, T + 8])
    UCs = g.dscr("rg_UC%d" % j, [D, T])
    HFs = g.dscr("rg_HF%d" % j, [D, T])
    Gv = Gs.rearrange("(c p) t -> p c t", p=128)
    Uv = Us.rearrange("(c p) t -> p c t", p=128)
    UCv = UCs.rearrange("(c p) t -> p c t", p=128)
    HFv = HFs.rearrange("(c p) t -> p c t", p=128)
    Yv = Ysc.rearrange("(c p) t -> p c t", p=128)
    upad = [g.seqs[0][0], g.seqs[1][0]]

    with contextlib.ExitStack() as ps_:
        xres = Tl(ps_.enter_context(nc.sbuf_tensor(_un() + "ra_x", [128, DC, NT], F32)), 4)
        xb = Tl(ps_.enter_context(nc.sbuf_tensor(_un() + "ra_xb", [128, DC, NT], BF16)), 4)
        gb = Tl(ps_.enter_context(nc.sbuf_tensor(_un() + "ra_g", [128, DC, NT], F32)), 4)
        ub = Tl(ps_.enter_context(nc.sbuf_tensor(_un() + "ra_u", [128, DC, NT], F32)), 4)
        wsl = [Tl(ps_.enter_context(nc.sbuf_tensor(_un() + "ra_w%d" % i, [128, 8192], BF16))) for i in range(3)]
        t1 = [Tl(ps_.enter_context(nc.sbuf_tensor(_un() + "ra_t1%d" % i, [128, NT], F32))) for i in range(2)]
        t2 = [Tl(ps_.enter_context(nc.sbuf_tensor(_un() + "ra_t2%d" % i, [128, NT], F32))) for i in range(2)]
        zt = Tl(ps_.enter_context(nc.sbuf_tensor(_un() + "ra_z", [128, DC, 4], F32)))
        wctr = [0]
        k = [0]

        def consume(fc, p_):
            if fc < DC:
                a, b = t1[k[0] % 2], t2[k[0] % 2]
                k[0] += 1
                tr.op("act", lambda: nc.scalar.copy(a.t[:], p_.t[:]), R=p_.b, W=a.b)
                tr.op("act", lambda: nc.scalar.activation(b.t[:], p_.t[:], AF.Square), R=p_.b, W=b.b)
                tr.op("dve", lambda: nc.vector.tensor_scalar(b.t[:], b.t[:], 0.044715, 1.0, ALU.mult, ALU.add), R=b.b, W=b.b)
                tr.op("dve", lambda: nc.vector.tensor_tensor(b.t[:], b.t[:], a.t[:], ALU.mult), R=b.b + a.b, W=b.b)
                tr.op("act", lambda: nc.scalar.activation(b.t[:], b.t[:], AF.Exp, scale=-GELU_K), R=b.b, W=b.b)
                tr.op("dve", lambda: nc.vector.tensor_scalar(b.t[:], b.t[:], 1.0, None, ALU.add), R=b.b, W=b.b)
                tr.op("dve", lambda: nc.vector.reciprocal(b.t[:], b.t[:]), R=b.b, W=b.b)
                tr.op("dve", lambda: nc.vector.tensor_tensor(gb.t[:, fc, :], b.t[:], a.t[:], ALU.mult), R=b.b + a.b, W=[gb.b[fc // 4]])
            else:
                c = fc - DC
                g.copy_op(("act", "dve")[c % 2], ub.t[:, c, :], p_.t[:], p_.b, [ub.b[c // 4]])

        for si, (t0, L) in enumerate(g.seqs):
            for ti in range(L // NT):
                c0 = t0 + ti * NT
                for q in range(4):
                    tr.dma("sp", xres.t[:, q * 4:(q + 1) * 4, :], XTv[:, q * 4:(q + 1) * 4, c0:c0 + NT], W=[xres.b[q]])
                    g.copy_op(("act", "dve")[q % 2], xb.t[:, q * 4:(q + 1) * 4, :], xres.t[:, q * 4:(q + 1) * 4, :], [xres.b[q]], [xb.b[q]])
                g.linear("rg_in%d" % j, wsl, wctr, lambda kc: xb.t[:, kc, :], lambda kc: [xb.b[kc // 4]], consume)
                uc0 = upad[si] + ti * NT
                for q in range(4):
                    tr.dma("pool", Gv[:, q * 4:(q + 1) * 4, c0:c0 + NT], gb.t[:, q * 4:(q + 1) * 4, :], R=[gb.b[q]])
                    tr.dma("pool", Uv[:, q * 4:(q + 1) * 4, uc0:uc0 + NT], ub.t[:, q * 4:(q + 1) * 4, :], R=[ub.b[q]])
    tr.barrier()

    cw, cb = vc["rg_cw"], vc["rg_cb"]
    for d in range(2):
        with contextlib.ExitStack() as ps_:
            gw = Tl(ps_.enter_context(nc.sbuf_tensor(_un() + "rb_gw", [128, 2, 8, 2, 256], BF16)))
            uc = Tl(ps_.enter_context(nc.sbuf_tensor(_un() + "rb_uc", [128, DC, NT], F32)), 4)
            ucb = Tl(ps_.enter_context(nc.sbuf_tensor(_un() + "rb_ucb", [128, DC, NT], BF16)), 4)
            hh = Tl(ps_.enter_context(nc.sbuf_tensor(_un() + "rb_h", [128, DC, NT], F32)), 4)
            ab = Tl(ps_.enter_context(nc.sbuf_tensor(_un() + "rb_a", [128, 4, NT], F32)), 4)
            carry = Tl(ps_.enter_context(nc.sbuf_tensor(_un() + "rb_carry", [128, DC], F32)), DC)
            cl = Tl(ps_.enter_context(nc.sbuf_tensor(_un() + "rb_cl", [128, DC], F32)))
            clt = [Tl(ps_.enter_context(nc.sbuf_tensor(_un() + "rb_clt%d" % i, [128, DC], F32))) for i in range(3)]
            ta = [Tl(ps_.enter_context(nc.sbuf_tensor(_un() + "rb_ta%d" % i, [128, NT], F32))) for i in range(2)]
            tb = [Tl(ps_.enter_context(nc.sbuf_tensor(_un() + "rb_tb%d" % i, [128, NT], F32))) for i in range(2)]
            tcx = [Tl(ps_.enter_context(nc.sbuf_tensor(_un() + "rb_tc%d" % i, [128, NT], F32))) for i in range(2)]
            if d == 0:
                halo = Tl(ps_.enter_context(nc.sbuf_tensor(_un() + "rb_halo", [128, DC, NT + 4], F32)), 4)
            else:
                hf = Tl(ps_.enter_context(nc.sbuf_tensor(_un() + "rb_hf", [128, DC, NT], F32)), 4)
                gt = Tl(ps_.enter_context(nc.sbuf_tensor(_un() + "rb_gt", [128, DC, NT], F32)), 4)
                yo = Tl(ps_.enter_context(nc.sbuf_tensor(_un() + "rb_yo", [128, DC, NT], BF16)), 4)
            for gi, nm in enumerate(("rg_w_a", "rg_w_x")):
                src = g.W[nm][j][d].rearrange("n (kc p) o -> p n kc o", p=128)
                tr.dma("pool", gw.t[:, gi], src, W=gw.b)
            lam = vecs.t[:, vc["rg_lam"] + d * DC: vc["rg_lam"] + (d + 1) * DC]
            e_, u_, l_ = clt
            tr.op("act", lambda: nc.scalar.activation(e_.t[:], lam, AF.Exp, scale=-1.0), R=vecs.b, W=e_.b)
            tr.op("dve", lambda: nc.vector.tensor_scalar(u_.t[:], e_.t[:], 1.0, None, ALU.add), R=e_.b, W=u_.b)
            tr.op("act", lambda: nc.scalar.activation(l_.t[:], u_.t[:], AF.Ln), R=u_.b, W=l_.b)
            tr.op("dve", lambda: nc.vector.tensor_scalar(u_.t[:], u_.t[:], -1.0, None, ALU.add), R=u_.b, W=u_.b)
            tr.op("dve", lambda: nc.vector.reciprocal(u_.t[:], u_.t[:]), R=u_.b, W=u_.b)
            tr.op("dve", lambda: nc.vector.tensor_tensor(u_.t[:], u_.t[:], e_.t[:], ALU.mult), R=u_.b + e_.b, W=u_.b)
            tr.op("dve", lambda: nc.vector.tensor_tensor(l_.t[:], l_.t[:], u_.t[:], ALU.mult), R=u_.b + l_.b, W=l_.b)
            tr.op("dve", lambda: nc.vector.tensor_scalar(cl.t[:], l_.t[:], -8.0, None, ALU.mult), R=l_.b, W=cl.b)
            nba = vc["rg_ba"] + d * DC
            nbx = vc["rg_bx"] + d * DC
            k = [0]
            for si, (t0, L) in enumerate(g.seqs):
                tr.op("dve", lambda: nc.vector.memset(carry.t[:], 0.0), W=carry.b)
                ntile = L // NT
                order = range(ntile) if d == 0 else range(ntile - 1, -1, -1)
                for ti in order:
                    c0 = t0 + ti * NT
                    if d == 0:
                        lo = max(ti * NT - 1, 0)
                        hi = min(ti * NT + NT + 2, L)
                        o0 = lo - (ti * NT - 1)
                        if ti == 0:
                            tr.op("pool", lambda: nc.gpsimd.memset(halo.t[:, :, 0:1], 0.0), W=halo.b)
                        if ti == ntile - 1:
                            tr.op("pool", lambda: nc.gpsimd.memset(halo.t[:, :, NT + 1:NT + 3], 0.0), W=halo.b)
                        for q in range(4):
                            tr.dma("sp", halo.t[:, q * 4:(q + 1) * 4, o0:o0 + hi - lo], Uv[:, q * 4:(q + 1) * 4, t0 + lo:t0 + hi], W=[halo.b[q]])
                        for c in range(DC):
                            o = uc.t[:, c, :]
                            tr.op("dve", lambda: nc.vector.tensor_scalar(o, halo.t[:, c, 0:NT], vecs.t[:, cw + c:cw + c + 1], vecs.t[:, cb + c:cb + c + 1],
                                                                         ALU.mult, ALU.add),
                                  R=[halo.b[c // 4]] + vecs.b, W=[uc.b[c // 4]])
                            for kk in range(1, 4):
                                e = "dve"
                                tr.op(e, lambda: tr.eng[e].scalar_tensor_tensor(o, halo.t[:, c, kk:kk + NT],
                                                                                 vecs.t[:, cw + kk * DC + c:cw + kk * DC + c + 1], o, ALU.mult, ALU.add),
                                      R=[halo.b[c // 4], uc.b[c // 4]] + vecs.b, W=[uc.b[c // 4]])
                        for q in range(4):
                            tr.dma("pool", UCv[:, q * 4:(q + 1) * 4, c0:c0 + NT], uc.t[:, q * 4:(q + 1) * 4, :], R=[uc.b[q]])
                    else:
                        for q in range(4):
                            tr.dma("sp", uc.t[:, q * 4:(q + 1) * 4, :], UCv[:, q * 4:(q + 1) * 4, c0:c0 + NT], W=[uc.b[q]])
                            tr.dma("sp", hf.t[:, q * 4:(q + 1) * 4, :], HFv[:, q * 4:(q + 1) * 4, c0:c0 + NT], W=[hf.b[q]])
                            tr.dma("sp", gt.t[:, q * 4:(q + 1) * 4, :], Gv[:, q * 4:(q + 1) * 4, c0:c0 + NT], W=[gt.b[q]])
                    for q in range(4):
                        g.copy_op(("act", "dve")[q % 2], ucb.t[:, q * 4:(q + 1) * 4, :], uc.t[:, q * 4:(q + 1) * 4, :], [uc.b[q]], [ucb.b[q]])
                    for c in range(DC):
                        n, jj = c // 2, c % 2
                        pr, pi_ = g.nps(0, 6), g.nps(0, 6)
                        for gi, p_ in ((0, pr), (1, pi_)):
                            for kc in range(2):
                                tr.op("pe", lambda: nc.tensor.matmul(p_.t[:], lhsT=gw.t[:, gi, n, kc, jj * 128:(jj + 1) * 128], rhs=ucb.t[:, 2 * n + kc, :],
                                                                    start=(kc == 0), stop=(kc == 1)),
                                      R=gw.b + [ucb.b[(2 * n + kc) // 4]], W=p_.b, inc=(kc == 1))
                        a_, b_, c_ = ta[k[0] % 2], tb[k[0] % 2], tcx[k[0] % 2]
                        k[0] += 1
                        av = ab.t[:, c % 4, :]
                        tr.op("act", lambda: nc.scalar.activation(a_.t[:], pr.t[:], AF.Exp, bias=nvecs.t[:, nba + c:nba + c + 1], scale=-1.0),
                              R=pr.b + nvecs.b, W=a_.b)
                        tr.op("act", lambda: nc.scalar.activation(b_.t[:], pi_.t[:], AF.Exp, bias=nvecs.t[:, nbx + c:nbx + c + 1], scale=-1.0),
                              R=pi_.b + nvecs.b, W=b_.b)
                        tr.op("act", lambda: nc.scalar.activation(a_.t[:], a_.t[:], AF.Ln, bias=1.0, scale=1.0), R=a_.b, W=a_.b)
                        tr.op("act", lambda: nc.scalar.activation(b_.t[:], b_.t[:], AF.Ln, bias=1.0, scale=1.0), R=b_.b, W=b_.b)
                        tr.op("act", lambda: nc.scalar.activation(a_.t[:], a_.t[:], AF.Exp, scale=-1.0), R=a_.b, W=a_.b)
                        tr.op("act", lambda: nc.scalar.activation(b_.t[:], b_.t[:], AF.Exp, scale=-1.0), R=b_.b, W=b_.b)
                        tr.op("act", lambda: nc.scalar.activation(av, a_.t[:], AF.Exp, scale=cl.t[:, c:c + 1]), R=a_.b + cl.b, W=[ab.b[c % 4]])
                        tr.op("dve", lambda: nc.vector.tensor_tensor(c_.t[:], av, av, ALU.mult), R=[ab.b[c % 4]], W=c_.b)
                        tr.op("act", lambda: nc.scalar.activation(c_.t[:], c_.t[:], AF.Ln, bias=1.0, scale=-1.0), R=c_.b, W=c_.b)
                        tr.op("act", lambda: nc.scalar.activation(c_.t[:], c_.t[:], AF.Exp, scale=0.5), R=c_.b, W=c_.b)
                        tr.op("dve", lambda: nc.vector.tensor_tensor(b_.t[:], b_.t[:], uc.t[:, c, :], ALU.mult), R=b_.b + [uc.b[c // 4]], W=b_.b)
                        tr.op("dve", lambda: nc.vector.tensor_tensor(b_.t[:], b_.t[:], c_.t[:], ALU.mult), R=b_.b + c_.b, W=b_.b)
                        if d == 0:
                            tr.op("dve", lambda: nc.vector.tensor_tensor_scan(hh.t[:, c, :], av, b_.t[:], carry.t[:, c:c + 1], ALU.mult, ALU.add),
                                  R=[ab.b[c % 4], carry.b[c]] + b_.b, W=[hh.b[c // 4]])
                            tr.op("pool", lambda: nc.gpsimd.tensor_copy(carry.t[:, c:c + 1], hh.t[:, c, NT - 1:NT]), R=[hh.b[c // 4]], W=[carry.b[c]])
                        else:
                            tr.op("dve", lambda: nc.vector.tensor_tensor_scan(hh.t[:, c, ::-1], ab.t[:, c % 4, ::-1], b_.t[:, ::-1], carry.t[:, c:c + 1],
                                                                              ALU.mult, ALU.add),
                                  R=[ab.b[c % 4], carry.b[c]] + b_.b, W=[hh.b[c // 4]])
                            tr.op("pool", lambda: nc.gpsimd.tensor_copy(carry.t[:, c:c + 1], hh.t[:, c, 0:1]), R=[hh.b[c // 4]], W=[carry.b[c]])
                            tr.op("pool", lambda: nc.gpsimd.tensor_tensor(hh.t[:, c, :], hh.t[:, c, :], hf.t[:, c, :], ALU.add),
                                  R=[hh.b[c // 4], hf.b[c // 4]], W=[hh.b[c // 4]])
                            tr.op("dve", lambda: nc.vector.tensor_tensor(yo.t[:, c, :], hh.t[:, c, :], gt.t[:, c, :], ALU.mult),
                                  R=[hh.b[c // 4], gt.b[c // 4]], W=[yo.b[c // 4]])
                    for q in range(4):
                        if d == 0:
                            tr.dma("pool", HFv[:, q * 4:(q + 1) * 4, c0:c0 + NT], hh.t[:, q * 4:(q + 1) * 4, :], R=[hh.b[q]])
                        else:
                            tr.dma("pool", Yv[:, q * 4:(q + 1) * 4, c0:c0 + NT], yo.t[:, q * 4:(q + 1) * 4, :], R=[yo.b[q]])
        tr.barrier()


def ssd_mixer(g, j, vc, Ysc):
    nc, tr, vecs, nvecs, ps, tri, ident, ones_f = g.nc, g.tr, g.vecs, g.nvecs, g.ps, g.tri, g.ident, g.ones_f
    T = g.T
    XTv = g.XTv
    XBC = g.dscr("ssd_XBC", [6144, T])
    Zs = g.dscr("ssd_Z", [4096, T])
    DTA = g.dscr("ssd_DTA", [256, T])
    YA = g.dscr("ssd_YA", [T, 4096])
    XBCv = XBC.rearrange("(c p) t -> p c t", p=128)
    Zv = Zs.rearrange("(c p) t -> p c t", p=128)
    DTAv = DTA.rearrange("(c p) t -> p c t", p=128)
    Yv = Ysc.rearrange("(c p) t -> p c t", p=128)
    V = lambda fn, R, W: tr.op("dve", fn, R=R, W=W)
    A = lambda fn, R, W: tr.op("act", fn, R=R, W=W)
    P = lambda fn, R, W: tr.op("pool", fn, R=R, W=W)
    with contextlib.ExitStack() as ps_:
        xres = Tl(ps_.enter_context(nc.sbuf_tensor(_un() + "s1_x", [128, DC, NT], F32)), 4)
        xb = Tl(ps_.enter_context(nc.sbuf_tensor(_un() + "s1_xb", [128, DC, NT], BF16)), 4)
        stg = [Tl(ps_.enter_context(nc.sbuf_tensor(_un() + "s1_st%d" % i, [128, 4, NT], F32))) for i in range(4)]
        dta = Tl(ps_.enter_context(nc.sbuf_tensor(_un() + "s1_dta", [128, 2, NT], F32)))
        avec = Tl(ps_.enter_context(nc.sbuf_tensor(_un() + "s1_av", [128, 1], F32)))
        wsl = [Tl(ps_.enter_context(nc.sbuf_tensor(_un() + "s1_w%d" % i, [128, 8192], BF16))) for i in range(3)]
        t1 = [Tl(ps_.enter_context(nc.sbuf_tensor(_un() + "s1_t1%d" % i, [128, NT], F32))) for i in range(2)]
        t2 = [Tl(ps_.enter_context(nc.sbuf_tensor(_un() + "s1_t2%d" % i, [128, NT], F32))) for i in range(2)]
        wctr = [0]
        k = [0]
        sg = [0]
        A(lambda: nc.scalar.activation(avec.t[:], vecs.t[:, vc["ssd_alog"]:vc["ssd_alog"] + 1], AF.Exp), vecs.b, avec.b)
        V(lambda: nc.vector.tensor_scalar(avec.t[:], avec.t[:], -1.0, None, ALU.mult), avec.b, avec.b)
        cur = {}

        def consume(fc, p_):
            q = fc // 4
            if fc % 4 == 0:
                cur["s"] = stg[sg[0] % 4]
                sg[0] += 1
            st_ = cur["s"]
            if fc < 32:
                A(lambda: nc.scalar.activation(st_.t[:, fc % 4, :], p_.t[:], AF.Silu), p_.b, st_.b)
            else:
                g.copy_op(("act", "dve")[fc % 2], st_.t[:, fc % 4, :], p_.t[:], p_.b, st_.b)
            if fc % 4 == 3:
                c0 = cur["c0"]
                if fc < 32:
                    tr.dma("pool", Zv[:, q * 4:(q + 1) * 4, c0:c0 + NT], st_.t[:], R=st_.b)
                else:
                    tr.dma("pool", XBCv[:, (q - 8) * 4:(q - 7) * 4, c0:c0 + NT], st_.t[:], R=st_.b)

        def consume_dt(fc, p_):
            a = t1[k[0] % 2]
            k[0] += 1
            A(lambda: nc.scalar.activation(a.t[:], p_.t[:], AF.Exp, bias=vecs.t[:, vc["ssd_dtb"]:vc["ssd_dtb"] + 1], scale=1.0), p_.b + vecs.b, a.b)
            A(lambda: nc.scalar.activation(dta.t[:, 0, :], a.t[:], AF.Ln, bias=1.0, scale=1.0), a.b, dta.b)
            V(lambda: nc.vector.tensor_scalar(dta.t[:, 1, :], dta.t[:, 0, :], avec.t[:, 0:1], None, ALU.mult), dta.b + avec.b, dta.b)
            c0 = cur["c0"]
            tr.dma("pool", DTAv[:, :, c0:c0 + NT], dta.t[:], R=dta.b)

        for si, (t0, L) in enumerate(g.seqs):
            for ti in range(L // NT):
                c0 = t0 + ti * NT
                cur["c0"] = c0
                for q in range(4):
                    tr.dma("sp", xres.t[:, q * 4:(q + 1) * 4, :], XTv[:, q * 4:(q + 1) * 4, c0:c0 + NT], W=[xres.b[q]])
                    g.copy_op(("act", "dve")[q % 2], xb.t[:, q * 4:(q + 1) * 4, :], xres.t[:, q * 4:(q + 1) * 4, :], [xres.b[q]], [xb.b[q]])
                g.linear("ssd_in", wsl, wctr, lambda kc: xb.t[:, kc, :], lambda kc: [xb.b[kc // 4]], consume)
                g.linear("ssd_dt", wsl, wctr, lambda kc: xb.t[:, kc, :], lambda kc: [xb.b[kc // 4]], consume_dt)
    tr.barrier()

    cw, cbv = vc["ssd_cw"], vc["ssd_cb"]
    CH = 128
    CVs = g.dscr("ssd_CV", [6144, T])
    CVv = CVs.rearrange("(c p) t -> p c t", p=128)
    with contextlib.ExitStack() as ps_:
        hl = [Tl(ps_.enter_context(nc.sbuf_tensor(_un() + "sb_hl%d" % i, [128, 8, NT + 4], F32))) for i in range(2)]
        co = [Tl(ps_.enter_context(nc.sbuf_tensor(_un() + "sb_co%d" % i, [128, 8, NT], F32))) for i in range(2)]
        n = 0
        for si, (t0, L) in enumerate(g.seqs):
            ntile = L // NT
            for ti in range(ntile):
                lo = max(ti * NT - 1, 0)
                hi = min(ti * NT + NT + 2, L)
                o0 = lo - (ti * NT - 1)
                for gq in range(6):
                    h_, o_ = hl[n % 2], co[n % 2]
                    n += 1
                    if ti == 0:
                        V(lambda: nc.vector.memset(h_.t[:, :, 0:1], 0.0), [], h_.b)
                    if ti == ntile - 1:
                        V(lambda: nc.vector.memset(h_.t[:, :, NT + 1:NT + 3], 0.0), [], h_.b)
                    tr.dma("sp", h_.t[:, :, o0:o0 + hi - lo], XBCv[:, gq * 8:(gq + 1) * 8, t0 + lo:t0 + hi], W=h_.b)
                    for cc in range(8):
                        c = gq * 8 + cc
                        o = o_.t[:, cc, :]
                        A(lambda: nc.scalar.activation(o, h_.t[:, cc, 0:NT], AF.Identity, bias=vecs.t[:, cbv + c:cbv + c + 1],
                                                       scale=vecs.t[:, cw + c:cw + c + 1]), h_.b + vecs.b, o_.b)
                        for kk in range(1, 4):
                            V(lambda: nc.vector.scalar_tensor_tensor(o, h_.t[:, cc, kk:kk + NT], vecs.t[:, cw + kk * 48 + c:cw + kk * 48 + c + 1], o,
                                                                      ALU.mult, ALU.add), h_.b + vecs.b + o_.b, o_.b)
                    A(lambda: nc.scalar.activation(o_.t[:], o_.t[:], AF.Silu), o_.b, o_.b)
                    tr.dma("pool", CVv[:, gq * 8:(gq + 1) * 8, t0 + ti * NT:t0 + (ti + 1) * NT], o_.t[:], R=o_.b)
    tr.barrier()

    for dr in range(2):
        with contextlib.ExitStack() as ps_:
            sb_ = lambda nm, shp, dt=F32, n=1: Tl(ps_.enter_context(nc.sbuf_tensor(_un() + "s2_" + nm, shp, dt)), n)
            cv = sb_("cv", [128, 48, CH], F32, 6)
            cvb = sb_("cvb", [128, 16, CH], BF16)
            xtok = sb_("xtok", [128, 4096], BF16, 8)
            btok = sb_("btok", [128, 1024], BF16, 2)
            xdt = sb_("xdt", [128, 4096], BF16)
            xdd = xtok
            dain = sb_("dain", [128, 2, CH])
            datok = sb_("datok", [128, 256])
            dcs = sb_("dcs", [128, 192])
            cbm = [sb_("cbm%d" % i, [128, CH]) for i in range(2)]
            srh = [sb_("srh%d" % i, [128, 8, CH]) for i in range(2)]
            ee = [sb_("ee%d" % i, [128, 8, CH]) for i in range(2)]
            mt = [sb_("mt%d" % i, [128, 8, CH], BF16) for i in range(2)]
            h32 = sb_("h32", [128, 8, 512], F32, 8)
            hbf = sb_("hbf", [128, 8, 512], BF16, 8)
            yacc = sb_("yacc", [128, 4096], F32, 8)
            if dr == 1:
                yprev = sb_("yprev", [128, 4096])
                yF = sb_("yF", [128, 32, CH], F32, 8)
                zt = sb_("zt", [128, 32, CH])
                rs_ = sb_("rs", [128, 8, CH])
                yout = sb_("yout", [128, 32, 2 * CH], BF16)
            SLm = tri.t[:, 0 if dr == 0 else 2, :]
            LIm = tri.t[:, 1 if dr == 0 else 3, :]
            pseg = [ps[3], ps[4]]
            py, po, pst = ps[5], ps[6], ps[7]
            gn = [0]
            for si, (t0, L) in enumerate(g.seqs):
                nch = L // CH
                V(lambda: nc.vector.memset(h32.t[:], 0.0), [], h32.b)
                V(lambda: nc.vector.memset(hbf.t[:], 0.0), [], hbf.b)
                order = range(nch) if dr == 0 else range(nch - 1, -1, -1)
                for ci in order:
                    c0 = t0 + ci * CH
                    for q in range(6):
                        tr.dma("sp", cv.t[:, q * 8:(q + 1) * 8, :], CVv[:, q * 8:(q + 1) * 8, c0:c0 + CH], W=[cv.b[q]])
                    A(lambda: nc.scalar.copy(cvb.t[:], cv.t[:, 32:48, :]), cv.b[4:6], cvb.b)
                    for q in range(10):
                        p_ = g.nps(0, 3)
                        for cc in range(4):
                            c = q * 4 + cc
                            tr.op("pe", lambda: nc.tensor.transpose(p_.t[:, cc * 128:(cc + 1) * 128], cv.t[:, c, :], ident.t[:]),
                                  R=cv.b + ident.b, W=p_.b, inc=(cc == 3))
                        if q < 8:
                            g.copy_op(("act", "dve")[q % 2], xtok.t[:, q * 512:(q + 1) * 512], p_.t[:], p_.b, [xtok.b[q]])
                        else:
                            g.copy_op(("act", "dve")[q % 2], btok.t[:, (q - 8) * 512:(q - 7) * 512], p_.t[:], p_.b, [btok.b[q - 8]])
                    tr.dma("sp", dain.t[:], DTAv[:, :, c0:c0 + CH], W=dain.b)
                    p_ = g.nps(0, 3)
                    for cc in range(2):
                        tr.op("pe", lambda: nc.tensor.transpose(p_.t[:, cc * 128:(cc + 1) * 128], dain.t[:, cc, :], ident.t[:]),
                              R=dain.b + ident.b, W=p_.b, inc=(cc == 1))
                    A(lambda: nc.scalar.copy(datok.t[:], p_.t[:, 0:256]), p_.b, datok.b)
                    a_dir = datok.t[:, 128 + dr * 64:128 + (dr + 1) * 64]
                    dt_dir = datok.t[:, dr * 64:(dr + 1) * 64]
                    p_ = g.nps(0, 3)
                    for i_, lm in enumerate((LIm, SLm, ones_f.t[:])):
                        tr.op("pe", lambda: nc.tensor.matmul(p_.t[:, i_ * 64:(i_ + 1) * 64], lhsT=lm, rhs=a_dir, start=True, stop=True),
                              R=tri.b + ones_f.b + datok.b, W=p_.b, inc=(i_ == 2))
                    A(lambda: nc.scalar.activation(dcs.t[:], p_.t[:, 0:192], AF.Exp), p_.b, dcs.b)
                    dec = dcs.t[:, 0:64]
                    dend = dcs.t[:, 64:128]
                    cdec = dcs.t[:, 128:192]
                    V(lambda: nc.vector.tensor_tensor(xdt.t[:].rearrange("p (h q) -> p h q", h=64), xtok.t[:].rearrange("p (h q) -> p h q", h=64),
                                                      dt_dir.unsqueeze(2).to_broadcast([128, 64, 64]), ALU.mult), xtok.b + datok.b, xdt.b)
                    V(lambda: nc.vector.tensor_tensor(xdd.t[:].rearrange("p (h q) -> p h q", h=64), xdt.t[:].rearrange("p (h q) -> p h q", h=64),
                                                      dend.unsqueeze(2).to_broadcast([128, 64, 64]), ALU.mult), xdt.b + dcs.b, xdd.b)
                    for gi in range(8):
                        cb_, sr_, e_, m_ = cbm[gn[0] % 2], srh[gn[0] % 2], ee[gn[0] % 2], mt[gn[0] % 2]
                        gn[0] += 1
                        BT = cvb.t[:, gi, :]
                        CT = cvb.t[:, 8 + gi, :]
                        p_ = g.nps(0, 3)
                        tr.op("pe", lambda: nc.tensor.matmul(p_.t[:, 0:CH], lhsT=BT, rhs=CT, start=True, stop=True), R=cvb.b, W=p_.b)
                        V(lambda: nc.vector.tensor_tensor(cb_.t[:], p_.t[:, 0:CH], LIm, ALU.mult), p_.b + tri.b, cb_.b)
                        V(lambda: nc.vector.tensor_tensor(sr_.t[:], a_dir[:, gi * 8:(gi + 1) * 8].unsqueeze(2).to_broadcast([128, 8, CH]),
                                                          LIm.unsqueeze(1).to_broadcast([128, 8, CH]), ALU.mult), datok.b + tri.b, sr_.b)
                        for i_ in range(2):
                            tr.op("pe", lambda: nc.tensor.matmul(pseg[i_].t[:], lhsT=SLm, rhs=sr_.t[:, i_ * 4:(i_ + 1) * 4, :].rearrange("p h t -> p (h t)"),
                                                                start=True, stop=True), R=tri.b + sr_.b, W=pseg[i_].b)
                            A(lambda: nc.scalar.activation(e_.t[:, i_ * 4:(i_ + 1) * 4, :].rearrange("p h t -> p (h t)"), pseg[i_].t[:], AF.Exp),
                              pseg[i_].b, e_.b)
                        V(lambda: nc.vector.tensor_tensor(m_.t[:], e_.t[:], cb_.t[:].unsqueeze(1).to_broadcast([128, 8, CH]), ALU.mult), e_.b + cb_.b, m_.b)
                        for hh in range(8):
                            hd = gi * 8 + hh
                            tr.op("pe", lambda: nc.tensor.matmul(py.t[:, hh * 64:(hh + 1) * 64], lhsT=m_.t[:, hh, :], rhs=xdt.t[:, hd * 64:(hd + 1) * 64],
                                                                start=True, stop=True), R=m_.b + xdt.b, W=py.b, inc=(hh == 7))
                        tr.op("pe", lambda: nc.tensor.matmul(po.t[:], lhsT=CT, rhs=hbf.t[:, gi, :], start=True, stop=True), R=cvb.b + [hbf.b[gi]], W=po.b)
                        ya = yacc.t[:, gi * 512:(gi + 1) * 512]
                        V(lambda: nc.vector.tensor_tensor(ya.rearrange("p (h q) -> p h q", h=8), po.t[:].rearrange("p (h q) -> p h q", h=8),
                                                          dec[:, gi * 8:(gi + 1) * 8].unsqueeze(2).to_broadcast([128, 8, 64]), ALU.mult),
                          po.b + dcs.b, [yacc.b[gi]])
                        V(lambda: nc.vector.tensor_tensor(ya, ya, py.t[:], ALU.add), py.b + [yacc.b[gi]], [yacc.b[gi]])
                        tr.op("pe", lambda: nc.tensor.matmul(pst.t[:], lhsT=btok.t[:, gi * 128:(gi + 1) * 128], rhs=xdd.t[:, gi * 512:(gi + 1) * 512],
                                                            start=True, stop=True), R=btok.b + xdd.b, W=pst.b)
                        hv = h32.t[:, gi, :]
                        V(lambda: nc.vector.tensor_tensor(hv.rearrange("p (h q) -> p h q", h=8), hv.rearrange("p (h q) -> p h q", h=8),
                                                          cdec[:, gi * 8:(gi + 1) * 8].unsqueeze(2).to_broadcast([128, 8, 64]), ALU.mult),
                          [h32.b[gi]] + dcs.b, [h32.b[gi]])
                        V(lambda: nc.vector.tensor_tensor(hv, hv, pst.t[:], ALU.add), pst.b + [h32.b[gi]], [h32.b[gi]])
                        A(lambda: nc.scalar.copy(hbf.t[:, gi, :], hv), [h32.b[gi]], [hbf.b[gi]])
                    YAv = YA[c0:c0 + CH, :]
                    if dr == 0:
                        tr.dma("pool", YAv, yacc.t[:], R=yacc.b)
                        continue
                    tr.dma("sp", yprev.t[:], YAv, W=yprev.b)
                    tr.dma("sp", zt.t[:], Zv[:, :, c0:c0 + CH], W=zt.b)
                    V(lambda: nc.vector.tensor_tensor(yacc.t[:], yacc.t[:], yprev.t[:], ALU.add), yacc.b + yprev.b, yacc.b)
                    for q in range(8):
                        p_ = g.nps(0, 3)
                        for cc in range(4):
                            c = q * 4 + cc
                            tr.op("pe", lambda: nc.tensor.transpose(p_.t[:, cc * 128:(cc + 1) * 128], yacc.t[:, c * 128:(c + 1) * 128], ident.t[:]),
                                  R=yacc.b + ident.b, W=p_.b, inc=(cc == 3))
                        for cc in range(4):
                            c = q * 4 + cc
                            V(lambda: nc.vector.scalar_tensor_tensor(yF.t[:, c, :], cv.t[:, c, :], vecs.t[:, vc["ssd_D"] + c:vc["ssd_D"] + c + 1],
                                                                      p_.t[:, cc * 128:(cc + 1) * 128], ALU.mult, ALU.add),
                              cv.b + vecs.b + p_.b, [yF.b[q]])
                    V(lambda: nc.vector.tensor_tensor(yF.t[:], yF.t[:], zt.t[:], ALU.mult), yF.b + zt.b, yF.b)
                    sq = yprev
                    sqv = sq.t[:].rearrange("p (c t) -> p c t", c=32)
                    A(lambda: nc.scalar.activation(sqv, yF.t[:], AF.Square), yF.b, sq.b)
                    pn = [g.nps(0, 3), g.nps(0, 3)]
                    for gi in range(8):
                        for i_ in range(4):
                            tr.op("pe", lambda: nc.tensor.matmul(pn[gi // 4].t[:, (gi % 4) * 128:(gi % 4 + 1) * 128], lhsT=ones_f.t[:], rhs=sqv[:, gi * 4 + i_, :],
                                                                start=(i_ == 0), stop=(i_ == 3)), R=ones_f.b + sq.b, W=pn[gi // 4].b,
                                  inc=(i_ == 3 and gi % 4 == 3))
                    for i_ in range(2):
                        V(lambda: nc.vector.tensor_scalar(rs_.t[:, i_ * 4:(i_ + 1) * 4, :].rearrange("p h t -> p (h t)"), pn[i_].t[:], 1.0 / 512, LN_EPS,
                                                          ALU.mult, ALU.add), pn[i_].b, rs_.b)
                    A(lambda: nc.scalar.activation(rs_.t[:], rs_.t[:], AF.Ln), rs_.b, rs_.b)
                    A(lambda: nc.scalar.activation(rs_.t[:], rs_.t[:], AF.Exp, scale=-0.5), rs_.b, rs_.b)
                    sub = ci % 2
                    for c in range(32):
                        V(lambda: nc.vector.scalar_tensor_tensor(yout.t[:, c, sub * CH:(sub + 1) * CH], yF.t[:, c, :], vecs.t[:, vc["ssd_ng"] + c:vc["ssd_ng"] + c + 1],
                                                                  rs_.t[:, c // 4, :], ALU.mult, ALU.mult), yF.b + vecs.b + rs_.b, yout.b)
                    if sub == 0:
                        tc0 = t0 + (ci // 2) * 2 * CH
                        for q in range(4):
                            tr.dma("pool", Yv[:, q * 8:(q + 1) * 8, tc0:tc0 + 2 * CH], yout.t[:, q * 8:(q + 1) * 8, :], R=yout.b)
        tr.barrier()


def na_static():
    R = 32
    kr2 = np.arange(128) // 64
    kc = np.arange(128) % 64
    qr8 = np.arange(512) // 64
    qc = np.arange(512) % 64
    wst = np.clip(qc - 8, 0, 48)
    colok = (kc[:, None] >= wst[None, :]) & (kc[:, None] < wst[None, :] + 16)
    dx = np.clip(kc[:, None] - qc[None, :] + 15, 0, 30)
    valid = np.zeros((3, 8, 128, 512), bool)
    dy = np.zeros((3, 8, 128, 512), np.int64)
    for ty, m in enumerate((0, 1, 3)):
        base = int(np.clip(8 * m - 4, 0, R - 16))
        r = 8 * m + qr8
        rs = np.clip(r - 4, 0, R - 8)
        for jj in range(8):
            kr = base + 2 * jj + kr2
            rowok = (kr[:, None] >= rs[None, :]) & (kr[:, None] < rs[None, :] + 8)
            valid[ty, jj] = rowok & colok
            dy[ty, jj] = np.clip(kr[:, None] - r[None, :] + 7, 0, 14)
    return valid, dy, np.broadcast_to(dx, (3, 8, 128, 512))


def na_table(rpb):
    valid, dy, dx = na_static()
    rpb = np.asarray(rpb, np.float32)
    tab = rpb[:, dy, dx]
    tab = np.where(valid[None], tab, np.float32(NEG)).astype(np.float32)
    return np.ascontiguousarray(np.transpose(tab, (1, 0, 2, 3, 4)))


def na_mixer(g, j, vc, Ysc):
    nc, tr, vecs, ps = g.nc, g.tr, g.vecs, g.ps
    T = g.T
    XTv = g.XTv
    valid, _, _ = na_static()
    jlist = [[jj for jj in range(8) if valid[ty, jj].any()] for ty in range(3)]
    QT = g.dscr("na_QT", [D, T], BF16)
    KT = g.dscr("na_KT", [D, T], BF16)
    VT = g.dscr("na_VT", [T, D], BF16)
    QTv = QT.rearrange("(c p) t -> p c t", p=128)
    KTv = KT.rearrange("(c p) t -> p c t", p=128)
    VTv = VT.rearrange("(n p) f -> p n f", p=128)
    Yv = Ysc.rearrange("(c p) t -> p c t", p=128)
    scale = 128 ** -0.5
    dst, KC, NF, _pw = g.Wb["na_qkv"]
    with contextlib.ExitStack() as ps_:
        xres = Tl(ps_.enter_context(nc.sbuf_tensor(_un() + "n1_x", [128, DC, NT], F32)), 4)
        xb = Tl(ps_.enter_context(nc.sbuf_tensor(_un() + "n1_xb", [128, DC, NT], BF16)), 4)
        qo = Tl(ps_.enter_context(nc.sbuf_tensor(_un() + "n1_q", [128, DC, NT], BF16)), 4)
        ko = Tl(ps_.enter_context(nc.sbuf_tensor(_un() + "n1_k", [128, DC, NT], BF16)), 4)
        vo = Tl(ps_.enter_context(nc.sbuf_tensor(_un() + "n1_v", [128, 4, D], BF16)), 4)
        bv = Tl(ps_.enter_context(nc.sbuf_tensor(_un() + "n1_bv", [128, D], F32)))
        bqs = Tl(ps_.enter_context(nc.sbuf_tensor(_un() + "n1_bqs", [128, DC], F32)))
        wsl = [Tl(ps_.enter_context(nc.sbuf_tensor(_un() + "n1_w%d" % i, [128, 8192], BF16))) for i in range(3)]
        wctr = [0]
        tr.op("dve", lambda: nc.vector.tensor_scalar(bqs.t[:], vecs.t[:, vc["na_bq"]:vc["na_bq"] + DC], scale, None, ALU.mult), R=vecs.b, W=bqs.b)
        bk = vc["na_bk"]

        def consume(fc, p_):
            if fc < DC:
                tr.op("act", lambda: nc.scalar.activation(qo.t[:, fc, :], p_.t[:], AF.Identity, bias=bqs.t[:, fc:fc + 1], scale=scale),
                      R=p_.b + bqs.b, W=[qo.b[fc // 4]])
            else:
                c = fc - DC
                tr.op("act", lambda: nc.scalar.activation(ko.t[:, c, :], p_.t[:], AF.Identity, bias=vecs.t[:, bk + c:bk + c + 1], scale=1.0),
                      R=p_.b + vecs.b, W=[ko.b[c // 4]])

        for si, (t0, L) in enumerate(g.seqs):
            for ti in range(L // NT):
                c0 = t0 + ti * NT
                for q in range(4):
                    tr.dma("sp", xres.t[:, q * 4:(q + 1) * 4, :], XTv[:, q * 4:(q + 1) * 4, c0:c0 + NT], W=[xres.b[q]])
                    g.copy_op(("act", "dve")[q % 2], xb.t[:, q * 4:(q + 1) * 4, :], xres.t[:, q * 4:(q + 1) * 4, :], [xres.b[q]], [xb.b[q]])
                g.linear("na_qkv", wsl, wctr, lambda kc: xb.t[:, kc, :], lambda kc: [xb.b[kc // 4]], consume, f_lo=0, f_hi=32)
                for fp in range(0 if not os.environ.get("NA_NOV") else 4, 4):
                    slot = wsl[wctr[0] % 3]
                    wctr[0] += 1
                    tr.dma("sp", slot.t[:], dst[8 + fp], W=slot.b)
                    wv = slot.t[:].rearrange("p (k f) -> p k f", k=16)
                    for b in range(4):
                        p_ = g.nps(0, 4)
                        for kc in range(16):
                            tr.op("pe", lambda: nc.tensor.matmul(p_.t[:], lhsT=xb.t[:, kc, b * 128:(b + 1) * 128], rhs=wv[:, kc, :],
                                                                start=(kc == 0), stop=(kc == 15)),
                                  R=slot.b + [xb.b[kc // 4]], W=p_.b, inc=(kc == 15))
                        g.copy_op(("act", "dve")[b % 2], vo.t[:, b, fp * 512:(fp + 1) * 512], p_.t[:], p_.b, [vo.b[b]])
                for q in range(4):
                    tr.dma("pool", QTv[:, q * 4:(q + 1) * 4, c0:c0 + NT], qo.t[:, q * 4:(q + 1) * 4, :], R=[qo.b[q]])
                    tr.dma("pool", KTv[:, q * 4:(q + 1) * 4, c0:c0 + NT], ko.t[:, q * 4:(q + 1) * 4, :], R=[ko.b[q]])
                    tr.dma("pool", VTv[:, c0 // 128 + q, :], vo.t[:, q, :], R=[vo.b[q]])
    tr.barrier()
    if os.environ.get("NA_SKIP2"):
        return
    LM = max(L for _, L in g.seqs)
    with contextlib.ExitStack() as ps_:
        qh = [Tl(ps_.enter_context(nc.sbuf_tensor(_un() + "n2_q%d" % i, [128, LM], BF16))) for i in range(2)]
        kh = [Tl(ps_.enter_context(nc.sbuf_tensor(_un() + "n2_k%d" % i, [128, LM], BF16))) for i in range(2)]
        vh = [Tl(ps_.enter_context(nc.sbuf_tensor(_un() + "n2_v%d" % i, [128, LM // 128, 128], BF16))) for i in range(2)]
        oh = [Tl(ps_.enter_context(nc.sbuf_tensor(_un() + "n2_o%d" % i, [128, LM], BF16))) for i in range(2)]
        tab = Tl(ps_.enter_context(nc.sbuf_tensor(_un() + "n2_tab", [128, 3, 8, NT], F32)), 3)
        sb = [Tl(ps_.enter_context(nc.sbuf_tensor(_un() + "n2_sb%d" % i, [128, NT], F32))) for i in range(3)]
        pb = [Tl(ps_.enter_context(nc.sbuf_tensor(_un() + "n2_pb%d" % i, [128, NT], BF16))) for i in range(3)]
        ri = [Tl(ps_.enter_context(nc.sbuf_tensor(_un() + "n2_ri%d" % i, [128, NT], F32))) for i in range(2)]
        n = 0
        k = 0
        mm = 0
        for si, (t0, L) in enumerate(g.seqs):
            R = L // 64
            nm = L // NT
            for h in range(16):
                q_, k_, v_, o_ = qh[n % 2], kh[n % 2], vh[n % 2], oh[n % 2]
                n += 1
                tr.dma("sp", q_.t[:, 0:L], QTv[:, h, t0:t0 + L], W=q_.b)
                tr.dma("sp", k_.t[:, 0:L], KTv[:, h, t0:t0 + L], W=k_.b)
                tr.dma("sp", v_.t[:, 0:L // 128, :], VTv[:, t0 // 128:(t0 + L) // 128, h * 128:(h + 1) * 128], W=v_.b)
                if si == 0 or True:
                    for ty in range(3):
                        tr.dma("sp", tab.t[:, ty], g.natab_d[ty, h].rearrange("j p q -> p j q"), W=[tab.b[ty]])
                for m in range(nm):
                    ty = 0 if m == 0 else (2 if m == nm - 1 else 1)
                    base = int(np.clip(8 * m - 4, 0, R - 16))
                    po, pr = ps[4 + mm % 2], ps[6 + mm % 2]
                    mm += 1
                    jl = jlist[ty]
                    stg = {}

                    def s_stage(jj):
                        nonlocal k
                        tok = (base + 2 * jj) * 64
                        p_s = g.nps(0, 4)
                        s_, p_ = sb[k % 3], pb[k % 3]
                        k += 1
                        tr.op("pe", lambda: nc.tensor.matmul(p_s.t[:], lhsT=k_.t[:, tok:tok + 128], rhs=q_.t[:, m * NT:(m + 1) * NT], start=True, stop=True),
                              R=k_.b + q_.b, W=p_s.b)
                        tr.op("dve", lambda: nc.vector.tensor_tensor(s_.t[:], p_s.t[:], tab.t[:, ty, jj, :], ALU.add), R=p_s.b + [tab.b[ty]], W=s_.b)
                        tr.op("act", lambda: nc.scalar.activation(p_.t[:], s_.t[:], AF.Exp), R=s_.b, W=p_.b)
                        stg[jj] = (tok, p_)

                    def pv_stage(jj):
                        tok, p_ = stg.pop(jj)
                        tr.op("pe", lambda: nc.tensor.matmul(po.t[:], lhsT=v_.t[:, tok // 128, :], rhs=p_.t[:], start=(jj == jl[0]), stop=(jj == jl[-1])),
                              R=v_.b + p_.b, W=po.b)
                        tr.op("pe", lambda: nc.tensor.matmul(pr.t[:], lhsT=g.ones_b.t[:], rhs=p_.t[:], start=(jj == jl[0]), stop=(jj == jl[-1])),
                              R=g.ones_b.b + p_.b, W=pr.b)

                    LA_ = 2
                    for i_ in range(min(LA_, len(jl))):
                        s_stage(jl[i_])
                    for i_ in range(len(jl)):
                        if i_ + LA_ < len(jl):
                            s_stage(jl[i_ + LA_])
                        pv_stage(jl[i_])
                    r_ = ri[mm % 2]
                    tr.op("dve", lambda: nc.vector.reciprocal(r_.t[:], pr.t[:]), R=pr.b, W=r_.b)
                    tr.op("dve", lambda: nc.vector.tensor_tensor(r_.t[:], po.t[:], r_.t[:], ALU.mult), R=po.b + r_.b, W=r_.b)
                    tr.op("act", lambda: nc.scalar.activation(o_.t[:, m * NT:(m + 1) * NT], r_.t[:], AF.Identity,
                                                              bias=vecs.t[:, vc["na_bvp"] + h:vc["na_bvp"] + h + 1], scale=1.0),
                          R=r_.b + vecs.b, W=o_.b)
                tr.dma("pool", Yv[:, h, t0:t0 + L], o_.t[:, 0:L], R=o_.b)
    tr.barrier()


LAYERS = [("rg", 0), ("ssd", 0), ("na", 0), ("rg", 1)]
WNAMES = ["rg_w_in", "rg_w_a", "rg_w_x", "rg_w_out", "ssd_w_in", "ssd_w_out", "na_w_qkv", "na_w_out", "mlp_w_up", "mlp_w_down"]


def pack_consts(inp, layers):
    cols = []
    vcols = []
    off = [0]

    def add(v):
        a = _vec_layout(v)
        cols.append(a)
        o = off[0]
        off[0] += a.shape[1]
        return o

    rws = [np.zeros(16, np.float32)]
    roff = [16]

    def radd(v):
        v = np.asarray(v, np.float32).reshape(-1)
        rws.append(v)
        o = roff[0]
        roff[0] += v.size
        return o

    for li, (kind, j) in enumerate(layers):
        vc = {}
        vc["ln1g"] = add(inp["ln1_g"][li])
        vc["ln1b"] = add(inp["ln1_b"][li])
        vc["ln2g"] = add(inp["ln2_g"][li])
        vc["ln2b"] = add(inp["ln2_b"][li])
        if kind == "rg":
            vc["rg_cw"] = add(inp["rg_conv_w"][j][0])
            for kk in range(1, 4):
                add(inp["rg_conv_w"][j][kk])
            vc["rg_cb"] = add(inp["rg_conv_b"][j])
            vc["rg_ba"] = add(inp["rg_b_a"][j][0])
            add(inp["rg_b_a"][j][1])
            vc["rg_bx"] = add(inp["rg_b_x"][j][0])
            add(inp["rg_b_x"][j][1])
            vc["rg_lam"] = add(inp["rg_lambda"][j][0])
            add(inp["rg_lambda"][j][1])
        if kind == "ssd":
            cwv = np.asarray(inp["ssd_conv_w"][j], np.float32)
            vc["ssd_cw"] = add(cwv[0])
            for kk in range(1, 4):
                add(cwv[kk])
            vc["ssd_cb"] = add(inp["ssd_conv_b"][j])
            vc["ssd_dtb"] = add(np.asarray(inp["ssd_dt_bias"][j], np.float32).reshape(-1))
            vc["ssd_alog"] = add(np.asarray(inp["ssd_a_log"][j], np.float32).reshape(-1))
            vc["ssd_D"] = add(np.repeat(np.asarray(inp["ssd_d"][j], np.float32), 64))
            vc["ssd_ng"] = add(inp["ssd_norm_g"][j])
        if kind == "na":
            bq = np.asarray(inp["na_b_qkv"][j], np.float32)
            vc["na_bq"] = add(bq[0:D])
            vc["na_bk"] = add(bq[D:2 * D])
            vc["na_bvp"] = add(bq[2 * D:3 * D])
        vcols.append(vc)
    vecs = np.ascontiguousarray(np.concatenate(cols, axis=1))
    rows = np.ascontiguousarray(np.concatenate(rws)[None, :])
    return vecs, vcols, rows


def run(inp, layers, LA, LB, n_cores, xa_list, xb_list):
    global g_vcols
    vecs, vcols, rows = pack_consts(inp, layers)
    g_vcols = vcols
    wshapes = {k: tuple(np.asarray(inp[k]).shape) for k in WNAMES}
    has_na = any(k == "na" for k, _ in layers)
    natab = na_table(inp["na_rpb"][0]) if has_na else None
    import time as _t
    _t0 = _t.time()
    nc = build(LA, LB, layers, wshapes, vecs.shape[1], rows.shape[1], natab.shape if has_na else None)
    print("build_s", _t.time() - _t0, flush=True)
    ident = np.eye(128, dtype=np.float32)
    kk = np.arange(128)
    tri = np.stack([(kk[:, None] > kk[None, :]), (kk[:, None] <= kk[None, :]), (kk[:, None] < kk[None, :]), (kk[:, None] >= kk[None, :])]).astype(np.float32)
    base = {k: np.ascontiguousarray(np.asarray(inp[k], np.float32)) for k in WNAMES}
    base.update(vecs=vecs, rows=rows, ident=ident, tri=tri)
    if has_na:
        base["natab"] = natab
    in_maps = []
    for c in range(n_cores):
        m = dict(base)
        m["xa"] = np.ascontiguousarray(xa_list[c], dtype=np.float32)
        m["xb"] = np.ascontiguousarray(xb_list[c], dtype=np.float32)
        in_maps.append(m)
    res = run_bass_kernel_spmd(nc, in_maps, core_ids=list(range(n_cores)))
    return [(r["ya"], r["yb"]) for r in res.results]


def kernel(**inputs):
    xp = np.asarray(inputs["x_prompt"], np.float32)
    xs = np.asarray(inputs["x_sample"], np.float32)
    xa_list = [xp[c] for c in range(8)]
    xb_list = [xs[c % 4] for c in range(8)]
    outs = run(inputs, LAYERS, 2048, 8192, 8, xa_list, xb_list)
    yp = np.stack([outs[c][0] for c in range(8)], axis=0)
    ys = np.stack([outs[c][1] for c in range(4)], axis=0)
    return (yp, ys)
```

```python
import contextlib
import math
import os
import numpy as np
import concourse.bass as bass
import concourse.mybir as mybir
from concourse.bass_utils import run_bass_kernel_spmd

F32 = mybir.dt.float32
BF16 = mybir.dt.bfloat16
AF = mybir.ActivationFunctionType
ALU = mybir.AluOpType

D = 2048
DC = 16
DEPTH = 4
ALPHA = (2 * DEPTH) ** 0.25
LN_EPS = 1e-5
NT = 512
MLP_H = 8192
NEG = -30000.0
GELU_K = 1.5957691216057308
NS = 8


_uc = [0]


def _un():
    _uc[0] += 1
    return "t%d_" % _uc[0]


class Buf:
    __slots__ = ("w", "r")

    def __init__(self):
        self.w = None
        self.r = {}


class Tl:
    def __init__(self, t, n=1):
        self.t = t
        self.b = [Buf() for _ in range(n)]


class TR:
    def __init__(self, nc, es):
        self.nc = nc
        self.eng = {"pe": nc.tensor, "act": nc.scalar, "dve": nc.vector, "pool": nc.gpsimd, "sp": nc.sync}
        self.sem = {k: es.enter_context(nc.semaphore("p_" + k)) for k in ("pe", "act", "dve", "pool")}
        self.cnt = {k: 0 for k in self.sem}
        self.dsem = {q: [es.enter_context(nc.semaphore("d_%s%d" % (q, i))) for i in range(NS)] for q in ("sp", "pool")}
        self.dval = {q: [0] * NS for q in self.dsem}
        self.dn = {q: 0 for q in self.dsem}
        self.waited = {}

    def _wait(self, e, ev):
        key, sem, val = ev
        k = (e, key)
        if self.waited.get(k, 0) >= val:
            return
        self.eng[e].wait_ge(sem, val)
        self.waited[k] = val

    def _deps(self, e, R, W, same):
        for b in R:
            if b.w is not None and (same or b.w[0] != e):
                self._wait(e, b.w)
        for b in W:
            if b.w is not None and (same or b.w[0] != e):
                self._wait(e, b.w)
            for ev in b.r.values():
                if same or ev[0] != e:
                    self._wait(e, ev)

    def op(self, e, fn, R=(), W=(), inc=True):
        self._deps(e, R, W, e != "pe")
        ins = fn()
        if inc:
            self.cnt[e] += 1
            ins.then_inc(self.sem[e], 1)
            ev = (e, self.sem[e], self.cnt[e])
        else:
            ev = (e, self.sem[e], self.cnt[e] + 1)
        for b in R:
            b.r[e] = ev
        for b in W:
            b.w = ev
            b.r = {}
        return ins

    def dma(self, q, out, in_, R=(), W=()):
        i = self.dn[q] % NS
        self.dn[q] += 1
        sem = self.dsem[q][i]
        key = "%s%d" % (q, i)
        if self.dval[q][i] > 0:
            self._wait(q, (key, sem, self.dval[q][i]))
        self._deps(q, R, W, True)
        ins = self.eng[q].dma_start(out=out, in_=in_)
        ins.then_inc(sem, 16)
        self.dval[q][i] += 16
        ev = (key, sem, self.dval[q][i])
        for b in R:
            b.r[key] = ev
        for b in W:
            b.w = ev
            b.r = {}

    def barrier(self):
        evs = [(k, self.sem[k], self.cnt[k]) for k in self.sem if self.cnt[k] > 0]
        for q in self.dsem:
            for i in range(NS):
                if self.dval[q][i] > 0:
                    evs.append(("%s%d" % (q, i), self.dsem[q][i], self.dval[q][i]))
        for e in self.eng:
            for ev in evs:
                self._wait(e, ev)


class Ctx:
    pass


def _vec_layout(v):
    v = np.asarray(v, np.float32).reshape(-1)
    return np.ascontiguousarray(v.reshape(-1, 128).T)


def build(LA, LB, layers, wshapes, nvec, nrow, natab_shape):
    T = LA + LB
    seqs = [(0, LA), (LA, LB)]
    nc = bass.Bass("TRN2", target_bir_lowering=False)
    g = Ctx()
    g.nc = nc

    def din(name, shape, dt=F32):
        return nc.dram_tensor(name, list(shape), dt, kind="ExternalInput").ap()

    def dscr(name, shape, dt=F32):
        return nc.dram_tensor(name, list(shape), dt, kind="Internal").ap()

    xa = din("xa", [LA, D])
    xbm = din("xb", [LB, D])
    ya = nc.dram_tensor("ya", [LA, D], F32, kind="ExternalOutput").ap()
    yb = nc.dram_tensor("yb", [LB, D], F32, kind="ExternalOutput").ap()
    W = {k: din(k, s) for k, s in wshapes.items()}
    vecs_d = din("vecs", [128, nvec])
    rows_d = din("rows", [1, nrow])
    ident_d = din("ident", [128, 128])
    tri_d = din("tri", [4, 128, 128])
    if natab_shape is not None:
        natab_d = din("natab", natab_shape)

    XT = dscr("XT", [D, T])
    XTv = XT.rearrange("(c p) t -> p c t", p=128)

    with contextlib.ExitStack() as es:
        tr = TR(nc, es)
        vecs = Tl(es.enter_context(nc.sbuf_tensor(_un() + "vecs", [128, nvec], F32)))
        nvecs = Tl(es.enter_context(nc.sbuf_tensor(_un() + "nvecs", [128, nvec], F32)))
        ident = Tl(es.enter_context(nc.sbuf_tensor(_un() + "ident", [128, 128], F32)))
        ones_f = Tl(es.enter_context(nc.sbuf_tensor(_un() + "ones_f", [128, 128], F32)))
        ones_b = Tl(es.enter_context(nc.sbuf_tensor(_un() + "ones_b", [128, 128], BF16)))
        ps = [Tl(es.enter_context(nc.psum_tensor(_un() + "ps%d" % i, [128, 512], F32))) for i in range(8)]
        tr.dma("sp", vecs.t[:], vecs_d, W=vecs.b)
        tr.dma("sp", ident.t[:], ident_d, W=ident.b)
        tri = Tl(es.enter_context(nc.sbuf_tensor(_un() + "tri", [128, 4, 128], F32)))
        tr.dma("sp", tri.t[:], tri_d.rearrange("n p q -> p n q"), W=tri.b)
        g.tri = tri
        tr.op("act", lambda: nc.scalar.activation(nvecs.t[:], vecs.t[:], AF.Copy, scale=-1.0), R=vecs.b, W=nvecs.b)
        tr.op("dve", lambda: nc.vector.memset(ones_f.t[:], 1.0), W=ones_f.b)
        tr.op("dve", lambda: nc.vector.memset(ones_b.t[:], 1.0), W=ones_b.b)
        g.tr, g.vecs, g.nvecs, g.ident, g.ones_f, g.ones_b, g.ps = tr, vecs, nvecs, ident, ones_f, ones_b, ps
        psi = [0]

        def nps(lo=0, hi=8):
            i = lo + psi[0] % (hi - lo)
            psi[0] += 1
            return ps[i]

        rr = [0]

        def rot(engs):
            rr[0] += 1
            return engs[rr[0] % len(engs)]

        def copy_op(e, out, in_, R, Wb):
            if e == "act":
                tr.op("act", lambda: nc.scalar.copy(out, in_), R=R, W=Wb)
            elif e == "dve":
                tr.op("dve", lambda: nc.vector.tensor_copy(out, in_), R=R, W=Wb)
            else:
                tr.op("pool", lambda: nc.gpsimd.tensor_copy(out, in_), R=R, W=Wb)

        Wb = {}

        def convert_all(specs):
            with contextlib.ExitStack() as ps_:
                stage = [Tl(ps_.enter_context(nc.sbuf_tensor(_un() + "cst%d" % i, [128, 8192], F32))) for i in range(3)]
                cvt = [Tl(ps_.enter_context(nc.sbuf_tensor(_un() + "ccv%d" % i, [128, 8192], BF16))) for i in range(3)]
                n = 0
                for name, src, KC, NF in specs:
                    PW = min(8192 // KC, NF)
                    npan = NF // PW
                    dst = dscr("wb_" + name, [npan, 128, KC * PW], BF16)
                    Wb[name] = (dst, KC, NF, PW)
                    sv = src.rearrange("(k p) f -> p k f", p=128)
                    for pi in range(npan):
                        s = n % 3
                        n += 1
                        tr.dma("sp", stage[s].t[:, 0:KC * PW].rearrange("p (k f) -> p k f", k=KC), sv[:, :, pi * PW:(pi + 1) * PW],
                               W=stage[s].b)
                        copy_op(("act", "dve", "pool", "dve", "act")[n % 5], cvt[s].t[:, 0:KC * PW], stage[s].t[:, 0:KC * PW], stage[s].b, cvt[s].b)
                        tr.dma("pool", dst[pi], cvt[s].t[:, 0:KC * PW], R=cvt[s].b)
            tr.barrier()

        specs = []
        for li, (kind, j) in enumerate(layers):
            if kind == "rg" and ("rg_in%d" % j) not in [s[0] for s in specs]:
                specs.append(("rg_in%d" % j, W["rg_w_in"][j], 16, 4096))
                specs.append(("rg_out%d" % j, W["rg_w_out"][j], 16, 2048))
            if kind == "ssd" and "ssd_in" not in [s[0] for s in specs]:
                specs.append(("ssd_in", W["ssd_w_in"][j][:, 0:10240], 16, 10240))
                specs.append(("ssd_dt", W["ssd_w_in"][j][:, 10240:10368], 16, 128))
                specs.append(("ssd_out", W["ssd_w_out"][j], 32, 2048))
            if kind == "na" and "na_qkv" not in [s[0] for s in specs]:
                specs.append(("na_qkv", W["na_w_qkv"][j], 16, 6144))
                specs.append(("na_out", W["na_w_out"][j], 16, 2048))
            specs.append(("up%d" % li, W["mlp_w_up"][li], 16, 8192))
            specs.append(("dn%d" % li, W["mlp_w_down"][li], 64, 2048))
        convert_all(specs)

        def linear(wname, wslots, wctr, rhs, rhsR, consume, f_lo=0, f_hi=None, pslo=0, pshi=4):
            dst, KC, NF, PW = Wb[wname]
            cpp = PW // 128
            if f_hi is None:
                f_hi = NF // 128
            for pi in range(f_lo // cpp, (f_hi + cpp - 1) // cpp):
                slot = wslots[wctr[0] % len(wslots)]
                wctr[0] += 1
                tr.dma("sp", slot.t[:, 0:KC * PW], dst[pi], W=slot.b)
                wv = slot.t[:, 0:KC * PW].rearrange("p (k f) -> p k f", k=KC)
                for cc in range(cpp):
                    fc = pi * cpp + cc
                    if fc < f_lo or fc >= f_hi:
                        continue
                    p_ = nps(pslo, pshi)
                    for kc in range(KC):
                        tr.op("pe", lambda: nc.tensor.matmul(p_.t[:], lhsT=wv[:, kc, cc * 128:(cc + 1) * 128], rhs=rhs(kc),
                                                            start=(kc == 0), stop=(kc == KC - 1)),
                              R=slot.b + rhsR(kc), W=p_.b, inc=(kc == KC - 1))
                    consume(fc, p_)

        def transpose_in():
            with contextlib.ExitStack() as ps_:
                xt = [Tl(ps_.enter_context(nc.sbuf_tensor(_un() + "p0x%d" % i, [128, 4, D], F32))) for i in range(2)]
                xo = [Tl(ps_.enter_context(nc.sbuf_tensor(_un() + "p0o%d" % i, [128, DC, NT], F32)), 4) for i in range(2)]
                n = 0
                for (src, (t0, L)) in ((xa, seqs[0]), (xbm, seqs[1])):
                    sv = src.rearrange("(n b p) d -> n p b d", p=128, b=4)
                    for ti in range(L // NT):
                        a, o = xt[n % 2], xo[n % 2]
                        n += 1
                        tr.dma("sp", a.t[:], sv[ti], W=a.b)
                        for c in range(DC):
                            p_ = nps()
                            for b in range(4):
                                tr.op("pe", lambda: nc.tensor.transpose(p_.t[:, b * 128:(b + 1) * 128], a.t[:, b, c * 128:(c + 1) * 128],
                                                                        ident.t[:]),
                                      R=a.b + ident.b, W=p_.b, inc=(b == 3))
                            copy_op(("act", "dve")[c % 2], o.t[:, c, :], p_.t[:], p_.b, [o.b[c // 4]])
                            if c % 4 == 3:
                                q = c // 4
                                tr.dma("pool", XTv[:, q * 4:(q + 1) * 4, t0 + ti * NT:t0 + (ti + 1) * NT], o.t[:, q * 4:(q + 1) * 4, :],
                                       R=[o.b[q]])
            tr.barrier()

        def transpose_out():
            with contextlib.ExitStack() as ps_:
                xi = [Tl(ps_.enter_context(nc.sbuf_tensor(_un() + "pzx%d" % i, [128, DC, NT], F32))) for i in range(2)]
                xo = [Tl(ps_.enter_context(nc.sbuf_tensor(_un() + "pzo%d" % i, [128, 4, D], F32)), 4) for i in range(2)]
                n = 0
                for (dst, (t0, L)) in ((ya, seqs[0]), (yb, seqs[1])):
                    dv = dst.rearrange("(n b p) d -> n p b d", p=128, b=4)
                    for ti in range(L // NT):
                        a, o = xi[n % 2], xo[n % 2]
                        n += 1
                        tr.dma("sp", a.t[:], XTv[:, :, t0 + ti * NT:t0 + (ti + 1) * NT], W=a.b)
                        for b in range(4):
                            for c4 in range(4):
                                p_ = nps()
                                for cc in range(4):
                                    c = c4 * 4 + cc
                                    tr.op("pe", lambda: nc.tensor.transpose(p_.t[:, cc * 128:(cc + 1) * 128], a.t[:, c, b * 128:(b + 1) * 128],
                                                                            ident.t[:]),
                                          R=a.b + ident.b, W=p_.b, inc=(cc == 3))
                                copy_op(("act", "dve")[c4 % 2], o.t[:, b, c4 * 512:(c4 + 1) * 512], p_.t[:], p_.b, [o.b[b]])
                            tr.dma("pool", dv[ti][:, b, :], o.t[:, b, :], R=[o.b[b]])
            tr.barrier()

        def post_pass(li, outname, KCY, Ysc, vc):
            Yv = Ysc.rearrange("(c p) t -> p c t", p=128)
            with contextlib.ExitStack() as ps_:
                hb = Tl(ps_.enter_context(nc.sbuf_tensor(_un() + "hb", [128, 64, NT], BF16)), 64)
                xres = Tl(ps_.enter_context(nc.sbuf_tensor(_un() + "xres", [128, DC, NT], F32)), DC)
                xb = Tl(ps_.enter_context(nc.sbuf_tensor(_un() + "xbf", [128, DC, NT], BF16)), DC)
                wsl = [Tl(ps_.enter_context(nc.sbuf_tensor(_un() + "wsl%d" % i, [128, 8192], BF16))) for i in range(3)]
                tmp = [Tl(ps_.enter_context(nc.sbuf_tensor(_un() + "ptmp%d" % i, [128, NT], F32))) for i in range(3)]
                tmpb = [Tl(ps_.enter_context(nc.sbuf_tensor(_un() + "ptmpb%d" % i, [128, NT], BF16))) for i in range(4)]
                st = [Tl(ps_.enter_context(nc.sbuf_tensor(_un() + "pst%d" % i, [128, NT], F32))) for i in range(3)]
                wctr = [0]
                tctr = [0]

                def layer_norm(gcol, bcol, want_bf):
                    s1, s2 = ps[4], ps[5]
                    for c in range(DC):
                        t_ = tmpb[tctr[0] % 4]
                        tctr[0] += 1
                        tr.op("act", lambda: nc.scalar.activation(t_.t[:], xres.t[:, c, :], AF.Square), R=[xres.b[c]], W=t_.b)
                        u_ = tmpb[tctr[0] % 4]
                        tctr[0] += 1
                        tr.op("act", lambda: nc.scalar.copy(u_.t[:], xres.t[:, c, :]), R=[xres.b[c]], W=u_.b)
                        tr.op("pe", lambda: nc.tensor.matmul(s1.t[:], lhsT=ones_b.t[:], rhs=u_.t[:], start=(c == 0), stop=(c == DC - 1)),
                              R=ones_b.b + u_.b, W=s1.b)
                        tr.op("pe", lambda: nc.tensor.matmul(s2.t[:], lhsT=ones_b.t[:], rhs=t_.t[:], start=(c == 0), stop=(c == DC - 1)),
                              R=ones_b.b + t_.b, W=s2.b)
                    m, v, r = st
                    tr.op("act", lambda: nc.scalar.activation(m.t[:], s1.t[:], AF.Copy, scale=1.0 / D), R=s1.b, W=m.b)
                    tr.op("dve", lambda: nc.vector.tensor_tensor(v.t[:], m.t[:], m.t[:], ALU.mult), R=m.b, W=v.b)
                    tr.op("dve", lambda: nc.vector.scalar_tensor_tensor(v.t[:], s2.t[:], 1.0 / D, v.t[:], ALU.mult, ALU.subtract),
                          R=s2.b + v.b, W=v.b)
                    tr.op("dve", lambda: nc.vector.tensor_scalar(v.t[:], v.t[:], LN_EPS, None, ALU.add), R=v.b, W=v.b)
                    tr.op("act", lambda: nc.scalar.activation(r.t[:], v.t[:], AF.Ln), R=v.b, W=r.b)
                    tr.op("act", lambda: nc.scalar.activation(r.t[:], r.t[:], AF.Exp, scale=-0.5), R=r.b, W=r.b)
                    for c in range(DC):
                        xc = xres.t[:, c, :]
                        tr.op("dve", lambda: nc.vector.tensor_tensor(xc, xc, m.t[:], ALU.subtract), R=[xres.b[c]] + m.b, W=[xres.b[c]])
                        tr.op("dve", lambda: nc.vector.tensor_tensor(xc, xc, r.t[:], ALU.mult), R=[xres.b[c]] + r.b, W=[xres.b[c]])
                        if want_bf:
                            tr.op("act", lambda: nc.scalar.activation(xb.t[:, c, :], xc, AF.Identity, bias=vecs.t[:, bcol + c:bcol + c + 1],
                                                                      scale=vecs.t[:, gcol + c:gcol + c + 1]),
                                  R=[xres.b[c]] + vecs.b, W=[xb.b[c]])
                        tr.op("act", lambda: nc.scalar.activation(xc, xc, AF.Identity, bias=vecs.t[:, bcol + c:bcol + c + 1],
                                                                  scale=vecs.t[:, gcol + c:gcol + c + 1]),
                              R=[xres.b[c]] + vecs.b, W=[xres.b[c]])

                def resid(fc, p_):
                    xc = xres.t[:, fc, :]
                    tr.op("dve", lambda: nc.vector.scalar_tensor_tensor(xc, xc, ALPHA, p_.t[:], ALU.mult, ALU.add),
                          R=[xres.b[fc]] + p_.b, W=[xres.b[fc]])

                def up_consume(fc, p_):
                    t_ = tmp[tctr[0] % 3]
                    tctr[0] += 1
                    tr.op("act", lambda: nc.scalar.activation(t_.t[:], p_.t[:], AF.Relu), R=p_.b, W=t_.b)
                    e = ("dve", "pool")[fc % 2]
                    tr.op(e, lambda: tr.eng[e].tensor_tensor(hb.t[:, fc, :], t_.t[:], t_.t[:], ALU.mult), R=t_.b, W=[hb.b[fc]])

                for (t0, L) in seqs:
                    for ti in range(L // NT):
                        c0 = t0 + ti * NT
                        for q in range(KCY // 4):
                            tr.dma("sp", hb.t[:, q * 4:(q + 1) * 4, :], Yv[:, q * 4:(q + 1) * 4, c0:c0 + NT], W=hb.b[q * 4:(q + 1) * 4])
                        for q in range(4):
                            tr.dma("sp", xres.t[:, q * 4:(q + 1) * 4, :], XTv[:, q * 4:(q + 1) * 4, c0:c0 + NT], W=xres.b[q * 4:(q + 1) * 4])
                        linear(outname, wsl, wctr, lambda kc: hb.t[:, kc, :], lambda kc: [hb.b[kc]], resid)
                        layer_norm(vc["ln1g"], vc["ln1b"], True)
                        linear("up%d" % li, wsl, wctr, lambda kc: xb.t[:, kc, :], lambda kc: [xb.b[kc]], up_consume)
                        linear("dn%d" % li, wsl, wctr, lambda kc: hb.t[:, kc, :], lambda kc: [hb.b[kc]], resid)
                        layer_norm(vc["ln2g"], vc["ln2b"], False)
                        for q in range(4):
                            tr.dma("pool", XTv[:, q * 4:(q + 1) * 4, c0:c0 + NT], xres.t[:, q * 4:(q + 1) * 4, :], R=xres.b[q * 4:(q + 1) * 4])
            tr.barrier()

        g.linear, g.nps, g.copy_op, g.Wb, g.dscr, g.XTv, g.seqs, g.T, g.W = linear, nps, copy_op, Wb, dscr, XTv, seqs, T, W
        g.rows_d, g.tri_d = rows_d, tri_d
        if natab_shape is not None:
            g.natab_d = natab_d

        transpose_in()
        ysc = {}
        for li, (kind, j) in enumerate(layers):
            vc = g_vcols[li]
            if kind == "rg":
                if "rg" not in ysc:
                    ysc["rg"] = dscr("Y_rg", [2048, T], BF16)
                rg_mixer(g, j, vc, ysc["rg"])
                post_pass(li, "rg_out%d" % j, 16, ysc["rg"], vc)
            elif kind == "ssd":
                ysc["ssd"] = dscr("Y_ssd", [4096, T], BF16)
                ssd_mixer(g, j, vc, ysc["ssd"])
                post_pass(li, "ssd_out", 32, ysc["ssd"], vc)
            elif kind == "na":
                ysc["na"] = dscr("Y_na", [2048, T], BF16)
                na_mixer(g, j, vc, ysc["na"])
                post_pass(li, "na_out", 16, ysc["na"], vc)
        transpose_out()
    return nc


g_vcols = []


def rg_mixer(g, j, vc, Ysc):
    nc, tr, vecs, nvecs, ps = g.nc, g.tr, g.vecs, g.nvecs, g.ps
    T = g.T
    XTv = g.XTv
    Gs = g.dscr("rg_G%d" % j, [D, T])
    Us = g.dscr("rg_U%d" % j, [D, T + 8])
    UCs = g.dscr("rg_UC%d" % j, [D, T])
    HFs = g.dscr("rg_HF%d" % j, [D, T])
    Gv = Gs.rearrange("(c p) t -> p c t", p=128)
    Uv = Us.rearrange("(c p) t -> p c t", p=128)
    UCv = UCs.rearrange("(c p) t -> p c t", p=128)
    HFv = HFs.rearrange("(c p) t -> p c t", p=128)
    Yv = Ysc.rearrange("(c p) t -> p c t", p=128)
    upad = [g.seqs[0][0], g.seqs[1][0]]

    with contextlib.ExitStack() as ps_:
        xres = Tl(ps_.enter_context(nc.sbuf_tensor(_un() + "ra_x", [128, DC, NT], F32)), 4)
        xb = Tl(ps_.enter_context(nc.sbuf_tensor(_un() + "ra_xb", [128, DC, NT], BF16)), 4)
        gb = Tl(ps_.enter_context(nc.sbuf_tensor(_un() + "ra_g", [128, DC, NT], F32)), 4)
        ub = Tl(ps_.enter_context(nc.sbuf_tensor(_un() + "ra_u", [128, DC, NT], F32)), 4)
        wsl = [Tl(ps_.enter_context(nc.sbuf_tensor(_un() + "ra_w%d" % i, [128, 8192], BF16))) for i in range(3)]
        t1 = [Tl(ps_.enter_context(nc.sbuf_tensor(_un() + "ra_t1%d" % i, [128, NT], F32))) for i in range(2)]
        t2 = [Tl(ps_.enter_context(nc.sbuf_tensor(_un() + "ra_t2%d" % i, [128, NT], F32))) for i in range(2)]
        zt = Tl(ps_.enter_context(nc.sbuf_tensor(_un() + "ra_z", [128, DC, 4], F32)))
        wctr = [0]
        k = [0]

        def consume(fc, p_):
            if fc < DC:
                a, b = t1[k[0] % 2], t2[k[0] % 2]
                k[0] += 1
                tr.op("act", lambda: nc.scalar.copy(a.t[:], p_.t[:]), R=p_.b, W=a.b)
                tr.op("act", lambda: nc.scalar.activation(b.t[:], p_.t[:], AF.Square), R=p_.b, W=b.b)
                tr.op("dve", lambda: nc.vector.tensor_scalar(b.t[:], b.t[:], 0.044715, 1.0, ALU.mult, ALU.add), R=b.b, W=b.b)
                tr.op("dve", lambda: nc.vector.tensor_tensor(b.t[:], b.t[:], a.t[:], ALU.mult), R=b.b + a.b, W=b.b)
                tr.op("act", lambda: nc.scalar.activation(b.t[:], b.t[:], AF.Exp, scale=-GELU_K), R=b.b, W=b.b)
                tr.op("dve", lambda: nc.vector.tensor_scalar(b.t[:], b.t[:], 1.0, None, ALU.add), R=b.b, W=b.b)
                tr.op("dve", lambda: nc.vector.reciprocal(b.t[:], b.t[:]), R=b.b, W=b.b)
                tr.op("dve", lambda: nc.vector.tensor_tensor(gb.t[:, fc, :], b.t[:], a.t[:], ALU.mult), R=b.b + a.b, W=[gb.b[fc // 4]])
            else:
                c = fc - DC
                g.copy_op(("act", "dve")[c % 2], ub.t[:, c, :], p_.t[:], p_.b, [ub.b[c // 4]])

        for si, (t0, L) in enumerate(g.seqs):
            for ti in range(L // NT):
                c0 = t0 + ti * NT
                for q in range(4):
                    tr.dma("sp", xres.t[:, q * 4:(q + 1) * 4, :], XTv[:, q * 4:(q + 1) * 4, c0:c0 + NT], W=[xres.b[q]])
                    g.copy_op(("act", "dve")[q % 2], xb.t[:, q * 4:(q + 1) * 4, :], xres.t[:, q * 4:(q + 1) * 4, :], [xres.b[q]], [xb.b[q]])
                g.linear("rg_in%d" % j, wsl, wctr, lambda kc: xb.t[:, kc, :], lambda kc: [xb.b[kc // 4]], consume)
                uc0 = upad[si] + ti * NT
                for q in range(4):
                    tr.dma("pool", Gv[:, q * 4:(q + 1) * 4, c0:c0 + NT], gb.t[:, q * 4:(q + 1) * 4, :], R=[gb.b[q]])
                    tr.dma("pool", Uv[:, q * 4:(q + 1) * 4, uc0:uc0 + NT], ub.t[:, q * 4:(q + 1) * 4, :], R=[ub.b[q]])
    tr.barrier()

    cw, cb = vc["rg_cw"], vc["rg_cb"]
    for d in range(2):
        with contextlib.ExitStack() as ps_:
            gw = Tl(ps_.enter_context(nc.sbuf_tensor(_un() + "rb_gw", [128, 2, 8, 2, 256], BF16)))
            uc = Tl(ps_.enter_context(nc.sbuf_tensor(_un() + "rb_uc", [128, DC, NT], F32)), 4)
            ucb = Tl(ps_.enter_context(nc.sbuf_tensor(_un() + "rb_ucb", [128, DC, NT], BF16)), 4)
            hh = Tl(ps_.enter_context(nc.sbuf_tensor(_un() + "rb_h", [128, DC, NT], F32)), 4)
            ab = Tl(ps_.enter_context(nc.sbuf_tensor(_un() + "rb_a", [128, 4, NT], F32)), 4)
            carry = Tl(ps_.enter_context(nc.sbuf_tensor(_un() + "rb_carry", [128, DC], F32)), DC)
            cl = Tl(ps_.enter_context(nc.sbuf_tensor(_un() + "rb_cl", [128, DC], F32)))
            clt = [Tl(ps_.enter_context(nc.sbuf_tensor(_un() + "rb_clt%d" % i, [128, DC], F32))) for i in range(3)]
            ta = [Tl(ps_.enter_context(nc.sbuf_tensor(_un() + "rb_ta%d" % i, [128, NT], F32))) for i in range(2)]
            tb = [Tl(ps_.enter_context(nc.sbuf_tensor(_un() + "rb_tb%d" % i, [128, NT], F32))) for i in range(2)]
            tcx = [Tl(ps_.enter_context(nc.sbuf_tensor(_un() + "rb_tc%d" % i, [128, NT], F32))) for i in range(2)]
            if d == 0:
                halo = Tl(ps_.enter_context(nc.sbuf_tensor(_un() + "rb_halo", [128, DC, NT + 4], F32)), 4)
            else:
                hf = Tl(ps_.enter_context(nc.sbuf_tensor(_un() + "rb_hf", [128, DC, NT], F32)), 4)
                gt = Tl(ps_.enter_context(nc.sbuf_tensor(_un() + "rb_gt", [128, DC, NT], F32)), 4)
                yo = Tl(ps_.enter_context(nc.sbuf_tensor(_un() + "rb_yo", [128, DC, NT], BF16)), 4)
            for gi, nm in enumerate(("rg_w_a", "rg_w_x")):
                src = g.W[nm][j][d].rearrange("n (kc p) o -> p n kc o", p=128)
                tr.dma("pool", gw.t[:, gi], src, W=gw.b)
            lam = vecs.t[:, vc["rg_lam"] + d * DC: vc["rg_lam"] + (d + 1) * DC]
            e_, u_, l_ = clt
            tr.op("act", lambda: nc.scalar.activation(e_.t[:], lam, AF.Exp, scale=-1.0), R=vecs.b, W=e_.b)
            tr.op("dve", lambda: nc.vector.tensor_scalar(u_.t[:], e_.t[:], 1.0, None, ALU.add), R=e_.b, W=u_.b)
            tr.op("act", lambda: nc.scalar.activation(l_.t[:], u_.t[:], AF.Ln), R=u_.b, W=l_.b)
            tr.op("dve", lambda: nc.vector.tensor_scalar(u_.t[:], u_.t[:], -1.0, None, ALU.add), R=u_.b, W=u_.b)
            tr.op("dve", lambda: nc.vector.reciprocal(u_.t[:], u_.t[:]), R=u_.b, W=u_.b)
            tr.op("dve", lambda: nc.vector.tensor_tensor(u_.t[:], u_.t[:], e_.t[:], ALU.mult), R=u_.b + e_.b, W=u_.b)
            tr.op("dve", lambda: nc.vector.tensor_tensor(l_.t[:], l_.t[:], u_.t[:], ALU.mult), R=u_.b + l_.b, W=l_.b)
            tr.op("dve", lambda: nc.vector.tensor_scalar(cl.t[:], l_.t[:], -8.0, None, ALU.mult), R=l_.b, W=cl.b)
            nba = vc["rg_ba"] + d * DC
            nbx = vc["rg_bx"] + d * DC
            k = [0]
            for si, (t0, L) in enumerate(g.seqs):
                tr.op("dve", lambda: nc.vector.memset(carry.t[:], 0.0), W=carry.b)
                ntile = L // NT
                order = range(ntile) if d == 0 else range(ntile - 1, -1, -1)
                for ti in order:
                    c0 = t0 + ti * NT
                    if d == 0:
                        lo = max(ti * NT - 1, 0)
                        hi = min(ti * NT + NT + 2, L)
                        o0 = lo - (ti * NT - 1)
                        if ti == 0:
                            tr.op("pool", lambda: nc.gpsimd.memset(halo.t[:, :, 0:1], 0.0), W=halo.b)
                        if ti == ntile - 1:
                            tr.op("pool", lambda: nc.gpsimd.memset(halo.t[:, :, NT + 1:NT + 3], 0.0), W=halo.b)
                        for q in range(4):
                            tr.dma("sp", halo.t[:, q * 4:(q + 1) * 4, o0:o0 + hi - lo], Uv[:, q * 4:(q + 1) * 4, t0 + lo:t0 + hi], W=[halo.b[q]])
                        for c in range(DC):
                            o = uc.t[:, c, :]
                            tr.op("dve", lambda: nc.vector.tensor_scalar(o, halo.t[:, c, 0:NT], vecs.t[:, cw + c:cw + c + 1], vecs.t[:, cb + c:cb + c + 1],
                                                                         ALU.mult, ALU.add),
                                  R=[halo.b[c // 4]] + vecs.b, W=[uc.b[c // 4]])
                            for kk in range(1, 4):
                                e = "dve"
                                tr.op(e, lambda: tr.eng[e].scalar_tensor_tensor(o, halo.t[:, c, kk:kk + NT],
                                                                                 vecs.t[:, cw + kk * DC + c:cw + kk * DC + c + 1], o, ALU.mult, ALU.add),
                                      R=[halo.b[c // 4], uc.b[c // 4]] + vecs.b, W=[uc.b[c // 4]])
                        for q in range(4):
                            tr.dma("pool", UCv[:, q * 4:(q + 1) * 4, c0:c0 + NT], uc.t[:, q * 4:(q + 1) * 4, :], R=[uc.b[q]])
                    else:
                        for q in range(4):
                            tr.dma("sp", uc.t[:, q * 4:(q + 1) * 4, :], UCv[:, q * 4:(q + 1) * 4, c0:c0 + NT], W=[uc.b[q]])
                            tr.dma("sp", hf.t[:, q * 4:(q + 1) * 4, :], HFv[:, q * 4:(q + 1) * 4, c0:c0 + NT], W=[hf.b[q]])
                            tr.dma("sp", gt.t[:, q * 4:(q + 1) * 4, :], Gv[:, q * 4:(q + 1) * 4, c0:c0 + NT], W=[gt.b[q]])
                    for q in range(4):
                        g.copy_op(("act", "dve")[q % 2], ucb.t[:, q * 4:(q + 1) * 4, :], uc.t[:, q * 4:(q + 1) * 4, :], [uc.b[q]], [ucb.b[q]])
                    for c in range(DC):
                        n, jj = c // 2, c % 2
                        pr, pi_ = g.nps(0, 6), g.nps(0, 6)
                        for gi, p_ in ((0, pr), (1, pi_)):
                            for kc in range(2):
                                tr.op("pe", lambda: nc.tensor.matmul(p_.t[:], lhsT=gw.t[:, gi, n, kc, jj * 128:(jj + 1) * 128], rhs=ucb.t[:, 2 * n + kc, :],
                                                                    start=(kc == 0), stop=(kc == 1)),
                                      R=gw.b + [ucb.b[(2 * n + kc) // 4]], W=p_.b, inc=(kc == 1))
                        a_, b_, c_ = ta[k[0] % 2], tb[k[0] % 2], tcx[k[0] % 2]
                        k[0] += 1
                        av = ab.t[:, c % 4, :]
                        tr.op("act", lambda: nc.scalar.activation(a_.t[:], pr.t[:], AF.Exp, bias=nvecs.t[:, nba + c:nba + c + 1], scale=-1.0),
                              R=pr.b + nvecs.b, W=a_.b)
                        tr.op("act", lambda: nc.scalar.activation(b_.t[:], pi_.t[:], AF.Exp, bias=nvecs.t[:, nbx + c:nbx + c + 1], scale=-1.0),
                              R=pi_.b + nvecs.b, W=b_.b)
                        tr.op("act", lambda: nc.scalar.activation(a_.t[:], a_.t[:], AF.Ln, bias=1.0, scale=1.0), R=a_.b, W=a_.b)
                        tr.op("act", lambda: nc.scalar.activation(b_.t[:], b_.t[:], AF.Ln, bias=1.0, scale=1.0), R=b_.b, W=b_.b)
                        tr.op("act", lambda: nc.scalar.activation(a_.t[:], a_.t[:], AF.Exp, scale=-1.0), R=a_.b, W=a_.b)
                        tr.op("act", lambda: nc.scalar.activation(b_.t[:], b_.t[:], AF.Exp, scale=-1.0), R=b_.b, W=b_.b)
                        tr.op("act", lambda: nc.scalar.activation(av, a_.t[:], AF.Exp, scale=cl.t[:, c:c + 1]), R=a_.b + cl.b, W=[ab.b[c % 4]])
                        tr.op("dve", lambda: nc.vector.tensor_tensor(c_.t[:], av, av, ALU.mult), R=[ab.b[c % 4]], W=c_.b)
                        tr.op("act", lambda: nc.scalar.activation(c_.t[:], c_.t[:], AF.Ln, bias=1.0, scale=-1.0), R=c_.b, W=c_.b)
                        tr.op("act", lambda: nc.scalar.activation(c_.t[:], c_.t[:], AF.Exp, scale=0.5), R=c_.b, W=c_.b)
                        tr.op("dve", lambda: nc.vector.tensor_tensor(b_.t[:], b_.t[:], uc.t[:, c, :], ALU.mult), R=b_.b + [uc.b[c // 4]], W=b_.b)
                        tr.op("dve", lambda: nc.vector.tensor_tensor(b_.t[:], b_.t[:], c_.t[:], ALU.mult), R=b_.b + c_.b, W=b_.b)
                        if d == 0:
                            tr.op("dve", lambda: nc.vector.tensor_tensor_scan(hh.t[:, c, :], av, b_.t[:], carry.t[:, c:c + 1], ALU.mult, ALU.add),
                                  R=[ab.b[c % 4], carry.b[c]] + b_.b, W=[hh.b[c // 4]])
                            tr.op("pool", lambda: nc.gpsimd.tensor_copy(carry.t[:, c:c + 1], hh.t[:, c, NT - 1:NT]), R=[hh.b[c // 4]], W=[carry.b[c]])
                        else:
                            tr.op("dve", lambda: nc.vector.tensor_tensor_scan(hh.t[:, c, ::-1], ab.t[:, c % 4, ::-1], b_.t[:, ::-1], carry.t[:, c:c + 1],
                                                                              ALU.mult, ALU.add),
                                  R=[ab.b[c % 4], carry.b[c]] + b_.b, W=[hh.b[c // 4]])
                            tr.op("pool", lambda: nc.gpsimd.tensor_copy(carry.t[:, c:c + 1], hh.t[:, c, 0:1]), R=[hh.b[c // 4]], W=[carry.b[c]])
                            tr.op("pool", lambda: nc.gpsimd.tensor_tensor(hh.t[:, c, :], hh.t[:, c, :], hf.t[:, c, :], ALU.add),
                                  R=[hh.b[c // 4], hf.b[c // 4]], W=[hh.b[c // 4]])
                            tr.op("dve", lambda: nc.vector.tensor_tensor(yo.t[:, c, :], hh.t[:, c, :], gt.t[:, c, :], ALU.mult),
                                  R=[hh.b[c // 4], gt.b[c // 4]], W=[yo.b[c // 4]])
                    for q in range(4):
                        if d == 0:
                            tr.dma("pool", HFv[:, q * 4:(q + 1) * 4, c0:c0 + NT], hh.t[:, q * 4:(q + 1) * 4, :], R=[hh.b[q]])
                        else:
                            tr.dma("pool", Yv[:, q * 4:(q + 1) * 4, c0:c0 + NT], yo.t[:, q * 4:(q + 1) * 4, :], R=[yo.b[q]])
        tr.barrier()


def ssd_mixer(g, j, vc, Ysc):
    nc, tr, vecs, nvecs, ps, tri, ident, ones_f = g.nc, g.tr, g.vecs, g.nvecs, g.ps, g.tri, g.ident, g.ones_f
    T = g.T
    XTv = g.XTv
    XBC = g.dscr("ssd_XBC", [6144, T])
    Zs = g.dscr("ssd_Z", [4096, T])
    DTA = g.dscr("ssd_DTA", [256, T])
    YA = g.dscr("ssd_YA", [T, 4096])
    XBCv = XBC.rearrange("(c p) t -> p c t", p=128)
    Zv = Zs.rearrange("(c p) t -> p c t", p=128)
    DTAv = DTA.rearrange("(c p) t -> p c t", p=128)
    Yv = Ysc.rearrange("(c p) t -> p c t", p=128)
    V = lambda fn, R, W: tr.op("dve", fn, R=R, W=W)
    A = lambda fn, R, W: tr.op("act", fn, R=R, W=W)
    P = lambda fn, R, W: tr.op("pool", fn, R=R, W=W)
    cw, cbv = vc["ssd_cw"], vc["ssd_cb"]
    CH = 128
    CVs = g.dscr("ssd_CV", [6144, T])
    CVv = CVs.rearrange("(c p) t -> p c t", p=128)
    with contextlib.ExitStack() as ps_:
        xres = Tl(ps_.enter_context(nc.sbuf_tensor(_un() + "s1_x", [128, DC, NT], F32)), 4)
        xb = Tl(ps_.enter_context(nc.sbuf_tensor(_un() + "s1_xb", [128, DC, NT], BF16)), 4)
        stg = [Tl(ps_.enter_context(nc.sbuf_tensor(_un() + "s1_st%d" % i, [128, 4, NT], F32))) for i in range(4)]
        dta = Tl(ps_.enter_context(nc.sbuf_tensor(_un() + "s1_dta", [128, 2, NT], F32)))
        avec = Tl(ps_.enter_context(nc.sbuf_tensor(_un() + "s1_av", [128, 1], F32)))
        wsl = [Tl(ps_.enter_context(nc.sbuf_tensor(_un() + "s1_w%d" % i, [128, 8192], BF16))) for i in range(3)]
        t1 = [Tl(ps_.enter_context(nc.sbuf_tensor(_un() + "s1_t1%d" % i, [128, NT], F32))) for i in range(2)]
        t2 = [Tl(ps_.enter_context(nc.sbuf_tensor(_un() + "s1_t2%d" % i, [128, NT], F32))) for i in range(2)]
        wctr = [0]
        k = [0]
        sg = [0]
        hlc = Tl(ps_.enter_context(nc.sbuf_tensor(_un() + "s1_hl", [128, 8, NT + 4], F32)))
        coc = Tl(ps_.enter_context(nc.sbuf_tensor(_un() + "s1_co", [128, 8, NT], F32)))
        xbuf = {}
        pending = []

        def conv_group(si, t0, L, ti, gq):
            ntile = L // NT
            lo = max(ti * NT - 1, 0)
            hi = min(ti * NT + NT + 2, L)
            o0 = lo - (ti * NT - 1)
            h_, o_ = hlc, coc
            if ti == 0:
                V(lambda: nc.vector.memset(h_.t[:, :, 0:1], 0.0), [], h_.b)
            if ti == ntile - 1:
                V(lambda: nc.vector.memset(h_.t[:, :, NT + 1:NT + 3], 0.0), [], h_.b)
            rb = [xbuf[(si, tt, gg)] for tt in (ti - 1, ti, ti + 1) if 0 <= tt < ntile for gg in (2 * gq, 2 * gq + 1)]
            tr.dma("sp", h_.t[:, :, o0:o0 + hi - lo], XBCv[:, gq * 8:(gq + 1) * 8, t0 + lo:t0 + hi], R=rb, W=h_.b)
            for cc in range(8):
                c = gq * 8 + cc
                o = o_.t[:, cc, :]
                A(lambda: nc.scalar.activation(o, h_.t[:, cc, 0:NT], AF.Identity, bias=vecs.t[:, cbv + c:cbv + c + 1],
                                               scale=vecs.t[:, cw + c:cw + c + 1]), h_.b + vecs.b, o_.b)
                for kk in range(1, 4):
                    V(lambda: nc.vector.scalar_tensor_tensor(o, h_.t[:, cc, kk:kk + NT], vecs.t[:, cw + kk * 48 + c:cw + kk * 48 + c + 1], o,
                                                              ALU.mult, ALU.add), h_.b + vecs.b + o_.b, o_.b)
            A(lambda: nc.scalar.activation(o_.t[:], o_.t[:], AF.Silu), o_.b, o_.b)
            tr.dma("pool", CVv[:, gq * 8:(gq + 1) * 8, t0 + ti * NT:t0 + (ti + 1) * NT], o_.t[:], R=o_.b)

        A(lambda: nc.scalar.activation(avec.t[:], vecs.t[:, vc["ssd_alog"]:vc["ssd_alog"] + 1], AF.Exp), vecs.b, avec.b)
        V(lambda: nc.vector.tensor_scalar(avec.t[:], avec.t[:], -1.0, None, ALU.mult), avec.b, avec.b)
        cur = {}

        def consume(fc, p_):
            q = fc // 4
            if fc % 4 == 0:
                cur["s"] = stg[sg[0] % 4]
                sg[0] += 1
            st_ = cur["s"]
            if fc < 32:
                A(lambda: nc.scalar.activation(st_.t[:, fc % 4, :], p_.t[:], AF.Silu), p_.b, st_.b)
            else:
                g.copy_op(("act", "dve")[fc % 2], st_.t[:, fc % 4, :], p_.t[:], p_.b, st_.b)
            if fc % 4 == 3:
                c0 = cur["c0"]
                if fc < 32:
                    tr.dma("pool", Zv[:, q * 4:(q + 1) * 4, c0:c0 + NT], st_.t[:], R=st_.b)
                else:
                    xb_ = Buf()
                    xbuf[(cur["si"], cur["ti"], q - 8)] = xb_
                    tr.dma("pool", XBCv[:, (q - 8) * 4:(q - 7) * 4, c0:c0 + NT], st_.t[:], R=st_.b, W=[xb_])
            if fc % 12 == 10 and pending:
                pending.pop(0)()

        def consume_dt(fc, p_):
            a = t1[k[0] % 2]
            k[0] += 1
            A(lambda: nc.scalar.activation(a.t[:], p_.t[:], AF.Exp, bias=vecs.t[:, vc["ssd_dtb"]:vc["ssd_dtb"] + 1], scale=1.0), p_.b + vecs.b, a.b)
            A(lambda: nc.scalar.activation(dta.t[:, 0, :], a.t[:], AF.Ln, bias=1.0, scale=1.0), a.b, dta.b)
            V(lambda: nc.vector.tensor_scalar(dta.t[:, 1, :], dta.t[:, 0, :], avec.t[:, 0:1], None, ALU.mult), dta.b + avec.b, dta.b)
            c0 = cur["c0"]
            tr.dma("pool", DTAv[:, :, c0:c0 + NT], dta.t[:], R=dta.b)

        for si, (t0, L) in enumerate(g.seqs):
            for ti in range(L // NT):
                c0 = t0 + ti * NT
                cur["c0"] = c0
                cur["si"], cur["ti"] = si, ti
                for q in range(4):
                    tr.dma("sp", xres.t[:, q * 4:(q + 1) * 4, :], XTv[:, q * 4:(q + 1) * 4, c0:c0 + NT], W=[xres.b[q]])
                    g.copy_op(("act", "dve")[q % 2], xb.t[:, q * 4:(q + 1) * 4, :], xres.t[:, q * 4:(q + 1) * 4, :], [xres.b[q]], [xb.b[q]])
                g.linear("ssd_in", wsl, wctr, lambda kc: xb.t[:, kc, :], lambda kc: [xb.b[kc // 4]], consume)
                g.linear("ssd_dt", wsl, wctr, lambda kc: xb.t[:, kc, :], lambda kc: [xb.b[kc // 4]], consume_dt)
                while pending:
                    pending.pop(0)()
                if ti >= 1:
                    for gq in range(6):
                        pending.append(lambda si=si, t0=t0, L=L, tj=ti - 1, gq=gq: conv_group(si, t0, L, tj, gq))
                if ti == L // NT - 1:
                    while pending:
                        pending.pop(0)()
                    for gq in range(6):
                        conv_group(si, t0, L, ti, gq)
    tr.barrier()

    for dr in range(2):
        with contextlib.ExitStack() as ps_:
            sb_ = lambda nm, shp, dt=F32, n=1: Tl(ps_.enter_context(nc.sbuf_tensor(_un() + "s2_" + nm, shp, dt)), n)
            cv = sb_("cv", [128, 48, CH], F32, 6)
            cvb = sb_("cvb", [128, 16, CH], BF16)
            xtok = sb_("xtok", [128, 4096], BF16, 8)
            btok = sb_("btok", [128, 1024], BF16, 2)
            xdt = sb_("xdt", [128, 4096], BF16)
            xdd = xtok
            dain = sb_("dain", [128, 2, CH])
            datok = sb_("datok", [128, 256])
            dcs = sb_("dcs", [128, 192])
            cbm = [sb_("cbm%d" % i, [128, CH]) for i in range(2)]
            srh = [sb_("srh%d" % i, [128, 8, CH]) for i in range(2)]
            ee = [sb_("ee%d" % i, [128, 8, CH]) for i in range(2)]
            mt = [sb_("mt%d" % i, [128, 8, CH], BF16) for i in range(2)]
            h32 = sb_("h32", [128, 8, 512], F32, 8)
            hbf = sb_("hbf", [128, 8, 512], BF16, 8)
            yacc = sb_("yacc", [128, 4096], F32, 8)
            if dr == 1:
                yprev = sb_("yprev", [128, 4096])
                yF = sb_("yF", [128, 32, CH], F32, 8)
                zt = sb_("zt", [128, 32, CH])
                rs_ = sb_("rs", [128, 8, CH])
                yout = sb_("yout", [128, 32, 2 * CH], BF16)
            SLm = tri.t[:, 0 if dr == 0 else 2, :]
            LIm = tri.t[:, 1 if dr == 0 else 3, :]
            pseg = [ps[3], ps[4]]
            py, po, pst = ps[5], ps[6], ps[7]
            gn = [0]
            for si, (t0, L) in enumerate(g.seqs):
                nch = L // CH
                V(lambda: nc.vector.memset(h32.t[:], 0.0), [], h32.b)
                V(lambda: nc.vector.memset(hbf.t[:], 0.0), [], hbf.b)
                order = range(nch) if dr == 0 else range(nch - 1, -1, -1)
                for ci in order:
                    c0 = t0 + ci * CH
                    for q in range(6):
                        tr.dma("sp", cv.t[:, q * 8:(q + 1) * 8, :], CVv[:, q * 8:(q + 1) * 8, c0:c0 + CH], W=[cv.b[q]])
                    A(lambda: nc.scalar.copy(cvb.t[:], cv.t[:, 32:48, :]), cv.b[4:6], cvb.b)
                    for q in range(10):
                        p_ = g.nps(0, 3)
                        for cc in range(4):
                            c = q * 4 + cc
                            tr.op("pe", lambda: nc.tensor.transpose(p_.t[:, cc * 128:(cc + 1) * 128], cv.t[:, c, :], ident.t[:]),
                                  R=cv.b + ident.b, W=p_.b, inc=(cc == 3))
                        if q < 8:
                            g.copy_op(("act", "dve")[q % 2], xtok.t[:, q * 512:(q + 1) * 512], p_.t[:], p_.b, [xtok.b[q]])
                        else:
                            g.copy_op(("act", "dve")[q % 2], btok.t[:, (q - 8) * 512:(q - 7) * 512], p_.t[:], p_.b, [btok.b[q - 8]])
                    tr.dma("sp", dain.t[:], DTAv[:, :, c0:c0 + CH], W=dain.b)
                    p_ = g.nps(0, 3)
                    for cc in range(2):
                        tr.op("pe", lambda: nc.tensor.transpose(p_.t[:, cc * 128:(cc + 1) * 128], dain.t[:, cc, :], ident.t[:]),
                              R=dain.b + ident.b, W=p_.b, inc=(cc == 1))
                    A(lambda: nc.scalar.copy(datok.t[:], p_.t[:, 0:256]), p_.b, datok.b)
                    a_dir = datok.t[:, 128 + dr * 64:128 + (dr + 1) * 64]
                    dt_dir = datok.t[:, dr * 64:(dr + 1) * 64]
                    p_ = g.nps(0, 3)
                    for i_, lm in enumerate((LIm, SLm, ones_f.t[:])):
                        tr.op("pe", lambda: nc.tensor.matmul(p_.t[:, i_ * 64:(i_ + 1) * 64], lhsT=lm, rhs=a_dir, start=True, stop=True),
                              R=tri.b + ones_f.b + datok.b, W=p_.b, inc=(i_ == 2))
                    A(lambda: nc.scalar.activation(dcs.t[:], p_.t[:, 0:192], AF.Exp), p_.b, dcs.b)
                    dec = dcs.t[:, 0:64]
                    dend = dcs.t[:, 64:128]
                    cdec = dcs.t[:, 128:192]
                    V(lambda: nc.vector.tensor_tensor(xdt.t[:].rearrange("p (h q) -> p h q", h=64), xtok.t[:].rearrange("p (h q) -> p h q", h=64),
                                                      dt_dir.unsqueeze(2).to_broadcast([128, 64, 64]), ALU.mult), xtok.b + datok.b, xdt.b)
                    V(lambda: nc.vector.tensor_tensor(xdd.t[:].rearrange("p (h q) -> p h q", h=64), xdt.t[:].rearrange("p (h q) -> p h q", h=64),
                                                      dend.unsqueeze(2).to_broadcast([128, 64, 64]), ALU.mult), xdt.b + dcs.b, xdd.b)
                    for gi in range(8):
                        cb_, sr_, e_, m_ = cbm[gn[0] % 2], srh[gn[0] % 2], ee[gn[0] % 2], mt[gn[0] % 2]
                        gn[0] += 1
                        BT = cvb.t[:, gi, :]
                        CT = cvb.t[:, 8 + gi, :]
                        p_ = g.nps(0, 3)
                        tr.op("pe", lambda: nc.tensor.matmul(p_.t[:, 0:CH], lhsT=BT, rhs=CT, start=True, stop=True), R=cvb.b, W=p_.b)
                        V(lambda: nc.vector.tensor_tensor(cb_.t[:], p_.t[:, 0:CH], LIm, ALU.mult), p_.b + tri.b, cb_.b)
                        V(lambda: nc.vector.tensor_tensor(sr_.t[:], a_dir[:, gi * 8:(gi + 1) * 8].unsqueeze(2).to_broadcast([128, 8, CH]),
                                                          LIm.unsqueeze(1).to_broadcast([128, 8, CH]), ALU.mult), datok.b + tri.b, sr_.b)
                        for i_ in range(2):
                            tr.op("pe", lambda: nc.tensor.matmul(pseg[i_].t[:], lhsT=SLm, rhs=sr_.t[:, i_ * 4:(i_ + 1) * 4, :].rearrange("p h t -> p (h t)"),
                                                                start=True, stop=True), R=tri.b + sr_.b, W=pseg[i_].b)
                            A(lambda: nc.scalar.activation(e_.t[:, i_ * 4:(i_ + 1) * 4, :].rearrange("p h t -> p (h t)"), pseg[i_].t[:], AF.Exp),
                              pseg[i_].b, e_.b)
                        V(lambda: nc.vector.tensor_tensor(m_.t[:], e_.t[:], cb_.t[:].unsqueeze(1).to_broadcast([128, 8, CH]), ALU.mult), e_.b + cb_.b, m_.b)
                        for hh in range(8):
                            hd = gi * 8 + hh
                            tr.op("pe", lambda: nc.tensor.matmul(py.t[:, hh * 64:(hh + 1) * 64], lhsT=m_.t[:, hh, :], rhs=xdt.t[:, hd * 64:(hd + 1) * 64],
                                                                start=True, stop=True), R=m_.b + xdt.b, W=py.b, inc=(hh == 7))
                        tr.op("pe", lambda: nc.tensor.matmul(po.t[:], lhsT=CT, rhs=hbf.t[:, gi, :], start=True, stop=True), R=cvb.b + [hbf.b[gi]], W=po.b)
                        ya = yacc.t[:, gi * 512:(gi + 1) * 512]
                        V(lambda: nc.vector.tensor_tensor(ya.rearrange("p (h q) -> p h q", h=8), po.t[:].rearrange("p (h q) -> p h q", h=8),
                                                          dec[:, gi * 8:(gi + 1) * 8].unsqueeze(2).to_broadcast([128, 8, 64]), ALU.mult),
                          po.b + dcs.b, [yacc.b[gi]])
                        V(lambda: nc.vector.tensor_tensor(ya, ya, py.t[:], ALU.add), py.b + [yacc.b[gi]], [yacc.b[gi]])
                        tr.op("pe", lambda: nc.tensor.matmul(pst.t[:], lhsT=btok.t[:, gi * 128:(gi + 1) * 128], rhs=xdd.t[:, gi * 512:(gi + 1) * 512],
                                                            start=True, stop=True), R=btok.b + xdd.b, W=pst.b)
                        hv = h32.t[:, gi, :]
                        V(lambda: nc.vector.tensor_tensor(hv.rearrange("p (h q) -> p h q", h=8), hv.rearrange("p (h q) -> p h q", h=8),
                                                          cdec[:, gi * 8:(gi + 1) * 8].unsqueeze(2).to_broadcast([128, 8, 64]), ALU.mult),
                          [h32.b[gi]] + dcs.b, [h32.b[gi]])
                        V(lambda: nc.vector.tensor_tensor(hv, hv, pst.t[:], ALU.add), pst.b + [h32.b[gi]], [h32.b[gi]])
                        A(lambda: nc.scalar.copy(hbf.t[:, gi, :], hv), [h32.b[gi]], [hbf.b[gi]])
                    YAv = YA[c0:c0 + CH, :]
                    if dr == 0:
                        tr.dma("pool", YAv, yacc.t[:], R=yacc.b)
                        continue
                    tr.dma("sp", yprev.t[:], YAv, W=yprev.b)
                    tr.dma("sp", zt.t[:], Zv[:, :, c0:c0 + CH], W=zt.b)
                    V(lambda: nc.vector.tensor_tensor(yacc.t[:], yacc.t[:], yprev.t[:], ALU.add), yacc.b + yprev.b, yacc.b)
                    for q in range(8):
                        p_ = g.nps(0, 3)
                        for cc in range(4):
                            c = q * 4 + cc
                            tr.op("pe", lambda: nc.tensor.transpose(p_.t[:, cc * 128:(cc + 1) * 128], yacc.t[:, c * 128:(c + 1) * 128], ident.t[:]),
                                  R=yacc.b + ident.b, W=p_.b, inc=(cc == 3))
                        for cc in range(4):
                            c = q * 4 + cc
                            V(lambda: nc.vector.scalar_tensor_tensor(yF.t[:, c, :], cv.t[:, c, :], vecs.t[:, vc["ssd_D"] + c:vc["ssd_D"] + c + 1],
                                                                      p_.t[:, cc * 128:(cc + 1) * 128], ALU.mult, ALU.add),
                              cv.b + vecs.b + p_.b, [yF.b[q]])
                    V(lambda: nc.vector.tensor_tensor(yF.t[:], yF.t[:], zt.t[:], ALU.mult), yF.b + zt.b, yF.b)
                    sq = yprev
                    sqv = sq.t[:].rearrange("p (c t) -> p c t", c=32)
                    A(lambda: nc.scalar.activation(sqv, yF.t[:], AF.Square), yF.b, sq.b)
                    pn = [g.nps(0, 3), g.nps(0, 3)]
                    for gi in range(8):
                        for i_ in range(4):
                            tr.op("pe", lambda: nc.tensor.matmul(pn[gi // 4].t[:, (gi % 4) * 128:(gi % 4 + 1) * 128], lhsT=ones_f.t[:], rhs=sqv[:, gi * 4 + i_, :],
                                                                start=(i_ == 0), stop=(i_ == 3)), R=ones_f.b + sq.b, W=pn[gi // 4].b,
                                  inc=(i_ == 3 and gi % 4 == 3))
                    for i_ in range(2):
                        V(lambda: nc.vector.tensor_scalar(rs_.t[:, i_ * 4:(i_ + 1) * 4, :].rearrange("p h t -> p (h t)"), pn[i_].t[:], 1.0 / 512, LN_EPS,
                                                          ALU.mult, ALU.add), pn[i_].b, rs_.b)
                    A(lambda: nc.scalar.activation(rs_.t[:], rs_.t[:], AF.Ln), rs_.b, rs_.b)
                    A(lambda: nc.scalar.activation(rs_.t[:], rs_.t[:], AF.Exp, scale=-0.5), rs_.b, rs_.b)
                    sub = ci % 2
                    for c in range(32):
                        V(lambda: nc.vector.scalar_tensor_tensor(yout.t[:, c, sub * CH:(sub + 1) * CH], yF.t[:, c, :], vecs.t[:, vc["ssd_ng"] + c:vc["ssd_ng"] + c + 1],
                                                                  rs_.t[:, c // 4, :], ALU.mult, ALU.mult), yF.b + vecs.b + rs_.b, yout.b)
                    if sub == 0:
                        tc0 = t0 + (ci // 2) * 2 * CH
                        for q in range(4):
                            tr.dma("pool", Yv[:, q * 8:(q + 1) * 8, tc0:tc0 + 2 * CH], yout.t[:, q * 8:(q + 1) * 8, :], R=yout.b)
        tr.barrier()


def na_static():
    R = 32
    kr2 = np.arange(128) // 64
    kc = np.arange(128) % 64
    qr8 = np.arange(512) // 64
    qc = np.arange(512) % 64
    wst = np.clip(qc - 8, 0, 48)
    colok = (kc[:, None] >= wst[None, :]) & (kc[:, None] < wst[None, :] + 16)
    dx = np.clip(kc[:, None] - qc[None, :] + 15, 0, 30)
    valid = np.zeros((3, 8, 128, 512), bool)
    dy = np.zeros((3, 8, 128, 512), np.int64)
    for ty, m in enumerate((0, 1, 3)):
        base = int(np.clip(8 * m - 4, 0, R - 16))
        r = 8 * m + qr8
        rs = np.clip(r - 4, 0, R - 8)
        for jj in range(8):
            kr = base + 2 * jj + kr2
            rowok = (kr[:, None] >= rs[None, :]) & (kr[:, None] < rs[None, :] + 8)
            valid[ty, jj] = rowok & colok
            dy[ty, jj] = np.clip(kr[:, None] - r[None, :] + 7, 0, 14)
    return valid, dy, np.broadcast_to(dx, (3, 8, 128, 512))


def na_table(rpb):
    valid, dy, dx = na_static()
    rpb = np.asarray(rpb, np.float32)
    tab = rpb[:, dy, dx]
    tab = np.where(valid[None], tab, np.float32(NEG)).astype(np.float32)
    return np.ascontiguousarray(np.transpose(tab, (1, 0, 2, 3, 4)))


def na_mixer(g, j, vc, Ysc):
    nc, tr, vecs, ps = g.nc, g.tr, g.vecs, g.ps
    T = g.T
    XTv = g.XTv
    valid, _, _ = na_static()
    jlist = [[jj for jj in range(8) if valid[ty, jj].any()] for ty in range(3)]
    QT = g.dscr("na_QT", [D, T], BF16)
    KT = g.dscr("na_KT", [D, T], BF16)
    VT = g.dscr("na_VT", [T, D], BF16)
    QTv = QT.rearrange("(c p) t -> p c t", p=128)
    KTv = KT.rearrange("(c p) t -> p c t", p=128)
    VTv = VT.rearrange("(n p) f -> p n f", p=128)
    Yv = Ysc.rearrange("(c p) t -> p c t", p=128)
    scale = 128 ** -0.5
    dst, KC, NF, _pw = g.Wb["na_qkv"]
    with contextlib.ExitStack() as ps_:
        xres = Tl(ps_.enter_context(nc.sbuf_tensor(_un() + "n1_x", [128, DC, NT], F32)), 4)
        xb = Tl(ps_.enter_context(nc.sbuf_tensor(_un() + "n1_xb", [128, DC, NT], BF16)), 4)
        qo = Tl(ps_.enter_context(nc.sbuf_tensor(_un() + "n1_q", [128, DC, NT], BF16)), 4)
        ko = Tl(ps_.enter_context(nc.sbuf_tensor(_un() + "n1_k", [128, DC, NT], BF16)), 4)
        vo = Tl(ps_.enter_context(nc.sbuf_tensor(_un() + "n1_v", [128, 4, D], BF16)), 4)
        bv = Tl(ps_.enter_context(nc.sbuf_tensor(_un() + "n1_bv", [128, D], F32)))
        bqs = Tl(ps_.enter_context(nc.sbuf_tensor(_un() + "n1_bqs", [128, DC], F32)))
        wsl = [Tl(ps_.enter_context(nc.sbuf_tensor(_un() + "n1_w%d" % i, [128, 8192], BF16))) for i in range(3)]
        wctr = [0]
        tr.op("dve", lambda: nc.vector.tensor_scalar(bqs.t[:], vecs.t[:, vc["na_bq"]:vc["na_bq"] + DC], scale, None, ALU.mult), R=vecs.b, W=bqs.b)
        bk = vc["na_bk"]

        def consume(fc, p_):
            if fc < DC:
                tr.op("act", lambda: nc.scalar.activation(qo.t[:, fc, :], p_.t[:], AF.Identity, bias=bqs.t[:, fc:fc + 1], scale=scale),
                      R=p_.b + bqs.b, W=[qo.b[fc // 4]])
            else:
                c = fc - DC
                tr.op("act", lambda: nc.scalar.activation(ko.t[:, c, :], p_.t[:], AF.Identity, bias=vecs.t[:, bk + c:bk + c + 1], scale=1.0),
                      R=p_.b + vecs.b, W=[ko.b[c // 4]])

        for si, (t0, L) in enumerate(g.seqs):
            for ti in range(L // NT):
                c0 = t0 + ti * NT
                for q in range(4):
                    tr.dma("sp", xres.t[:, q * 4:(q + 1) * 4, :], XTv[:, q * 4:(q + 1) * 4, c0:c0 + NT], W=[xres.b[q]])
                    g.copy_op(("act", "dve")[q % 2], xb.t[:, q * 4:(q + 1) * 4, :], xres.t[:, q * 4:(q + 1) * 4, :], [xres.b[q]], [xb.b[q]])
                g.linear("na_qkv", wsl, wctr, lambda kc: xb.t[:, kc, :], lambda kc: [xb.b[kc // 4]], consume, f_lo=0, f_hi=32)
                for fp in range(0 if not os.environ.get("NA_NOV") else 4, 4):
                    slot = wsl[wctr[0] % 3]
                    wctr[0] += 1
                    tr.dma("sp", slot.t[:], dst[8 + fp], W=slot.b)
                    wv = slot.t[:].rearrange("p (k f) -> p k f", k=16)
                    for b in range(4):
                        p_ = g.nps(0, 4)
                        for kc in range(16):
                            tr.op("pe", lambda: nc.tensor.matmul(p_.t[:], lhsT=xb.t[:, kc, b * 128:(b + 1) * 128], rhs=wv[:, kc, :],
                                                                start=(kc == 0), stop=(kc == 15)),
                                  R=slot.b + [xb.b[kc // 4]], W=p_.b, inc=(kc == 15))
                        g.copy_op(("act", "dve")[b % 2], vo.t[:, b, fp * 512:(fp + 1) * 512], p_.t[:], p_.b, [vo.b[b]])
                for q in range(4):
                    tr.dma("pool", QTv[:, q * 4:(q + 1) * 4, c0:c0 + NT], qo.t[:, q * 4:(q + 1) * 4, :], R=[qo.b[q]])
                    tr.dma("pool", KTv[:, q * 4:(q + 1) * 4, c0:c0 + NT], ko.t[:, q * 4:(q + 1) * 4, :], R=[ko.b[q]])
                    tr.dma("pool", VTv[:, c0 // 128 + q, :], vo.t[:, q, :], R=[vo.b[q]])
    tr.barrier()
    if os.environ.get("NA_SKIP2"):
        return
    LM = max(L for _, L in g.seqs)
    with contextlib.ExitStack() as ps_:
        qh = [Tl(ps_.enter_context(nc.sbuf_tensor(_un() + "n2_q%d" % i, [128, LM], BF16))) for i in range(2)]
        kh = [Tl(ps_.enter_context(nc.sbuf_tensor(_un() + "n2_k%d" % i, [128, LM], BF16))) for i in range(2)]
        vh = [Tl(ps_.enter_context(nc.sbuf_tensor(_un() + "n2_v%d" % i, [128, LM // 128, 128], BF16))) for i in range(2)]
        oh = [Tl(ps_.enter_context(nc.sbuf_tensor(_un() + "n2_o%d" % i, [128, LM], BF16))) for i in range(2)]
        tab = Tl(ps_.enter_context(nc.sbuf_tensor(_un() + "n2_tab", [128, 3, 8, NT], F32)), 3)
        sb = [Tl(ps_.enter_context(nc.sbuf_tensor(_un() + "n2_sb%d" % i, [128, NT], F32))) for i in range(3)]
        pb = [Tl(ps_.enter_context(nc.sbuf_tensor(_un() + "n2_pb%d" % i, [128, NT], BF16))) for i in range(3)]
        ri = [Tl(ps_.enter_context(nc.sbuf_tensor(_un() + "n2_ri%d" % i, [128, NT], F32))) for i in range(2)]
        n = 0
        k = 0
        mm = 0
        for si, (t0, L) in enumerate(g.seqs):
            R = L // 64
            nm = L // NT
            for h in range(16):
                q_, k_, v_, o_ = qh[n % 2], kh[n % 2], vh[n % 2], oh[n % 2]
                n += 1
                tr.dma("sp", q_.t[:, 0:L], QTv[:, h, t0:t0 + L], W=q_.b)
                tr.dma("sp", k_.t[:, 0:L], KTv[:, h, t0:t0 + L], W=k_.b)
                tr.dma("sp", v_.t[:, 0:L // 128, :], VTv[:, t0 // 128:(t0 + L) // 128, h * 128:(h + 1) * 128], W=v_.b)
                if si == 0 or True:
                    for ty in range(3):
                        tr.dma("sp", tab.t[:, ty], g.natab_d[ty, h].rearrange("j p q -> p j q"), W=[tab.b[ty]])
                for m in range(nm):
                    ty = 0 if m == 0 else (2 if m == nm - 1 else 1)
                    base = int(np.clip(8 * m - 4, 0, R - 16))
                    po, pr = ps[4 + mm % 2], ps[6 + mm % 2]
                    mm += 1
                    jl = jlist[ty]
                    stg = {}

                    def s_stage(jj):
                        nonlocal k
                        tok = (base + 2 * jj) * 64
                        p_s = g.nps(0, 4)
                        s_, p_ = sb[k % 3], pb[k % 3]
                        k += 1
                        tr.op("pe", lambda: nc.tensor.matmul(p_s.t[:], lhsT=k_.t[:, tok:tok + 128], rhs=q_.t[:, m * NT:(m + 1) * NT], start=True, stop=True),
                              R=k_.b + q_.b, W=p_s.b)
                        tr.op("dve", lambda: nc.vector.tensor_tensor(s_.t[:], p_s.t[:], tab.t[:, ty, jj, :], ALU.add), R=p_s.b + [tab.b[ty]], W=s_.b)
                        tr.op("act", lambda: nc.scalar.activation(p_.t[:], s_.t[:], AF.Exp), R=s_.b, W=p_.b)
                        stg[jj] = (tok, p_)

                    def pv_stage(jj):
                        tok, p_ = stg.pop(jj)
                        tr.op("pe", lambda: nc.tensor.matmul(po.t[:], lhsT=v_.t[:, tok // 128, :], rhs=p_.t[:], start=(jj == jl[0]), stop=(jj == jl[-1])),
                              R=v_.b + p_.b, W=po.b)
                        tr.op("pe", lambda: nc.tensor.matmul(pr.t[:], lhsT=g.ones_b.t[:], rhs=p_.t[:], start=(jj == jl[0]), stop=(jj == jl[-1])),
                              R=g.ones_b.b + p_.b, W=pr.b)

                    LA_ = 2
                    for i_ in range(min(LA_, len(jl))):
                        s_stage(jl[i_])
                    for i_ in range(len(jl)):
                        if i_ + LA_ < len(jl):
                            s_stage(jl[i_ + LA_])
                        pv_stage(jl[i_])
                    r_ = ri[mm % 2]
                    tr.op("dve", lambda: nc.vector.reciprocal(r_.t[:], pr.t[:]), R=pr.b, W=r_.b)
                    tr.op("dve", lambda: nc.vector.tensor_tensor(r_.t[:], po.t[:], r_.t[:], ALU.mult), R=po.b + r_.b, W=r_.b)
                    tr.op("act", lambda: nc.scalar.activation(o_.t[:, m * NT:(m + 1) * NT], r_.t[:], AF.Identity,
                                                              bias=vecs.t[:, vc["na_bvp"] + h:vc["na_bvp"] + h + 1], scale=1.0),
                          R=r_.b + vecs.b, W=o_.b)
                tr.dma("pool", Yv[:, h, t0:t0 + L], o_.t[:, 0:L], R=o_.b)
    tr.barrier()


LAYERS = [("rg", 0), ("ssd", 0), ("na", 0), ("rg", 1)]
WNAMES = ["rg_w_in", "rg_w_a", "rg_w_x", "rg_w_out", "ssd_w_in", "ssd_w_out", "na_w_qkv", "na_w_out", "mlp_w_up", "mlp_w_down"]


def pack_consts(inp, layers):
    cols = []
    vcols = []
    off = [0]

    def add(v):
        a = _vec_layout(v)
        cols.append(a)
        o = off[0]
        off[0] += a.shape[1]
        return o

    rws = [np.zeros(16, np.float32)]
    roff = [16]

    def radd(v):
        v = np.asarray(v, np.float32).reshape(-1)
        rws.append(v)
        o = roff[0]
        roff[0] += v.size
        return o

    for li, (kind, j) in enumerate(layers):
        vc = {}
        vc["ln1g"] = add(inp["ln1_g"][li])
        vc["ln1b"] = add(inp["ln1_b"][li])
        vc["ln2g"] = add(inp["ln2_g"][li])
        vc["ln2b"] = add(inp["ln2_b"][li])
        if kind == "rg":
            vc["rg_cw"] = add(inp["rg_conv_w"][j][0])
            for kk in range(1, 4):
                add(inp["rg_conv_w"][j][kk])
            vc["rg_cb"] = add(inp["rg_conv_b"][j])
            vc["rg_ba"] = add(inp["rg_b_a"][j][0])
            add(inp["rg_b_a"][j][1])
            vc["rg_bx"] = add(inp["rg_b_x"][j][0])
            add(inp["rg_b_x"][j][1])
            vc["rg_lam"] = add(inp["rg_lambda"][j][0])
            add(inp["rg_lambda"][j][1])
        if kind == "ssd":
            cwv = np.asarray(inp["ssd_conv_w"][j], np.float32)
            vc["ssd_cw"] = add(cwv[0])
            for kk in range(1, 4):
                add(cwv[kk])
            vc["ssd_cb"] = add(inp["ssd_conv_b"][j])
            vc["ssd_dtb"] = add(np.asarray(inp["ssd_dt_bias"][j], np.float32).reshape(-1))
            vc["ssd_alog"] = add(np.asarray(inp["ssd_a_log"][j], np.float32).reshape(-1))
            vc["ssd_D"] = add(np.repeat(np.asarray(inp["ssd_d"][j], np.float32), 64))
            vc["ssd_ng"] = add(inp["ssd_norm_g"][j])
        if kind == "na":
            bq = np.asarray(inp["na_b_qkv"][j], np.float32)
            vc["na_bq"] = add(bq[0:D])
            vc["na_bk"] = add(bq[D:2 * D])
            vc["na_bvp"] = add(bq[2 * D:3 * D])
        vcols.append(vc)
    vecs = np.ascontiguousarray(np.concatenate(cols, axis=1))
    rows = np.ascontiguousarray(np.concatenate(rws)[None, :])
    return vecs, vcols, rows


def run(inp, layers, LA, LB, n_cores, xa_list, xb_list):
    global g_vcols
    vecs, vcols, rows = pack_consts(inp, layers)
    g_vcols = vcols
    wshapes = {k: tuple(np.asarray(inp[k]).shape) for k in WNAMES}
    has_na = any(k == "na" for k, _ in layers)
    natab = na_table(inp["na_rpb"][0]) if has_na else None
    import time as _t
    _t0 = _t.time()
    nc = build(LA, LB, layers, wshapes, vecs.shape[1], rows.shape[1], natab.shape if has_na else None)
    print("build_s", _t.time() - _t0, flush=True)
    ident = np.eye(128, dtype=np.float32)
    kk = np.arange(128)
    tri = np.stack([(kk[:, None] > kk[None, :]), (kk[:, None] <= kk[None, :]), (kk[:, None] < kk[None, :]), (kk[:, None] >= kk[None, :])]).astype(np.float32)
    base = {k: np.ascontiguousarray(np.asarray(inp[k], np.float32)) for k in WNAMES}
    base.update(vecs=vecs, rows=rows, ident=ident, tri=tri)
    if has_na:
        base["natab"] = natab
    in_maps = []
    for c in range(n_cores):
        m = dict(base)
        m["xa"] = np.ascontiguousarray(xa_list[c], dtype=np.float32)
        m["xb"] = np.ascontiguousarray(xb_list[c], dtype=np.float32)
        in_maps.append(m)
    res = run_bass_kernel_spmd(nc, in_maps, core_ids=list(range(n_cores)))
    return [(r["ya"], r["yb"]) for r in res.results]


def kernel(**inputs):
    xp = np.asarray(inputs["x_prompt"], np.float32)
    xs = np.asarray(inputs["x_sample"], np.float32)
    xa_list = [xp[c] for c in range(8)]
    xb_list = [xs[c % 4] for c in range(8)]
    outs = run(inputs, LAYERS, 2048, 8192, 8, xa_list, xb_list)
    yp = np.stack([outs[c][0] for c in range(8)], axis=0)
    ys = np.stack([outs[c][1] for c in range(4)], axis=0)
    return (yp, ys)
```

```python
import contextlib
import math
import os
import numpy as np
import concourse.bass as bass
import concourse.mybir as mybir
from concourse.bass_utils import run_bass_kernel_spmd

F32 = mybir.dt.float32
BF16 = mybir.dt.bfloat16
AF = mybir.ActivationFunctionType
ALU = mybir.AluOpType

D = 2048
DC = 16
DEPTH = 4
ALPHA = (2 * DEPTH) ** 0.25
LN_EPS = 1e-5
NT = 512
MLP_H = 8192
NEG = -30000.0
GELU_K = 1.5957691216057308
NS = 8


_uc = [0]


def _un():
    _uc[0] += 1
    return "t%d_" % _uc[0]


class Buf:
    __slots__ = ("w", "r")

    def __init__(self):
        self.w = None
        self.r = {}


class Tl:
    def __init__(self, t, n=1):
        self.t = t
        self.b = [Buf() for _ in range(n)]


class TR:
    def __init__(self, nc, es):
        self.nc = nc
        self.eng = {"pe": nc.tensor, "act": nc.scalar, "dve": nc.vector, "pool": nc.gpsimd, "sp": nc.sync}
        self.sem = {k: es.enter_context(nc.semaphore("p_" + k)) for k in ("pe", "act", "dve", "pool")}
        self.cnt = {k: 0 for k in self.sem}
        self.dsem = {q: [es.enter_context(nc.semaphore("d_%s%d" % (q, i))) for i in range(NS)] for q in ("sp", "pool")}
        self.dval = {q: [0] * NS for q in self.dsem}
        self.dn = {q: 0 for q in self.dsem}
        self.waited = {}

    def _wait(self, e, ev):
        key, sem, val = ev
        k = (e, key)
        if self.waited.get(k, 0) >= val:
            return
        self.eng[e].wait_ge(sem, val)
        self.waited[k] = val

    def _deps(self, e, R, W, same):
        for b in R:
            if b.w is not None and (same or b.w[0] != e):
                self._wait(e, b.w)
        for b in W:
            if b.w is not None and (same or b.w[0] != e):
                self._wait(e, b.w)
            for ev in b.r.values():
                if same or ev[0] != e:
                    self._wait(e, ev)

    def op(self, e, fn, R=(), W=(), inc=True):
        self._deps(e, R, W, e != "pe")
        ins = fn()
        if inc:
            self.cnt[e] += 1
            ins.then_inc(self.sem[e], 1)
            ev = (e, self.sem[e], self.cnt[e])
        else:
            ev = (e, self.sem[e], self.cnt[e] + 1)
        for b in R:
            b.r[e] = ev
        for b in W:
            b.w = ev
            b.r = {}
        return ins

    def dma(self, q, out, in_, R=(), W=()):
        i = self.dn[q] % NS
        self.dn[q] += 1
        sem = self.dsem[q][i]
        key = "%s%d" % (q, i)
        if self.dval[q][i] > 0:
            self._wait(q, (key, sem, self.dval[q][i]))
        self._deps(q, R, W, True)
        ins = self.eng[q].dma_start(out=out, in_=in_)
        ins.then_inc(sem, 16)
        self.dval[q][i] += 16
        ev = (key, sem, self.dval[q][i])
        for b in R:
            b.r[key] = ev
        for b in W:
            b.w = ev
            b.r = {}

    def barrier(self):
        evs = [(k, self.sem[k], self.cnt[k]) for k in self.sem if self.cnt[k] > 0]
        for q in self.dsem:
            for i in range(NS):
                if self.dval[q][i] > 0:
                    evs.append(("%s%d" % (q, i), self.dsem[q][i], self.dval[q][i]))
        for e in self.eng:
            for ev in evs:
                self._wait(e, ev)


class Ctx:
    pass


def _vec_layout(v):
    v = np.asarray(v, np.float32).reshape(-1)
    return np.ascontiguousarray(v.reshape(-1, 128).T)


def build(LA, LB, layers, wshapes, nvec, nrow, natab_shape):
    T = LA + LB
    seqs = [(0, LA), (LA, LB)]
    nc = bass.Bass("TRN2", target_bir_lowering=False)
    g = Ctx()
    g.nc = nc

    def din(name, shape, dt=F32):
        return nc.dram_tensor(name, list(shape), dt, kind="ExternalInput").ap()

    def dscr(name, shape, dt=F32):
        return nc.dram_tensor(name, list(shape), dt, kind="Internal").ap()

    xa = din("xa", [LA, D])
    xbm = din("xb", [LB, D])
    ya = nc.dram_tensor("ya", [LA, D], F32, kind="ExternalOutput").ap()
    yb = nc.dram_tensor("yb", [LB, D], F32, kind="ExternalOutput").ap()
    W = {k: din(k, s) for k, s in wshapes.items()}
    vecs_d = din("vecs", [128, nvec])
    rows_d = din("rows", [1, nrow])
    ident_d = din("ident", [128, 128])
    tri_d = din("tri", [4, 128, 128])
    if natab_shape is not None:
        natab_d = din("natab", natab_shape)

    XT = dscr("XT", [D, T])
    XTv = XT.rearrange("(c p) t -> p c t", p=128)

    with contextlib.ExitStack() as es:
        tr = TR(nc, es)
        vecs = Tl(es.enter_context(nc.sbuf_tensor(_un() + "vecs", [128, nvec], F32)))
        nvecs = Tl(es.enter_context(nc.sbuf_tensor(_un() + "nvecs", [128, nvec], F32)))
        ident = Tl(es.enter_context(nc.sbuf_tensor(_un() + "ident", [128, 128], F32)))
        ones_f = Tl(es.enter_context(nc.sbuf_tensor(_un() + "ones_f", [128, 128], F32)))
        ones_b = Tl(es.enter_context(nc.sbuf_tensor(_un() + "ones_b", [128, 128], BF16)))
        ps = [Tl(es.enter_context(nc.psum_tensor(_un() + "ps%d" % i, [128, 512], F32))) for i in range(8)]
        tr.dma("sp", vecs.t[:], vecs_d, W=vecs.b)
        tr.dma("sp", ident.t[:], ident_d, W=ident.b)
        tri = Tl(es.enter_context(nc.sbuf_tensor(_un() + "tri", [128, 4, 128], F32)))
        tr.dma("sp", tri.t[:], tri_d.rearrange("n p q -> p n q"), W=tri.b)
        g.tri = tri
        tr.op("act", lambda: nc.scalar.activation(nvecs.t[:], vecs.t[:], AF.Copy, scale=-1.0), R=vecs.b, W=nvecs.b)
        tr.op("dve", lambda: nc.vector.memset(ones_f.t[:], 1.0), W=ones_f.b)
        tr.op("dve", lambda: nc.vector.memset(ones_b.t[:], 1.0), W=ones_b.b)
        g.tr, g.vecs, g.nvecs, g.ident, g.ones_f, g.ones_b, g.ps = tr, vecs, nvecs, ident, ones_f, ones_b, ps
        psi = [0]

        def nps(lo=0, hi=8):
            i = lo + psi[0] % (hi - lo)
            psi[0] += 1
            return ps[i]

        rr = [0]

        def rot(engs):
            rr[0] += 1
            return engs[rr[0] % len(engs)]

        def copy_op(e, out, in_, R, Wb):
            if e == "act":
                tr.op("act", lambda: nc.scalar.copy(out, in_), R=R, W=Wb)
            elif e == "dve":
                tr.op("dve", lambda: nc.vector.tensor_copy(out, in_), R=R, W=Wb)
            else:
                tr.op("pool", lambda: nc.gpsimd.tensor_copy(out, in_), R=R, W=Wb)

        Wb = {}

        def convert_all(specs):
            with contextlib.ExitStack() as ps_:
                stage = [Tl(ps_.enter_context(nc.sbuf_tensor(_un() + "cst%d" % i, [128, 8192], F32))) for i in range(3)]
                cvt = [Tl(ps_.enter_context(nc.sbuf_tensor(_un() + "ccv%d" % i, [128, 8192], BF16))) for i in range(3)]
                n = 0
                for name, src, KC, NF in specs:
                    PW = min(8192 // KC, NF)
                    npan = NF // PW
                    dst = dscr("wb_" + name, [npan, 128, KC * PW], BF16)
                    Wb[name] = (dst, KC, NF, PW)
                    sv = src.rearrange("(k p) f -> p k f", p=128)
                    for pi in range(npan):
                        s = n % 3
                        n += 1
                        tr.dma("sp", stage[s].t[:, 0:KC * PW].rearrange("p (k f) -> p k f", k=KC), sv[:, :, pi * PW:(pi + 1) * PW],
                               W=stage[s].b)
                        copy_op(("act", "dve", "pool", "dve", "act")[n % 5], cvt[s].t[:, 0:KC * PW], stage[s].t[:, 0:KC * PW], stage[s].b, cvt[s].b)
                        tr.dma("pool", dst[pi], cvt[s].t[:, 0:KC * PW], R=cvt[s].b)
            tr.barrier()

        specs = []
        for li, (kind, j) in enumerate(layers):
            if kind == "rg" and ("rg_in%d" % j) not in [s[0] for s in specs]:
                specs.append(("rg_in%d" % j, W["rg_w_in"][j], 16, 4096))
                specs.append(("rg_out%d" % j, W["rg_w_out"][j], 16, 2048))
            if kind == "ssd" and "ssd_in" not in [s[0] for s in specs]:
                specs.append(("ssd_in", W["ssd_w_in"][j][:, 0:10240], 16, 10240))
                specs.append(("ssd_dt", W["ssd_w_in"][j][:, 10240:10368], 16, 128))
                specs.append(("ssd_out", W["ssd_w_out"][j], 32, 2048))
            if kind == "na" and "na_qkv" not in [s[0] for s in specs]:
                specs.append(("na_qkv", W["na_w_qkv"][j], 16, 6144))
                specs.append(("na_out", W["na_w_out"][j], 16, 2048))
            specs.append(("up%d" % li, W["mlp_w_up"][li], 16, 8192))
            specs.append(("dn%d" % li, W["mlp_w_down"][li], 64, 2048))
        convert_all(specs)

        def linear(wname, wslots, wctr, rhs, rhsR, consume, f_lo=0, f_hi=None, pslo=0, pshi=4):
            dst, KC, NF, PW = Wb[wname]
            cpp = PW // 128
            if f_hi is None:
                f_hi = NF // 128
            for pi in range(f_lo // cpp, (f_hi + cpp - 1) // cpp):
                slot = wslots[wctr[0] % len(wslots)]
                wctr[0] += 1
                tr.dma("sp", slot.t[:, 0:KC * PW], dst[pi], W=slot.b)
                wv = slot.t[:, 0:KC * PW].rearrange("p (k f) -> p k f", k=KC)
                for cc in range(cpp):
                    fc = pi * cpp + cc
                    if fc < f_lo or fc >= f_hi:
                        continue
                    p_ = nps(pslo, pshi)
                    for kc in range(KC):
                        tr.op("pe", lambda: nc.tensor.matmul(p_.t[:], lhsT=wv[:, kc, cc * 128:(cc + 1) * 128], rhs=rhs(kc),
                                                            start=(kc == 0), stop=(kc == KC - 1)),
                              R=slot.b + rhsR(kc), W=p_.b, inc=(kc == KC - 1))
                    consume(fc, p_)

        def transpose_in():
            with contextlib.ExitStack() as ps_:
                xt = [Tl(ps_.enter_context(nc.sbuf_tensor(_un() + "p0x%d" % i, [128, 4, D], F32))) for i in range(2)]
                xo = [Tl(ps_.enter_context(nc.sbuf_tensor(_un() + "p0o%d" % i, [128, DC, NT], F32)), 4) for i in range(2)]
                n = 0
                for (src, (t0, L)) in ((xa, seqs[0]), (xbm, seqs[1])):
                    sv = src.rearrange("(n b p) d -> n p b d", p=128, b=4)
                    for ti in range(L // NT):
                        a, o = xt[n % 2], xo[n % 2]
                        n += 1
                        tr.dma("sp", a.t[:], sv[ti], W=a.b)
                        for c in range(DC):
                            p_ = nps()
                            for b in range(4):
                                tr.op("pe", lambda: nc.tensor.transpose(p_.t[:, b * 128:(b + 1) * 128], a.t[:, b, c * 128:(c + 1) * 128],
                                                                        ident.t[:]),
                                      R=a.b + ident.b, W=p_.b, inc=(b == 3))
                            copy_op(("act", "dve")[c % 2], o.t[:, c, :], p_.t[:], p_.b, [o.b[c // 4]])
                            if c % 4 == 3:
                                q = c // 4
                                tr.dma("pool", XTv[:, q * 4:(q + 1) * 4, t0 + ti * NT:t0 + (ti + 1) * NT], o.t[:, q * 4:(q + 1) * 4, :],
                                       R=[o.b[q]])
            tr.barrier()

        def transpose_out():
            with contextlib.ExitStack() as ps_:
                xi = [Tl(ps_.enter_context(nc.sbuf_tensor(_un() + "pzx%d" % i, [128, DC, NT], F32))) for i in range(2)]
                xo = [Tl(ps_.enter_context(nc.sbuf_tensor(_un() + "pzo%d" % i, [128, 4, D], F32)), 4) for i in range(2)]
                n = 0
                for (dst, (t0, L)) in ((ya, seqs[0]), (yb, seqs[1])):
                    dv = dst.rearrange("(n b p) d -> n p b d", p=128, b=4)
                    for ti in range(L // NT):
                        a, o = xi[n % 2], xo[n % 2]
                        n += 1
                        tr.dma("sp", a.t[:], XTv[:, :, t0 + ti * NT:t0 + (ti + 1) * NT], W=a.b)
                        for b in range(4):
                            for c4 in range(4):
                                p_ = nps()
                                for cc in range(4):
                                    c = c4 * 4 + cc
                                    tr.op("pe", lambda: nc.tensor.transpose(p_.t[:, cc * 128:(cc + 1) * 128], a.t[:, c, b * 128:(b + 1) * 128],
                                                                            ident.t[:]),
                                          R=a.b + ident.b, W=p_.b, inc=(cc == 3))
                                copy_op(("act", "dve")[c4 % 2], o.t[:, b, c4 * 512:(c4 + 1) * 512], p_.t[:], p_.b, [o.b[b]])
                            tr.dma("pool", dv[ti][:, b, :], o.t[:, b, :], R=[o.b[b]])
            tr.barrier()

        def post_pass(li, outname, KCY, Ysc, vc):
            Yv = Ysc.rearrange("(c p) t -> p c t", p=128)
            with contextlib.ExitStack() as ps_:
                hb = Tl(ps_.enter_context(nc.sbuf_tensor(_un() + "hb", [128, 64, NT], BF16)), 64)
                xres = Tl(ps_.enter_context(nc.sbuf_tensor(_un() + "xres", [128, DC, NT], F32)), DC)
                xb = Tl(ps_.enter_context(nc.sbuf_tensor(_un() + "xbf", [128, DC, NT], BF16)), DC)
                wsl = [Tl(ps_.enter_context(nc.sbuf_tensor(_un() + "wsl%d" % i, [128, 8192], BF16))) for i in range(3)]
                tmp = [Tl(ps_.enter_context(nc.sbuf_tensor(_un() + "ptmp%d" % i, [128, NT], F32))) for i in range(3)]
                tmpb = [Tl(ps_.enter_context(nc.sbuf_tensor(_un() + "ptmpb%d" % i, [128, NT], BF16))) for i in range(4)]
                st = [Tl(ps_.enter_context(nc.sbuf_tensor(_un() + "pst%d" % i, [128, NT], F32))) for i in range(3)]
                wctr = [0]
                tctr = [0]

                def ln_stats_chunk(c):
                    s1, s2 = ps[4], ps[5]
                    t_ = tmpb[tctr[0] % 4]
                    tctr[0] += 1
                    tr.op("act", lambda: nc.scalar.activation(t_.t[:], xres.t[:, c, :], AF.Square), R=[xres.b[c]], W=t_.b)
                    u_ = tmpb[tctr[0] % 4]
                    tctr[0] += 1
                    tr.op("act", lambda: nc.scalar.copy(u_.t[:], xres.t[:, c, :]), R=[xres.b[c]], W=u_.b)
                    tr.op("pe", lambda: nc.tensor.matmul(s1.t[:], lhsT=ones_b.t[:], rhs=u_.t[:], start=(c == 0), stop=(c == DC - 1)),
                          R=ones_b.b + u_.b, W=s1.b)
                    tr.op("pe", lambda: nc.tensor.matmul(s2.t[:], lhsT=ones_b.t[:], rhs=t_.t[:], start=(c == 0), stop=(c == DC - 1)),
                          R=ones_b.b + t_.b, W=s2.b)

                def layer_norm(gcol, bcol, want_bf):
                    s1, s2 = ps[4], ps[5]
                    m, v, r = st
                    tr.op("act", lambda: nc.scalar.activation(m.t[:], s1.t[:], AF.Copy, scale=1.0 / D), R=s1.b, W=m.b)
                    tr.op("dve", lambda: nc.vector.tensor_tensor(v.t[:], m.t[:], m.t[:], ALU.mult), R=m.b, W=v.b)
                    tr.op("dve", lambda: nc.vector.scalar_tensor_tensor(v.t[:], s2.t[:], 1.0 / D, v.t[:], ALU.mult, ALU.subtract),
                          R=s2.b + v.b, W=v.b)
                    tr.op("dve", lambda: nc.vector.tensor_scalar(v.t[:], v.t[:], LN_EPS, None, ALU.add), R=v.b, W=v.b)
                    tr.op("act", lambda: nc.scalar.activation(r.t[:], v.t[:], AF.Ln), R=v.b, W=r.b)
                    tr.op("act", lambda: nc.scalar.activation(r.t[:], r.t[:], AF.Exp, scale=-0.5), R=r.b, W=r.b)
                    for c in range(DC):
                        xc = xres.t[:, c, :]
                        tr.op("dve", lambda: nc.vector.tensor_tensor(xc, xc, m.t[:], ALU.subtract), R=[xres.b[c]] + m.b, W=[xres.b[c]])
                        tr.op("dve", lambda: nc.vector.tensor_tensor(xc, xc, r.t[:], ALU.mult), R=[xres.b[c]] + r.b, W=[xres.b[c]])
                        if want_bf:
                            tr.op("act", lambda: nc.scalar.activation(xb.t[:, c, :], xc, AF.Identity, bias=vecs.t[:, bcol + c:bcol + c + 1],
                                                                      scale=vecs.t[:, gcol + c:gcol + c + 1]),
                                  R=[xres.b[c]] + vecs.b, W=[xb.b[c]])
                        else:
                            tr.op("act", lambda: nc.scalar.activation(xc, xc, AF.Identity, bias=vecs.t[:, bcol + c:bcol + c + 1],
                                                                      scale=vecs.t[:, gcol + c:gcol + c + 1]),
                                  R=[xres.b[c]] + vecs.b, W=[xres.b[c]])
                    if want_bf:
                        for c in range(DC):
                            xc = xres.t[:, c, :]
                            tr.op("act", lambda: nc.scalar.activation(xc, xc, AF.Identity, bias=vecs.t[:, bcol + c:bcol + c + 1],
                                                                      scale=vecs.t[:, gcol + c:gcol + c + 1]),
                                  R=[xres.b[c]] + vecs.b, W=[xres.b[c]])

                def resid(fc, p_):
                    xc = xres.t[:, fc, :]
                    tr.op("dve", lambda: nc.vector.scalar_tensor_tensor(xc, xc, ALPHA, p_.t[:], ALU.mult, ALU.add),
                          R=[xres.b[fc]] + p_.b, W=[xres.b[fc]])
                    if fc >= 1:
                        ln_stats_chunk(fc - 1)
                    if fc == DC - 1:
                        ln_stats_chunk(fc)

                def up_consume(fc, p_):
                    t_ = tmp[tctr[0] % 3]
                    tctr[0] += 1
                    tr.op("act", lambda: nc.scalar.activation(t_.t[:], p_.t[:], AF.Relu), R=p_.b, W=t_.b)
                    e = ("dve", "pool")[fc % 2]
                    tr.op(e, lambda: tr.eng[e].tensor_tensor(hb.t[:, fc, :], t_.t[:], t_.t[:], ALU.mult), R=t_.b, W=[hb.b[fc]])

                for (t0, L) in seqs:
                    for ti in range(L // NT):
                        c0 = t0 + ti * NT
                        for q in range(KCY // 4):
                            tr.dma("sp", hb.t[:, q * 4:(q + 1) * 4, :], Yv[:, q * 4:(q + 1) * 4, c0:c0 + NT], W=hb.b[q * 4:(q + 1) * 4])
                        for q in range(4):
                            tr.dma("sp", xres.t[:, q * 4:(q + 1) * 4, :], XTv[:, q * 4:(q + 1) * 4, c0:c0 + NT], W=xres.b[q * 4:(q + 1) * 4])
                        linear(outname, wsl, wctr, lambda kc: hb.t[:, kc, :], lambda kc: [hb.b[kc]], resid)
                        layer_norm(vc["ln1g"], vc["ln1b"], True)
                        linear("up%d" % li, wsl, wctr, lambda kc: xb.t[:, kc, :], lambda kc: [xb.b[kc]], up_consume)
                        linear("dn%d" % li, wsl, wctr, lambda kc: hb.t[:, kc, :], lambda kc: [hb.b[kc]], resid)
                        layer_norm(vc["ln2g"], vc["ln2b"], False)
                        for q in range(4):
                            tr.dma("pool", XTv[:, q * 4:(q + 1) * 4, c0:c0 + NT], xres.t[:, q * 4:(q + 1) * 4, :], R=xres.b[q * 4:(q + 1) * 4])
            tr.barrier()

        g.linear, g.nps, g.copy_op, g.Wb, g.dscr, g.XTv, g.seqs, g.T, g.W = linear, nps, copy_op, Wb, dscr, XTv, seqs, T, W
        g.rows_d, g.tri_d = rows_d, tri_d
        if natab_shape is not None:
            g.natab_d = natab_d

        transpose_in()
        ysc = {}
        for li, (kind, j) in enumerate(layers):
            vc = g_vcols[li]
            if kind == "rg":
                if "rg" not in ysc:
                    ysc["rg"] = dscr("Y_rg", [2048, T], BF16)
                rg_mixer(g, j, vc, ysc["rg"])
                post_pass(li, "rg_out%d" % j, 16, ysc["rg"], vc)
            elif kind == "ssd":
                ysc["ssd"] = dscr("Y_ssd", [4096, T], BF16)
                ssd_mixer(g, j, vc, ysc["ssd"])
                post_pass(li, "ssd_out", 32, ysc["ssd"], vc)
            elif kind == "na":
                ysc["na"] = dscr("Y_na", [2048, T], BF16)
                na_mixer(g, j, vc, ysc["na"])
                post_pass(li, "na_out", 16, ysc["na"], vc)
        transpose_out()
    return nc


g_vcols = []


def rg_mixer(g, j, vc, Ysc):
    nc, tr, vecs, nvecs, ps = g.nc, g.tr, g.vecs, g.nvecs, g.ps
    T = g.T
    XTv = g.XTv
    Gs = g.dscr("rg_G%d" % j, [D, T])
    Us = g.dscr("rg_U%d" % j, [D, T + 8])
    UCs = g.dscr("rg_UC%d" % j, [D, T])
    HFs = g.dscr("rg_HF%d" % j, [D, T])
    Gv = Gs.rearrange("(c p) t -> p c t", p=128)
    Uv = Us.rearrange("(c p) t -> p c t", p=128)
    UCv = UCs.rearrange("(c p) t -> p c t", p=128)
    HFv = HFs.rearrange("(c p) t -> p c t", p=128)
    Yv = Ysc.rearrange("(c p) t -> p c t", p=128)
    upad = [g.seqs[0][0], g.seqs[1][0]]

    with contextlib.ExitStack() as ps_:
        xres = Tl(ps_.enter_context(nc.sbuf_tensor(_un() + "ra_x", [128, DC, NT], F32)), 4)
        xb = Tl(ps_.enter_context(nc.sbuf_tensor(_un() + "ra_xb", [128, DC, NT], BF16)), 4)
        gb = Tl(ps_.enter_context(nc.sbuf_tensor(_un() + "ra_g", [128, DC, NT], F32)), 4)
        ub = Tl(ps_.enter_context(nc.sbuf_tensor(_un() + "ra_u", [128, DC, NT], F32)), 4)
        wsl = [Tl(ps_.enter_context(nc.sbuf_tensor(_un() + "ra_w%d" % i, [128, 8192], BF16))) for i in range(3)]
        t1 = [Tl(ps_.enter_context(nc.sbuf_tensor(_un() + "ra_t1%d" % i, [128, NT], F32))) for i in range(2)]
        t2 = [Tl(ps_.enter_context(nc.sbuf_tensor(_un() + "ra_t2%d" % i, [128, NT], F32))) for i in range(2)]
        zt = Tl(ps_.enter_context(nc.sbuf_tensor(_un() + "ra_z", [128, DC, 4], F32)))
        wctr = [0]
        k = [0]

        def consume(fc, p_):
            if fc < DC:
                a, b = t1[k[0] % 2], t2[k[0] % 2]
                k[0] += 1
                tr.op("act", lambda: nc.scalar.copy(a.t[:], p_.t[:]), R=p_.b, W=a.b)
                tr.op("act", lambda: nc.scalar.activation(b.t[:], p_.t[:], AF.Square), R=p_.b, W=b.b)
                tr.op("dve", lambda: nc.vector.tensor_scalar(b.t[:], b.t[:], 0.044715, 1.0, ALU.mult, ALU.add), R=b.b, W=b.b)
                tr.op("dve", lambda: nc.vector.tensor_tensor(b.t[:], b.t[:], a.t[:], ALU.mult), R=b.b + a.b, W=b.b)
                tr.op("act", lambda: nc.scalar.activation(b.t[:], b.t[:], AF.Exp, scale=-GELU_K), R=b.b, W=b.b)
                tr.op("dve", lambda: nc.vector.tensor_scalar(b.t[:], b.t[:], 1.0, None, ALU.add), R=b.b, W=b.b)
                tr.op("dve", lambda: nc.vector.reciprocal(b.t[:], b.t[:]), R=b.b, W=b.b)
                tr.op("dve", lambda: nc.vector.tensor_tensor(gb.t[:, fc, :], b.t[:], a.t[:], ALU.mult), R=b.b + a.b, W=[gb.b[fc // 4]])
            else:
                c = fc - DC
                g.copy_op(("act", "dve")[c % 2], ub.t[:, c, :], p_.t[:], p_.b, [ub.b[c // 4]])

        for si, (t0, L) in enumerate(g.seqs):
            for ti in range(L // NT):
                c0 = t0 + ti * NT
                for q in range(4):
                    tr.dma("sp", xres.t[:, q * 4:(q + 1) * 4, :], XTv[:, q * 4:(q + 1) * 4, c0:c0 + NT], W=[xres.b[q]])
                    g.copy_op(("act", "dve")[q % 2], xb.t[:, q * 4:(q + 1) * 4, :], xres.t[:, q * 4:(q + 1) * 4, :], [xres.b[q]], [xb.b[q]])
                g.linear("rg_in%d" % j, wsl, wctr, lambda kc: xb.t[:, kc, :], lambda kc: [xb.b[kc // 4]], consume)
                uc0 = upad[si] + ti * NT
                for q in range(4):
                    tr.dma("pool", Gv[:, q * 4:(q + 1) * 4, c0:c0 + NT], gb.t[:, q * 4:(q + 1) * 4, :], R=[gb.b[q]])
                    tr.dma("pool", Uv[:, q * 4:(q + 1) * 4, uc0:uc0 + NT], ub.t[:, q * 4:(q + 1) * 4, :], R=[ub.b[q]])
    tr.barrier()

    cw, cb = vc["rg_cw"], vc["rg_cb"]
    for d in range(2):
        with contextlib.ExitStack() as ps_:
            gw = Tl(ps_.enter_context(nc.sbuf_tensor(_un() + "rb_gw", [128, 2, 8, 2, 256], BF16)))
            uc = Tl(ps_.enter_context(nc.sbuf_tensor(_un() + "rb_uc", [128, DC, NT], F32)), 4)
            ucb = Tl(ps_.enter_context(nc.sbuf_tensor(_un() + "rb_ucb", [128, DC, NT], BF16)), 4)
            hh = Tl(ps_.enter_context(nc.sbuf_tensor(_un() + "rb_h", [128, DC, NT], F32)), 4)
            ab = Tl(ps_.enter_context(nc.sbuf_tensor(_un() + "rb_a", [128, 4, NT], F32)), 4)
            carry = Tl(ps_.enter_context(nc.sbuf_tensor(_un() + "rb_carry", [128, DC], F32)), DC)
            cl = Tl(ps_.enter_context(nc.sbuf_tensor(_un() + "rb_cl", [128, DC], F32)))
            clt = [Tl(ps_.enter_context(nc.sbuf_tensor(_un() + "rb_clt%d" % i, [128, DC], F32))) for i in range(3)]
            ta = [Tl(ps_.enter_context(nc.sbuf_tensor(_un() + "rb_ta%d" % i, [128, NT], F32))) for i in range(2)]
            tb = [Tl(ps_.enter_context(nc.sbuf_tensor(_un() + "rb_tb%d" % i, [128, NT], F32))) for i in range(2)]
            tcx = [Tl(ps_.enter_context(nc.sbuf_tensor(_un() + "rb_tc%d" % i, [128, NT], F32))) for i in range(2)]
            if d == 0:
                halo = Tl(ps_.enter_context(nc.sbuf_tensor(_un() + "rb_halo", [128, DC, NT + 4], F32)), 4)
            else:
                hf = Tl(ps_.enter_context(nc.sbuf_tensor(_un() + "rb_hf", [128, DC, NT], F32)), 4)
                gt = Tl(ps_.enter_context(nc.sbuf_tensor(_un() + "rb_gt", [128, DC, NT], F32)), 4)
                yo = Tl(ps_.enter_context(nc.sbuf_tensor(_un() + "rb_yo", [128, DC, NT], BF16)), 4)
            for gi, nm in enumerate(("rg_w_a", "rg_w_x")):
                src = g.W[nm][j][d].rearrange("n (kc p) o -> p n kc o", p=128)
                tr.dma("pool", gw.t[:, gi], src, W=gw.b)
            lam = vecs.t[:, vc["rg_lam"] + d * DC: vc["rg_lam"] + (d + 1) * DC]
            e_, u_, l_ = clt
            tr.op("act", lambda: nc.scalar.activation(e_.t[:], lam, AF.Exp, scale=-1.0), R=vecs.b, W=e_.b)
            tr.op("dve", lambda: nc.vector.tensor_scalar(u_.t[:], e_.t[:], 1.0, None, ALU.add), R=e_.b, W=u_.b)
            tr.op("act", lambda: nc.scalar.activation(l_.t[:], u_.t[:], AF.Ln), R=u_.b, W=l_.b)
            tr.op("dve", lambda: nc.vector.tensor_scalar(u_.t[:], u_.t[:], -1.0, None, ALU.add), R=u_.b, W=u_.b)
            tr.op("dve", lambda: nc.vector.reciprocal(u_.t[:], u_.t[:]), R=u_.b, W=u_.b)
            tr.op("dve", lambda: nc.vector.tensor_tensor(u_.t[:], u_.t[:], e_.t[:], ALU.mult), R=u_.b + e_.b, W=u_.b)
            tr.op("dve", lambda: nc.vector.tensor_tensor(l_.t[:], l_.t[:], u_.t[:], ALU.mult), R=u_.b + l_.b, W=l_.b)
            tr.op("dve", lambda: nc.vector.tensor_scalar(cl.t[:], l_.t[:], -8.0, None, ALU.mult), R=l_.b, W=cl.b)
            nba = vc["rg_ba"] + d * DC
            nbx = vc["rg_bx"] + d * DC
            k = [0]
            for si, (t0, L) in enumerate(g.seqs):
                tr.op("dve", lambda: nc.vector.memset(carry.t[:], 0.0), W=carry.b)
                ntile = L // NT
                order = range(ntile) if d == 0 else range(ntile - 1, -1, -1)
                for ti in order:
                    c0 = t0 + ti * NT
                    if d == 0:
                        lo = max(ti * NT - 1, 0)
                        hi = min(ti * NT + NT + 2, L)
                        o0 = lo - (ti * NT - 1)
                        if ti == 0:
                            tr.op("pool", lambda: nc.gpsimd.memset(halo.t[:, :, 0:1], 0.0), W=halo.b)
                        if ti == ntile - 1:
                            tr.op("pool", lambda: nc.gpsimd.memset(halo.t[:, :, NT + 1:NT + 3], 0.0), W=halo.b)
                        for q in range(4):
                            tr.dma("sp", halo.t[:, q * 4:(q + 1) * 4, o0:o0 + hi - lo], Uv[:, q * 4:(q + 1) * 4, t0 + lo:t0 + hi], W=[halo.b[q]])
                        for c in range(DC):
                            o = uc.t[:, c, :]
                            tr.op("dve", lambda: nc.vector.tensor_scalar(o, halo.t[:, c, 0:NT], vecs.t[:, cw + c:cw + c + 1], vecs.t[:, cb + c:cb + c + 1],
                                                                         ALU.mult, ALU.add),
                                  R=[halo.b[c // 4]] + vecs.b, W=[uc.b[c // 4]])
                            for kk in range(1, 4):
                                e = "dve"
                                tr.op(e, lambda: tr.eng[e].scalar_tensor_tensor(o, halo.t[:, c, kk:kk + NT],
                                                                                 vecs.t[:, cw + kk * DC + c:cw + kk * DC + c + 1], o, ALU.mult, ALU.add),
                                      R=[halo.b[c // 4], uc.b[c // 4]] + vecs.b, W=[uc.b[c // 4]])
                        for q in range(4):
                            tr.dma("pool", UCv[:, q * 4:(q + 1) * 4, c0:c0 + NT], uc.t[:, q * 4:(q + 1) * 4, :], R=[uc.b[q]])
                    else:
                        for q in range(4):
                            tr.dma("sp", uc.t[:, q * 4:(q + 1) * 4, :], UCv[:, q * 4:(q + 1) * 4, c0:c0 + NT], W=[uc.b[q]])
                            tr.dma("sp", hf.t[:, q * 4:(q + 1) * 4, :], HFv[:, q * 4:(q + 1) * 4, c0:c0 + NT], W=[hf.b[q]])
                            tr.dma("sp", gt.t[:, q * 4:(q + 1) * 4, :], Gv[:, q * 4:(q + 1) * 4, c0:c0 + NT], W=[gt.b[q]])
                    for q in range(4):
                        g.copy_op(("act", "dve")[q % 2], ucb.t[:, q * 4:(q + 1) * 4, :], uc.t[:, q * 4:(q + 1) * 4, :], [uc.b[q]], [ucb.b[q]])
                    for c in range(DC):
                        n, jj = c // 2, c % 2
                        pr, pi_ = g.nps(0, 6), g.nps(0, 6)
                        for gi, p_ in ((0, pr), (1, pi_)):
                            for kc in range(2):
                                tr.op("pe", lambda: nc.tensor.matmul(p_.t[:], lhsT=gw.t[:, gi, n, kc, jj * 128:(jj + 1) * 128], rhs=ucb.t[:, 2 * n + kc, :],
                                                                    start=(kc == 0), stop=(kc == 1)),
                                      R=gw.b + [ucb.b[(2 * n + kc) // 4]], W=p_.b, inc=(kc == 1))
                        a_, b_, c_ = ta[k[0] % 2], tb[k[0] % 2], tcx[k[0] % 2]
                        k[0] += 1
                        av = ab.t[:, c % 4, :]
                        tr.op("act", lambda: nc.scalar.activation(a_.t[:], pr.t[:], AF.Exp, bias=nvecs.t[:, nba + c:nba + c + 1], scale=-1.0),
                              R=pr.b + nvecs.b, W=a_.b)
                        tr.op("act", lambda: nc.scalar.activation(b_.t[:], pi_.t[:], AF.Exp, bias=nvecs.t[:, nbx + c:nbx + c + 1], scale=-1.0),
                              R=pi_.b + nvecs.b, W=b_.b)
                        tr.op("act", lambda: nc.scalar.activation(a_.t[:], a_.t[:], AF.Ln, bias=1.0, scale=1.0), R=a_.b, W=a_.b)
                        tr.op("act", lambda: nc.scalar.activation(b_.t[:], b_.t[:], AF.Ln, bias=1.0, scale=1.0), R=b_.b, W=b_.b)
                        tr.op("act", lambda: nc.scalar.activation(a_.t[:], a_.t[:], AF.Exp, scale=-1.0), R=a_.b, W=a_.b)
                        tr.op("act", lambda: nc.scalar.activation(b_.t[:], b_.t[:], AF.Exp, scale=-1.0), R=b_.b, W=b_.b)
                        tr.op("act", lambda: nc.scalar.activation(av, a_.t[:], AF.Exp, scale=cl.t[:, c:c + 1]), R=a_.b + cl.b, W=[ab.b[c % 4]])
                        tr.op("dve", lambda: nc.vector.tensor_tensor(c_.t[:], av, av, ALU.mult), R=[ab.b[c % 4]], W=c_.b)
                        tr.op("act", lambda: nc.scalar.activation(c_.t[:], c_.t[:], AF.Ln, bias=1.0, scale=-1.0), R=c_.b, W=c_.b)
                        tr.op("act", lambda: nc.scalar.activation(c_.t[:], c_.t[:], AF.Exp, scale=0.5), R=c_.b, W=c_.b)
                        tr.op("dve", lambda: nc.vector.tensor_tensor(b_.t[:], b_.t[:], uc.t[:, c, :], ALU.mult), R=b_.b + [uc.b[c // 4]], W=b_.b)
                        tr.op("dve", lambda: nc.vector.tensor_tensor(b_.t[:], b_.t[:], c_.t[:], ALU.mult), R=b_.b + c_.b, W=b_.b)
                        if d == 0:
                            tr.op("dve", lambda: nc.vector.tensor_tensor_scan(hh.t[:, c, :], av, b_.t[:], carry.t[:, c:c + 1], ALU.mult, ALU.add),
                                  R=[ab.b[c % 4], carry.b[c]] + b_.b, W=[hh.b[c // 4]])
                            tr.op("pool", lambda: nc.gpsimd.tensor_copy(carry.t[:, c:c + 1], hh.t[:, c, NT - 1:NT]), R=[hh.b[c // 4]], W=[carry.b[c]])
                        else:
                            tr.op("dve", lambda: nc.vector.tensor_tensor_scan(hh.t[:, c, ::-1], ab.t[:, c % 4, ::-1], b_.t[:, ::-1], carry.t[:, c:c + 1],
                                                                              ALU.mult, ALU.add),
                                  R=[ab.b[c % 4], carry.b[c]] + b_.b, W=[hh.b[c // 4]])
                            tr.op("pool", lambda: nc.gpsimd.tensor_copy(carry.t[:, c:c + 1], hh.t[:, c, 0:1]), R=[hh.b[c // 4]], W=[carry.b[c]])
                            tr.op("pool", lambda: nc.gpsimd.tensor_tensor(hh.t[:, c, :], hh.t[:, c, :], hf.t[:, c, :], ALU.add),
                                  R=[hh.b[c // 4], hf.b[c // 4]], W=[hh.b[c // 4]])
                            tr.op("dve", lambda: nc.vector.tensor_tensor(yo.t[:, c, :], hh.t[:, c, :], gt.t[:, c, :], ALU.mult),
                                  R=[hh.b[c // 4], gt.b[c // 4]], W=[yo.b[c // 4]])
                    for q in range(4):
                        if d == 0:
                            tr.dma("pool", HFv[:, q * 4:(q + 1) * 4, c0:c0 + NT], hh.t[:, q * 4:(q + 1) * 4, :], R=[hh.b[q]])
                        else:
                            tr.dma("pool", Yv[:, q * 4:(q + 1) * 4, c0:c0 + NT], yo.t[:, q * 4:(q + 1) * 4, :], R=[yo.b[q]])
        tr.barrier()


def ssd_mixer(g, j, vc, Ysc):
    nc, tr, vecs, nvecs, ps, tri, ident, ones_f = g.nc, g.tr, g.vecs, g.nvecs, g.ps, g.tri, g.ident, g.ones_f
    T = g.T
    XTv = g.XTv
    XBC = g.dscr("ssd_XBC", [6144, T])
    Zs = g.dscr("ssd_Z", [4096, T])
    DTA = g.dscr("ssd_DTA", [256, T])
    YA = g.dscr("ssd_YA", [T, 4096])
    XBCv = XBC.rearrange("(c p) t -> p c t", p=128)
    Zv = Zs.rearrange("(c p) t -> p c t", p=128)
    DTAv = DTA.rearrange("(c p) t -> p c t", p=128)
    Yv = Ysc.rearrange("(c p) t -> p c t", p=128)
    V = lambda fn, R, W: tr.op("dve", fn, R=R, W=W)
    A = lambda fn, R, W: tr.op("act", fn, R=R, W=W)
    P = lambda fn, R, W: tr.op("pool", fn, R=R, W=W)
    cw, cbv = vc["ssd_cw"], vc["ssd_cb"]
    CH = 128
    CVs = g.dscr("ssd_CV", [6144, T])
    CVv = CVs.rearrange("(c p) t -> p c t", p=128)
    with contextlib.ExitStack() as ps_:
        xres = Tl(ps_.enter_context(nc.sbuf_tensor(_un() + "s1_x", [128, DC, NT], F32)), 4)
        xb = Tl(ps_.enter_context(nc.sbuf_tensor(_un() + "s1_xb", [128, DC, NT], BF16)), 4)
        stg = [Tl(ps_.enter_context(nc.sbuf_tensor(_un() + "s1_st%d" % i, [128, 4, NT], F32))) for i in range(4)]
        dta = Tl(ps_.enter_context(nc.sbuf_tensor(_un() + "s1_dta", [128, 2, NT], F32)))
        avec = Tl(ps_.enter_context(nc.sbuf_tensor(_un() + "s1_av", [128, 1], F32)))
        wsl = [Tl(ps_.enter_context(nc.sbuf_tensor(_un() + "s1_w%d" % i, [128, 8192], BF16))) for i in range(3)]
        t1 = [Tl(ps_.enter_context(nc.sbuf_tensor(_un() + "s1_t1%d" % i, [128, NT], F32))) for i in range(2)]
        t2 = [Tl(ps_.enter_context(nc.sbuf_tensor(_un() + "s1_t2%d" % i, [128, NT], F32))) for i in range(2)]
        wctr = [0]
        k = [0]
        sg = [0]
        hlc = Tl(ps_.enter_context(nc.sbuf_tensor(_un() + "s1_hl", [128, 8, NT + 4], F32)))
        coc = Tl(ps_.enter_context(nc.sbuf_tensor(_un() + "s1_co", [128, 8, NT], F32)))
        xbuf = {}
        pending = []

        def conv_steps(si, t0, L, ti, gq):
            ntile = L // NT
            lo = max(ti * NT - 1, 0)
            hi = min(ti * NT + NT + 2, L)
            o0 = lo - (ti * NT - 1)
            h_, o_ = hlc, coc

            def load():
                if ti == 0:
                    V(lambda: nc.vector.memset(h_.t[:, :, 0:1], 0.0), [], h_.b)
                if ti == ntile - 1:
                    V(lambda: nc.vector.memset(h_.t[:, :, NT + 1:NT + 3], 0.0), [], h_.b)
                rb = [xbuf[(si, tt, gg)] for tt in (ti - 1, ti, ti + 1) if 0 <= tt < ntile for gg in (2 * gq, 2 * gq + 1)]
                tr.dma("sp", h_.t[:, :, o0:o0 + hi - lo], XBCv[:, gq * 8:(gq + 1) * 8, t0 + lo:t0 + hi], R=rb, W=h_.b)

            def taps(c2):
                for cc in (2 * c2, 2 * c2 + 1):
                    c = gq * 8 + cc
                    o = o_.t[:, cc, :]
                    A(lambda: nc.scalar.activation(o, h_.t[:, cc, 0:NT], AF.Identity, bias=vecs.t[:, cbv + c:cbv + c + 1],
                                                   scale=vecs.t[:, cw + c:cw + c + 1]), h_.b + vecs.b, o_.b)
                    for kk in range(1, 4):
                        V(lambda: nc.vector.scalar_tensor_tensor(o, h_.t[:, cc, kk:kk + NT], vecs.t[:, cw + kk * 48 + c:cw + kk * 48 + c + 1], o,
                                                                  ALU.mult, ALU.add), h_.b + vecs.b + o_.b, o_.b)

            def fin():
                A(lambda: nc.scalar.activation(o_.t[:], o_.t[:], AF.Silu), o_.b, o_.b)
                tr.dma("pool", CVv[:, gq * 8:(gq + 1) * 8, t0 + ti * NT:t0 + (ti + 1) * NT], o_.t[:], R=o_.b)

            return [lambda: (load(), taps(0)), lambda: taps(1), lambda: taps(2), lambda: (taps(3), fin())]

        A(lambda: nc.scalar.activation(avec.t[:], vecs.t[:, vc["ssd_alog"]:vc["ssd_alog"] + 1], AF.Exp), vecs.b, avec.b)
        V(lambda: nc.vector.tensor_scalar(avec.t[:], avec.t[:], -1.0, None, ALU.mult), avec.b, avec.b)
        cur = {}

        def consume(fc, p_):
            q = fc // 4
            if fc % 4 == 0:
                cur["s"] = stg[sg[0] % 4]
                sg[0] += 1
            st_ = cur["s"]
            if fc < 32:
                A(lambda: nc.scalar.activation(st_.t[:, fc % 4, :], p_.t[:], AF.Silu), p_.b, st_.b)
            else:
                g.copy_op("act", st_.t[:, fc % 4, :], p_.t[:], p_.b, st_.b)
            if fc % 4 == 3:
                c0 = cur["c0"]
                if fc < 32:
                    tr.dma("pool", Zv[:, q * 4:(q + 1) * 4, c0:c0 + NT], st_.t[:], R=st_.b)
                else:
                    xb_ = Buf()
                    xbuf[(cur["si"], cur["ti"], q - 8)] = xb_
                    tr.dma("pool", XBCv[:, (q - 8) * 4:(q - 7) * 4, c0:c0 + NT], st_.t[:], R=st_.b, W=[xb_])
            if fc % 3 == 2 and pending:
                pending.pop(0)()

        def consume_dt(fc, p_):
            a = t1[k[0] % 2]
            k[0] += 1
            A(lambda: nc.scalar.activation(a.t[:], p_.t[:], AF.Exp, bias=vecs.t[:, vc["ssd_dtb"]:vc["ssd_dtb"] + 1], scale=1.0), p_.b + vecs.b, a.b)
            A(lambda: nc.scalar.activation(dta.t[:, 0, :], a.t[:], AF.Ln, bias=1.0, scale=1.0), a.b, dta.b)
            V(lambda: nc.vector.tensor_scalar(dta.t[:, 1, :], dta.t[:, 0, :], avec.t[:, 0:1], None, ALU.mult), dta.b + avec.b, dta.b)
            c0 = cur["c0"]
            tr.dma("pool", DTAv[:, :, c0:c0 + NT], dta.t[:], R=dta.b)

        for si, (t0, L) in enumerate(g.seqs):
            for ti in range(L // NT):
                c0 = t0 + ti * NT
                cur["c0"] = c0
                cur["si"], cur["ti"] = si, ti
                for q in range(4):
                    tr.dma("sp", xres.t[:, q * 4:(q + 1) * 4, :], XTv[:, q * 4:(q + 1) * 4, c0:c0 + NT], W=[xres.b[q]])
                    g.copy_op(("act", "dve")[q % 2], xb.t[:, q * 4:(q + 1) * 4, :], xres.t[:, q * 4:(q + 1) * 4, :], [xres.b[q]], [xb.b[q]])
                g.linear("ssd_in", wsl, wctr, lambda kc: xb.t[:, kc, :], lambda kc: [xb.b[kc // 4]], consume)
                g.linear("ssd_dt", wsl, wctr, lambda kc: xb.t[:, kc, :], lambda kc: [xb.b[kc // 4]], consume_dt)
                while pending:
                    pending.pop(0)()
                if ti >= 1:
                    for gq in range(6):
                        pending.extend(conv_steps(si, t0, L, ti - 1, gq))
                if ti == L // NT - 1:
                    while pending:
                        pending.pop(0)()
                    for gq in range(6):
                        for st_fn in conv_steps(si, t0, L, ti, gq):
                            st_fn()
    tr.barrier()

    for dr in range(2):
        with contextlib.ExitStack() as ps_:
            sb_ = lambda nm, shp, dt=F32, n=1: Tl(ps_.enter_context(nc.sbuf_tensor(_un() + "s2_" + nm, shp, dt)), n)
            cv = sb_("cv", [128, 48, CH], F32, 6)
            cvb = sb_("cvb", [128, 16, CH], BF16)
            xtok = sb_("xtok", [128, 4096], BF16, 8)
            btok = sb_("btok", [128, 1024], BF16, 2)
            xdt = sb_("xdt", [128, 4096], BF16)
            xdd = xtok
            dain = sb_("dain", [128, 2, CH])
            datok = sb_("datok", [128, 256])
            dcs = sb_("dcs", [128, 192])
            cbm = [sb_("cbm%d" % i, [128, CH]) for i in range(2)]
            srh = [sb_("srh%d" % i, [128, 8, CH]) for i in range(2)]
            ee = [sb_("ee%d" % i, [128, 8, CH]) for i in range(2)]
            mt = [sb_("mt%d" % i, [128, 8, CH], BF16) for i in range(2)]
            h32 = sb_("h32", [128, 8, 512], F32, 8)
            hbf = sb_("hbf", [128, 8, 512], BF16, 8)
            yacc = sb_("yacc", [128, 4096], F32, 8)
            if dr == 1:
                yprev = sb_("yprev", [128, 4096])
                yF = sb_("yF", [128, 32, CH], F32, 8)
                zt = sb_("zt", [128, 32, CH])
                rs_ = sb_("rs", [128, 8, CH])
                yout = sb_("yout", [128, 32, 2 * CH], BF16)
            SLm = tri.t[:, 0 if dr == 0 else 2, :]
            LIm = tri.t[:, 1 if dr == 0 else 3, :]
            pseg = [ps[3], ps[4]]
            py, po, pst = ps[5], ps[6], ps[7]
            gn = [0]
            for si, (t0, L) in enumerate(g.seqs):
                nch = L // CH
                V(lambda: nc.vector.memset(h32.t[:], 0.0), [], h32.b)
                V(lambda: nc.vector.memset(hbf.t[:], 0.0), [], hbf.b)
                order = range(nch) if dr == 0 else range(nch - 1, -1, -1)
                for ci in order:
                    c0 = t0 + ci * CH
                    for q in range(6):
                        tr.dma("sp", cv.t[:, q * 8:(q + 1) * 8, :], CVv[:, q * 8:(q + 1) * 8, c0:c0 + CH], W=[cv.b[q]])
                    A(lambda: nc.scalar.copy(cvb.t[:], cv.t[:, 32:48, :]), cv.b[4:6], cvb.b)
                    for q in range(10):
                        p_ = g.nps(0, 3)
                        for cc in range(4):
                            c = q * 4 + cc
                            tr.op("pe", lambda: nc.tensor.transpose(p_.t[:, cc * 128:(cc + 1) * 128], cv.t[:, c, :], ident.t[:]),
                                  R=cv.b + ident.b, W=p_.b, inc=(cc == 3))
                        if q < 8:
                            g.copy_op(("act", "dve")[q % 2], xtok.t[:, q * 512:(q + 1) * 512], p_.t[:], p_.b, [xtok.b[q]])
                        else:
                            g.copy_op(("act", "dve")[q % 2], btok.t[:, (q - 8) * 512:(q - 7) * 512], p_.t[:], p_.b, [btok.b[q - 8]])
                    tr.dma("sp", dain.t[:], DTAv[:, :, c0:c0 + CH], W=dain.b)
                    p_ = g.nps(0, 3)
                    for cc in range(2):
                        tr.op("pe", lambda: nc.tensor.transpose(p_.t[:, cc * 128:(cc + 1) * 128], dain.t[:, cc, :], ident.t[:]),
                              R=dain.b + ident.b, W=p_.b, inc=(cc == 1))
                    A(lambda: nc.scalar.copy(datok.t[:], p_.t[:, 0:256]), p_.b, datok.b)
                    a_dir = datok.t[:, 128 + dr * 64:128 + (dr + 1) * 64]
                    dt_dir = datok.t[:, dr * 64:(dr + 1) * 64]
                    p_ = g.nps(0, 3)
                    for i_, lm in enumerate((LIm, SLm, ones_f.t[:])):
                        tr.op("pe", lambda: nc.tensor.matmul(p_.t[:, i_ * 64:(i_ + 1) * 64], lhsT=lm, rhs=a_dir, start=True, stop=True),
                              R=tri.b + ones_f.b + datok.b, W=p_.b, inc=(i_ == 2))
                    A(lambda: nc.scalar.activation(dcs.t[:], p_.t[:, 0:192], AF.Exp), p_.b, dcs.b)
                    dec = dcs.t[:, 0:64]
                    dend = dcs.t[:, 64:128]
                    cdec = dcs.t[:, 128:192]
                    V(lambda: nc.vector.tensor_tensor(xdt.t[:].rearrange("p (h q) -> p h q", h=64), xtok.t[:].rearrange("p (h q) -> p h q", h=64),
                                                      dt_dir.unsqueeze(2).to_broadcast([128, 64, 64]), ALU.mult), xtok.b + datok.b, xdt.b)
                    V(lambda: nc.vector.tensor_tensor(xdd.t[:].rearrange("p (h q) -> p h q", h=64), xdt.t[:].rearrange("p (h q) -> p h q", h=64),
                                                      dend.unsqueeze(2).to_broadcast([128, 64, 64]), ALU.mult), xdt.b + dcs.b, xdd.b)
                    for gi in range(8):
                        cb_, sr_, e_, m_ = cbm[gn[0] % 2], srh[gn[0] % 2], ee[gn[0] % 2], mt[gn[0] % 2]
                        gn[0] += 1
                        BT = cvb.t[:, gi, :]
                        CT = cvb.t[:, 8 + gi, :]
                        p_ = g.nps(0, 3)
                        tr.op("pe", lambda: nc.tensor.matmul(p_.t[:, 0:CH], lhsT=BT, rhs=CT, start=True, stop=True), R=cvb.b, W=p_.b)
                        V(lambda: nc.vector.tensor_tensor(cb_.t[:], p_.t[:, 0:CH], LIm, ALU.mult), p_.b + tri.b, cb_.b)
                        V(lambda: nc.vector.tensor_tensor(sr_.t[:], a_dir[:, gi * 8:(gi + 1) * 8].unsqueeze(2).to_broadcast([128, 8, CH]),
                                                          LIm.unsqueeze(1).to_broadcast([128, 8, CH]), ALU.mult), datok.b + tri.b, sr_.b)
                        for i_ in range(2):
                            tr.op("pe", lambda: nc.tensor.matmul(pseg[i_].t[:], lhsT=SLm, rhs=sr_.t[:, i_ * 4:(i_ + 1) * 4, :].rearrange("p h t -> p (h t)"),
                                                                start=True, stop=True), R=tri.b + sr_.b, W=pseg[i_].b)
                            A(lambda: nc.scalar.activation(e_.t[:, i_ * 4:(i_ + 1) * 4, :].rearrange("p h t -> p (h t)"), pseg[i_].t[:], AF.Exp),
                              pseg[i_].b, e_.b)
                        V(lambda: nc.vector.tensor_tensor(m_.t[:], e_.t[:], cb_.t[:].unsqueeze(1).to_broadcast([128, 8, CH]), ALU.mult), e_.b + cb_.b, m_.b)
                        for hh in range(8):
                            hd = gi * 8 + hh
                            tr.op("pe", lambda: nc.tensor.matmul(py.t[:, hh * 64:(hh + 1) * 64], lhsT=m_.t[:, hh, :], rhs=xdt.t[:, hd * 64:(hd + 1) * 64],
                                                                start=True, stop=True), R=m_.b + xdt.b, W=py.b, inc=(hh == 7))
                        tr.op("pe", lambda: nc.tensor.matmul(po.t[:], lhsT=CT, rhs=hbf.t[:, gi, :], start=True, stop=True), R=cvb.b + [hbf.b[gi]], W=po.b)
                        ya = yacc.t[:, gi * 512:(gi + 1) * 512]
                        V(lambda: nc.vector.tensor_tensor(ya.rearrange("p (h q) -> p h q", h=8), po.t[:].rearrange("p (h q) -> p h q", h=8),
                                                          dec[:, gi * 8:(gi + 1) * 8].unsqueeze(2).to_broadcast([128, 8, 64]), ALU.mult),
                          po.b + dcs.b, [yacc.b[gi]])
                        V(lambda: nc.vector.tensor_tensor(ya, ya, py.t[:], ALU.add), py.b + [yacc.b[gi]], [yacc.b[gi]])
                        tr.op("pe", lambda: nc.tensor.matmul(pst.t[:], lhsT=btok.t[:, gi * 128:(gi + 1) * 128], rhs=xdd.t[:, gi * 512:(gi + 1) * 512],
                                                            start=True, stop=True), R=btok.b + xdd.b, W=pst.b)
                        hv = h32.t[:, gi, :]
                        V(lambda: nc.vector.tensor_tensor(hv.rearrange("p (h q) -> p h q", h=8), hv.rearrange("p (h q) -> p h q", h=8),
                                                          cdec[:, gi * 8:(gi + 1) * 8].unsqueeze(2).to_broadcast([128, 8, 64]), ALU.mult),
                          [h32.b[gi]] + dcs.b, [h32.b[gi]])
                        V(lambda: nc.vector.tensor_tensor(hv, hv, pst.t[:], ALU.add), pst.b + [h32.b[gi]], [h32.b[gi]])
                        A(lambda: nc.scalar.copy(hbf.t[:, gi, :], hv), [h32.b[gi]], [hbf.b[gi]])
                    YAv = YA[c0:c0 + CH, :]
                    if dr == 0:
                        tr.dma("pool", YAv, yacc.t[:], R=yacc.b)
                        continue
                    tr.dma("sp", yprev.t[:], YAv, W=yprev.b)
                    tr.dma("sp", zt.t[:], Zv[:, :, c0:c0 + CH], W=zt.b)
                    V(lambda: nc.vector.tensor_tensor(yacc.t[:], yacc.t[:], yprev.t[:], ALU.add), yacc.b + yprev.b, yacc.b)
                    for q in range(8):
                        p_ = g.nps(0, 3)
                        for cc in range(4):
                            c = q * 4 + cc
                            tr.op("pe", lambda: nc.tensor.transpose(p_.t[:, cc * 128:(cc + 1) * 128], yacc.t[:, c * 128:(c + 1) * 128], ident.t[:]),
                                  R=yacc.b + ident.b, W=p_.b, inc=(cc == 3))
                        for cc in range(4):
                            c = q * 4 + cc
                            V(lambda: nc.vector.scalar_tensor_tensor(yF.t[:, c, :], cv.t[:, c, :], vecs.t[:, vc["ssd_D"] + c:vc["ssd_D"] + c + 1],
                                                                      p_.t[:, cc * 128:(cc + 1) * 128], ALU.mult, ALU.add),
                              cv.b + vecs.b + p_.b, [yF.b[q]])
                    V(lambda: nc.vector.tensor_tensor(yF.t[:], yF.t[:], zt.t[:], ALU.mult), yF.b + zt.b, yF.b)
                    sq = yprev
                    sqv = sq.t[:].rearrange("p (c t) -> p c t", c=32)
                    A(lambda: nc.scalar.activation(sqv, yF.t[:], AF.Square), yF.b, sq.b)
                    pn = [g.nps(0, 3), g.nps(0, 3)]
                    for gi in range(8):
                        for i_ in range(4):
                            tr.op("pe", lambda: nc.tensor.matmul(pn[gi // 4].t[:, (gi % 4) * 128:(gi % 4 + 1) * 128], lhsT=ones_f.t[:], rhs=sqv[:, gi * 4 + i_, :],
                                                                start=(i_ == 0), stop=(i_ == 3)), R=ones_f.b + sq.b, W=pn[gi // 4].b,
                                  inc=(i_ == 3 and gi % 4 == 3))
                    for i_ in range(2):
                        V(lambda: nc.vector.tensor_scalar(rs_.t[:, i_ * 4:(i_ + 1) * 4, :].rearrange("p h t -> p (h t)"), pn[i_].t[:], 1.0 / 512, LN_EPS,
                                                          ALU.mult, ALU.add), pn[i_].b, rs_.b)
                    A(lambda: nc.scalar.activation(rs_.t[:], rs_.t[:], AF.Ln), rs_.b, rs_.b)
                    A(lambda: nc.scalar.activation(rs_.t[:], rs_.t[:], AF.Exp, scale=-0.5), rs_.b, rs_.b)
                    sub = ci % 2
                    for c in range(32):
                        V(lambda: nc.vector.scalar_tensor_tensor(yout.t[:, c, sub * CH:(sub + 1) * CH], yF.t[:, c, :], vecs.t[:, vc["ssd_ng"] + c:vc["ssd_ng"] + c + 1],
                                                                  rs_.t[:, c // 4, :], ALU.mult, ALU.mult), yF.b + vecs.b + rs_.b, yout.b)
                    if sub == 0:
                        tc0 = t0 + (ci // 2) * 2 * CH
                        for q in range(4):
                            tr.dma("pool", Yv[:, q * 8:(q + 1) * 8, tc0:tc0 + 2 * CH], yout.t[:, q * 8:(q + 1) * 8, :], R=yout.b)
        tr.barrier()


def na_static():
    R = 32
    kr2 = np.arange(128) // 64
    kc = np.arange(128) % 64
    qr8 = np.arange(512) // 64
    qc = np.arange(512) % 64
    wst = np.clip(qc - 8, 0, 48)
    colok = (kc[:, None] >= wst[None, :]) & (kc[:, None] < wst[None, :] + 16)
    dx = np.clip(kc[:, None] - qc[None, :] + 15, 0, 30)
    valid = np.zeros((3, 8, 128, 512), bool)
    dy = np.zeros((3, 8, 128, 512), np.int64)
    for ty, m in enumerate((0, 1, 3)):
        base = int(np.clip(8 * m - 4, 0, R - 16))
        r = 8 * m + qr8
        rs = np.clip(r - 4, 0, R - 8)
        for jj in range(8):
            kr = base + 2 * jj + kr2
            rowok = (kr[:, None] >= rs[None, :]) & (kr[:, None] < rs[None, :] + 8)
            valid[ty, jj] = rowok & colok
            dy[ty, jj] = np.clip(kr[:, None] - r[None, :] + 7, 0, 14)
    return valid, dy, np.broadcast_to(dx, (3, 8, 128, 512))


def na_table(rpb):
    valid, dy, dx = na_static()
    rpb = np.asarray(rpb, np.float32)
    tab = rpb[:, dy, dx]
    tab = np.where(valid[None], tab, np.float32(NEG)).astype(np.float32)
    return np.ascontiguousarray(np.transpose(tab, (1, 0, 2, 3, 4)))


def na_mixer(g, j, vc, Ysc):
    nc, tr, vecs, ps = g.nc, g.tr, g.vecs, g.ps
    T = g.T
    XTv = g.XTv
    valid, _, _ = na_static()
    jlist = [[jj for jj in range(8) if valid[ty, jj].any()] for ty in range(3)]
    QT = g.dscr("na_QT", [D, T], BF16)
    KT = g.dscr("na_KT", [D, T], BF16)
    VT = g.dscr("na_VT", [T, D], BF16)
    QTv = QT.rearrange("(c p) t -> p c t", p=128)
    KTv = KT.rearrange("(c p) t -> p c t", p=128)
    VTv = VT.rearrange("(n p) f -> p n f", p=128)
    Yv = Ysc.rearrange("(c p) t -> p c t", p=128)
    scale = 128 ** -0.5
    dst, KC, NF, _pw = g.Wb["na_qkv"]
    with contextlib.ExitStack() as ps_:
        xres = Tl(ps_.enter_context(nc.sbuf_tensor(_un() + "n1_x", [128, DC, NT], F32)), 4)
        xb = Tl(ps_.enter_context(nc.sbuf_tensor(_un() + "n1_xb", [128, DC, NT], BF16)), 4)
        qo = Tl(ps_.enter_context(nc.sbuf_tensor(_un() + "n1_q", [128, DC, NT], BF16)), 4)
        ko = Tl(ps_.enter_context(nc.sbuf_tensor(_un() + "n1_k", [128, DC, NT], BF16)), 4)
        vo = Tl(ps_.enter_context(nc.sbuf_tensor(_un() + "n1_v", [128, 4, D], BF16)), 4)
        bv = Tl(ps_.enter_context(nc.sbuf_tensor(_un() + "n1_bv", [128, D], F32)))
        bqs = Tl(ps_.enter_context(nc.sbuf_tensor(_un() + "n1_bqs", [128, DC], F32)))
        wsl = [Tl(ps_.enter_context(nc.sbuf_tensor(_un() + "n1_w%d" % i, [128, 8192], BF16))) for i in range(3)]
        wctr = [0]
        tr.op("dve", lambda: nc.vector.tensor_scalar(bqs.t[:], vecs.t[:, vc["na_bq"]:vc["na_bq"] + DC], scale, None, ALU.mult), R=vecs.b, W=bqs.b)
        bk = vc["na_bk"]

        def consume(fc, p_):
            if fc < DC:
                tr.op("act", lambda: nc.scalar.activation(qo.t[:, fc, :], p_.t[:], AF.Identity, bias=bqs.t[:, fc:fc + 1], scale=scale),
                      R=p_.b + bqs.b, W=[qo.b[fc // 4]])
            else:
                c = fc - DC
                tr.op("act", lambda: nc.scalar.activation(ko.t[:, c, :], p_.t[:], AF.Identity, bias=vecs.t[:, bk + c:bk + c + 1], scale=1.0),
                      R=p_.b + vecs.b, W=[ko.b[c // 4]])

        for si, (t0, L) in enumerate(g.seqs):
            for ti in range(L // NT):
                c0 = t0 + ti * NT
                for q in range(4):
                    tr.dma("sp", xres.t[:, q * 4:(q + 1) * 4, :], XTv[:, q * 4:(q + 1) * 4, c0:c0 + NT], W=[xres.b[q]])
                    g.copy_op(("act", "dve")[q % 2], xb.t[:, q * 4:(q + 1) * 4, :], xres.t[:, q * 4:(q + 1) * 4, :], [xres.b[q]], [xb.b[q]])
                g.linear("na_qkv", wsl, wctr, lambda kc: xb.t[:, kc, :], lambda kc: [xb.b[kc // 4]], consume, f_lo=0, f_hi=32)
                for fp in range(0 if not os.environ.get("NA_NOV") else 4, 4):
                    slot = wsl[wctr[0] % 3]
                    wctr[0] += 1
                    tr.dma("sp", slot.t[:], dst[8 + fp], W=slot.b)
                    wv = slot.t[:].rearrange("p (k f) -> p k f", k=16)
                    for b in range(4):
                        p_ = g.nps(0, 4)
                        for kc in range(16):
                            tr.op("pe", lambda: nc.tensor.matmul(p_.t[:], lhsT=xb.t[:, kc, b * 128:(b + 1) * 128], rhs=wv[:, kc, :],
                                                                start=(kc == 0), stop=(kc == 15)),
                                  R=slot.b + [xb.b[kc // 4]], W=p_.b, inc=(kc == 15))
                        g.copy_op(("act", "dve")[b % 2], vo.t[:, b, fp * 512:(fp + 1) * 512], p_.t[:], p_.b, [vo.b[b]])
                for q in range(4):
                    tr.dma("pool", QTv[:, q * 4:(q + 1) * 4, c0:c0 + NT], qo.t[:, q * 4:(q + 1) * 4, :], R=[qo.b[q]])
                    tr.dma("pool", KTv[:, q * 4:(q + 1) * 4, c0:c0 + NT], ko.t[:, q * 4:(q + 1) * 4, :], R=[ko.b[q]])
                    tr.dma("pool", VTv[:, c0 // 128 + q, :], vo.t[:, q, :], R=[vo.b[q]])
    tr.barrier()
    if os.environ.get("NA_SKIP2"):
        return
    LM = max(L for _, L in g.seqs)
    with contextlib.ExitStack() as ps_:
        qh = [Tl(ps_.enter_context(nc.sbuf_tensor(_un() + "n2_q%d" % i, [128, LM], BF16))) for i in range(2)]
        kh = [Tl(ps_.enter_context(nc.sbuf_tensor(_un() + "n2_k%d" % i, [128, LM], BF16))) for i in range(2)]
        vh = [Tl(ps_.enter_context(nc.sbuf_tensor(_un() + "n2_v%d" % i, [128, LM // 128, 128], BF16))) for i in range(2)]
        oh = [Tl(ps_.enter_context(nc.sbuf_tensor(_un() + "n2_o%d" % i, [128, LM], BF16))) for i in range(2)]
        tab = Tl(ps_.enter_context(nc.sbuf_tensor(_un() + "n2_tab", [128, 3, 8, NT], F32)), 3)
        sb = [Tl(ps_.enter_context(nc.sbuf_tensor(_un() + "n2_sb%d" % i, [128, NT], F32))) for i in range(3)]
        pb = [Tl(ps_.enter_context(nc.sbuf_tensor(_un() + "n2_pb%d" % i, [128, NT], BF16))) for i in range(3)]
        ri = [Tl(ps_.enter_context(nc.sbuf_tensor(_un() + "n2_ri%d" % i, [128, NT], F32))) for i in range(2)]
        n = 0
        k = 0
        mm = 0
        for si, (t0, L) in enumerate(g.seqs):
            R = L // 64
            nm = L // NT
            for h in range(16):
                q_, k_, v_, o_ = qh[n % 2], kh[n % 2], vh[n % 2], oh[n % 2]
                n += 1
                tr.dma("sp", q_.t[:, 0:L], QTv[:, h, t0:t0 + L], W=q_.b)
                tr.dma("sp", k_.t[:, 0:L], KTv[:, h, t0:t0 + L], W=k_.b)
                tr.dma("sp", v_.t[:, 0:L // 128, :], VTv[:, t0 // 128:(t0 + L) // 128, h * 128:(h + 1) * 128], W=v_.b)
                if si == 0 or True:
                    for ty in range(3):
                        tr.dma("sp", tab.t[:, ty], g.natab_d[ty, h].rearrange("j p q -> p j q"), W=[tab.b[ty]])
                for m in range(nm):
                    ty = 0 if m == 0 else (2 if m == nm - 1 else 1)
                    base = int(np.clip(8 * m - 4, 0, R - 16))
                    po, pr = ps[4 + mm % 2], ps[6 + mm % 2]
                    mm += 1
                    jl = jlist[ty]
                    stg = {}

                    def s_stage(jj):
                        nonlocal k
                        tok = (base + 2 * jj) * 64
                        p_s = g.nps(0, 4)
                        s_, p_ = sb[k % 3], pb[k % 3]
                        k += 1
                        tr.op("pe", lambda: nc.tensor.matmul(p_s.t[:], lhsT=k_.t[:, tok:tok + 128], rhs=q_.t[:, m * NT:(m + 1) * NT], start=True, stop=True),
                              R=k_.b + q_.b, W=p_s.b)
                        tr.op("dve", lambda: nc.vector.tensor_tensor(s_.t[:], p_s.t[:], tab.t[:, ty, jj, :], ALU.add), R=p_s.b + [tab.b[ty]], W=s_.b)
                        tr.op("act", lambda: nc.scalar.activation(p_.t[:], s_.t[:], AF.Exp), R=s_.b, W=p_.b)
                        stg[jj] = (tok, p_)

                    def pv_stage(jj):
                        tok, p_ = stg.pop(jj)
                        tr.op("pe", lambda: nc.tensor.matmul(po.t[:], lhsT=v_.t[:, tok // 128, :], rhs=p_.t[:], start=(jj == jl[0]), stop=(jj == jl[-1])),
                              R=v_.b + p_.b, W=po.b)
                        tr.op("pe", lambda: nc.tensor.matmul(pr.t[:], lhsT=g.ones_b.t[:], rhs=p_.t[:], start=(jj == jl[0]), stop=(jj == jl[-1])),
                              R=g.ones_b.b + p_.b, W=pr.b)

                    LA_ = 2
                    for i_ in range(min(LA_, len(jl))):
                        s_stage(jl[i_])
                    for i_ in range(len(jl)):
                        if i_ + LA_ < len(jl):
                            s_stage(jl[i_ + LA_])
                        pv_stage(jl[i_])
                    r_ = ri[mm % 2]
                    tr.op("dve", lambda: nc.vector.reciprocal(r_.t[:], pr.t[:]), R=pr.b, W=r_.b)
                    tr.op("dve", lambda: nc.vector.tensor_tensor(r_.t[:], po.t[:], r_.t[:], ALU.mult), R=po.b + r_.b, W=r_.b)
                    tr.op("act", lambda: nc.scalar.activation(o_.t[:, m * NT:(m + 1) * NT], r_.t[:], AF.Identity,
                                                              bias=vecs.t[:, vc["na_bvp"] + h:vc["na_bvp"] + h + 1], scale=1.0),
                          R=r_.b + vecs.b, W=o_.b)
                tr.dma("pool", Yv[:, h, t0:t0 + L], o_.t[:, 0:L], R=o_.b)
    tr.barrier()


LAYERS = [("rg", 0), ("ssd", 0), ("na", 0), ("rg", 1)]
WNAMES = ["rg_w_in", "rg_w_a", "rg_w_x", "rg_w_out", "ssd_w_in", "ssd_w_out", "na_w_qkv", "na_w_out", "mlp_w_up", "mlp_w_down"]


def pack_consts(inp, layers):
    cols = []
    vcols = []
    off = [0]

    def add(v):
        a = _vec_layout(v)
        cols.append(a)
        o = off[0]
        off[0] += a.shape[1]
        return o

    rws = [np.zeros(16, np.float32)]
    roff = [16]

    def radd(v):
        v = np.asarray(v, np.float32).reshape(-1)
        rws.append(v)
        o = roff[0]
        roff[0] += v.size
        return o

    for li, (kind, j) in enumerate(layers):
        vc = {}
        vc["ln1g"] = add(inp["ln1_g"][li])
        vc["ln1b"] = add(inp["ln1_b"][li])
        vc["ln2g"] = add(inp["ln2_g"][li])
        vc["ln2b"] = add(inp["ln2_b"][li])
        if kind == "rg":
            vc["rg_cw"] = add(inp["rg_conv_w"][j][0])
            for kk in range(1, 4):
                add(inp["rg_conv_w"][j][kk])
            vc["rg_cb"] = add(inp["rg_conv_b"][j])
            vc["rg_ba"] = add(inp["rg_b_a"][j][0])
            add(inp["rg_b_a"][j][1])
            vc["rg_bx"] = add(inp["rg_b_x"][j][0])
            add(inp["rg_b_x"][j][1])
            vc["rg_lam"] = add(inp["rg_lambda"][j][0])
            add(inp["rg_lambda"][j][1])
        if kind == "ssd":
            cwv = np.asarray(inp["ssd_conv_w"][j], np.float32)
            vc["ssd_cw"] = add(cwv[0])
            for kk in range(1, 4):
                add(cwv[kk])
            vc["ssd_cb"] = add(inp["ssd_conv_b"][j])
            vc["ssd_dtb"] = add(np.asarray(inp["ssd_dt_bias"][j], np.float32).reshape(-1))
            vc["ssd_alog"] = add(np.asarray(inp["ssd_a_log"][j], np.float32).reshape(-1))
            vc["ssd_D"] = add(np.repeat(np.asarray(inp["ssd_d"][j], np.float32), 64))
            vc["ssd_ng"] = add(inp["ssd_norm_g"][j])
        if kind == "na":
            bq = np.asarray(inp["na_b_qkv"][j], np.float32)
            vc["na_bq"] = add(bq[0:D])
            vc["na_bk"] = add(bq[D:2 * D])
            vc["na_bvp"] = add(bq[2 * D:3 * D])
        vcols.append(vc)
    vecs = np.ascontiguousarray(np.concatenate(cols, axis=1))
    rows = np.ascontiguousarray(np.concatenate(rws)[None, :])
    return vecs, vcols, rows


def run(inp, layers, LA, LB, n_cores, xa_list, xb_list):
    global g_vcols
    vecs, vcols, rows = pack_consts(inp, layers)
    g_vcols = vcols
    wshapes = {k: tuple(np.asarray(inp[k]).shape) for k in WNAMES}
    has_na = any(k == "na" for k, _ in layers)
    natab = na_table(inp["na_rpb"][0]) if has_na else None
    import time as _t
    _t0 = _t.time()
    nc = build(LA, LB, layers, wshapes, vecs.shape[1], rows.shape[1], natab.shape if has_na else None)
    print("build_s", _t.time() - _t0, flush=True)
    ident = np.eye(128, dtype=np.float32)
    kk = np.arange(128)
    tri = np.stack([(kk[:, None] > kk[None, :]), (kk[:, None] <= kk[None, :]), (kk[:, None] < kk[None, :]), (kk[:, None] >= kk[None, :])]).astype(np.float32)
    base = {k: np.ascontiguousarray(np.asarray(inp[k], np.float32)) for k in WNAMES}
    base.update(vecs=vecs, rows=rows, ident=ident, tri=tri)
    if has_na:
        base["natab"] = natab
    in_maps = []
    for c in range(n_cores):
        m = dict(base)
        m["xa"] = np.ascontiguousarray(xa_list[c], dtype=np.float32)
        m["xb"] = np.ascontiguousarray(xb_list[c], dtype=np.float32)
        in_maps.append(m)
    res = run_bass_kernel_spmd(nc, in_maps, core_ids=list(range(n_cores)))
    return [(r["ya"], r["yb"]) for r in res.results]


def kernel(**inputs):
    xp = np.asarray(inputs["x_prompt"], np.float32)
    xs = np.asarray(inputs["x_sample"], np.float32)
    xa_list = [xp[c] for c in range(8)]
    xb_list = [xs[c % 4] for c in range(8)]
    outs = run(inputs, LAYERS, 2048, 8192, 8, xa_list, xb_list)
    yp = np.stack([outs[c][0] for c in range(8)], axis=0)
    ys = np.stack([outs[c][1] for c in range(4)], axis=0)
    return (yp, ys)
```
